# Optimizing a Trainium2 kernel written in Bass

```python
import math
import jax
import jax.numpy as jnp
from jax import lax
import numpy as np

D_MODEL = 1024
BATCH = 8
SEQ = 4096
DEPTH = 2

CTX_LEN = 256
GRID_W = 64
ROPE_THETA = 10000.0
Q_BLOCK = 128
EPS = 1e-6
N_MOD = 6

D_RNN = 512
RNN_BLOCKS = 8
RNN_BW = D_RNN // RNN_BLOCKS
CONV_W = 4
LRU_C = 8.0

DIFF_HEADS = 4
DIFF_DK = 64
DIFF_DV = 2 * DIFF_DK

GQA_HEADS = 8
GQA_KV_HEADS = 2
GQA_DH = 64

MLA_HEADS = 8
MLA_NOPE = 64
MLA_ROPE = 32
MLA_DV = 64
MLA_Q_RANK = 384
MLA_KV_RANK = 256

N_BRANCH = 4
BRANCH_W = 512

PEER_HEADS = 8
PEER_NKEYS = 128
PEER_EXPERTS = PEER_NKEYS * PEER_NKEYS
PEER_DK = 128
PEER_TOPK = 16
PEER_CHUNK = 128

IN_SPLIT = (D_RNN, D_RNN,
            DIFF_HEADS * 2 * DIFF_DK, DIFF_HEADS * 2 * DIFF_DK, DIFF_HEADS * DIFF_DV,
            GQA_HEADS * GQA_DH, GQA_KV_HEADS * GQA_DH, GQA_KV_HEADS * GQA_DH,
            MLA_Q_RANK, MLA_KV_RANK, MLA_ROPE,
            N_BRANCH * D_MODEL)
IN_COLS = sum(IN_SPLIT)

kernel_name = "hybrid_lru_diffattn_gqa_mla_peer_dit"


def rmsnorm(x, g):
    xf = x.astype(jnp.float32)
    y = xf * lax.rsqrt(jnp.mean(xf * xf, axis=-1, keepdims=True) + EPS)
    return (y * g.astype(jnp.float32)).astype(x.dtype)


def adaln(x, g, shift, scale):
    return rmsnorm(x, g) * (1.0 + scale) + shift


def split_cols(z):
    out, o = [], 0
    for w in IN_SPLIT:
        out.append(z[..., o:o + w])
        o += w
    return out


def heads(z, n, dh):
    b, s, _ = z.shape
    return z.reshape(b, s, n, dh).transpose(0, 2, 1, 3)


def merge_heads(o):
    b, h, s, d = o.shape
    return o.transpose(0, 2, 1, 3).reshape(b, s, h * d)


def axial_rope(rows, rope_dim):
    n_freq = rope_dim // 4
    inv_freq = ROPE_THETA ** (-jnp.arange(n_freq, dtype=jnp.float32) / n_freq)
    r = jnp.repeat(jnp.arange(rows, dtype=jnp.float32), GRID_W)
    col = jnp.tile(jnp.arange(GRID_W, dtype=jnp.float32), rows)
    ang = jnp.concatenate([r[:, None] * inv_freq, col[:, None] * inv_freq], axis=-1)
    return jnp.cos(ang), jnp.sin(ang)


def apply_rope(x, cos, sin):
    half = x.shape[-1] // 2
    xf = x.astype(jnp.float32)
    x1, x2 = xf[..., :half], xf[..., half:]
    return jnp.concatenate([x1 * cos - x2 * sin, x1 * sin + x2 * cos], axis=-1).astype(x.dtype)


def _probs(q, k, scale):
    s = jnp.einsum("bgrqd,bgkd->bgrqk", q.astype(jnp.float32), k.astype(jnp.float32)) * scale
    return jax.nn.softmax(s, axis=-1)


def attend(q, k, v, scale):
    p = _probs(q, k, scale)
    return jnp.einsum("bgrqk,bgkd->bgrqd", p, v.astype(jnp.float32)).astype(v.dtype)


def diff_attend(q1, q2, k1, k2, v, lam, scale):
    p = _probs(q1, k1, scale) - lam * _probs(q2, k2, scale)
    return jnp.einsum("bgrqk,bgkd->bgrqd", p, v.astype(jnp.float32)).astype(v.dtype)


def sweep_query_blocks(fn, *qs):
    n = qs[0].shape[-2]
    nb = n // Q_BLOCK

    def to_blocks(a):
        a = a.reshape(a.shape[:-2] + (nb, Q_BLOCK, a.shape[-1]))
        return jnp.moveaxis(a, -3, 0)

    out = lax.map(lambda blk: fn(*blk), tuple(to_blocks(a) for a in qs))
    out = jnp.moveaxis(out, 0, -3)
    return out.reshape(out.shape[:-3] + (n, out.shape[-1]))


def centred_dwconv(x, w, b):
    pad_l = (CONV_W - 1) // 2
    pad_r = CONV_W - 1 - pad_l
    y = lax.conv_general_dilated(x, w[:, None, :].astype(x.dtype), (1,), [(pad_l, pad_r)],
                                 dimension_numbers=("NWC", "WIO", "NWC"),
                                 feature_group_count=x.shape[-1])
    return y + b


def rglru_coeffs(u, w_a, b_a, w_i, b_i, lam):
    uf = u.astype(jnp.float32)
    ub = uf.reshape(uf.shape[:-1] + (RNN_BLOCKS, RNN_BW))

    def gate(w, b):
        z = jnp.einsum("bsnc,ncd->bsnd", ub, w.astype(jnp.float32)).reshape(uf.shape)
        return jax.nn.sigmoid(z + b.astype(jnp.float32))

    r = gate(w_a, b_a)
    i = gate(w_i, b_i)
    log_a = -LRU_C * r * jax.nn.softplus(-lam.astype(jnp.float32))
    return jnp.exp(log_a), jnp.sqrt(-jnp.expm1(2.0 * log_a)) * (i * uf)


def linear_scan(a, b, h0, reverse):
    edge = -1 if reverse else 0
    b = b.at[:, edge].add(a[:, edge] * h0)

    def combine(e1, e2):
        a1, b1 = e1
        a2, b2 = e2
        return a1 * a2, a2 * b1 + b2

    _, h = lax.associative_scan(combine, (a, b), reverse=reverse, axis=1)
    return h


def rglru_mixer(xa, ya, xac, yac, p, need_ctx):
    u = centred_dwconv(xa, p["conv_w"], p["conv_b"])
    uc = centred_dwconv(xac, p["conv_w"], p["conv_b"])
    hs, hcs = [], []
    for d, rev in enumerate((False, True)):
        args = (p["lru_wa"][d], p["lru_ba"][d], p["lru_wi"][d], p["lru_bi"][d], p["lru_lambda"][d])
        ac, bc = rglru_coeffs(uc, *args)
        hc = linear_scan(ac, bc, jnp.zeros_like(bc[:, 0]), rev)
        h_ctx_final = hc[:, 0] if rev else hc[:, -1]
        a, b = rglru_coeffs(u, *args)
        hs.append(linear_scan(a, b, h_ctx_final, rev))
        hcs.append(hc)
    out = (jax.nn.gelu(ya.astype(jnp.float32)) * (hs[0] + hs[1])).astype(xa.dtype)
    out_c = None
    if need_ctx:
        out_c = (jax.nn.gelu(yac.astype(jnp.float32)) * (hcs[0] + hcs[1])).astype(xac.dtype)
    return out, out_c


def diff_mixer(q, k, v, qc, kc, vc, p, rope, lam_init, need_ctx):
    lv = p["diff_lam"].astype(jnp.float32)
    lam = jnp.exp(jnp.sum(lv[0] * lv[1])) - jnp.exp(jnp.sum(lv[2] * lv[3])) + lam_init
    scale = DIFF_DK ** -0.5

    def split_qk(z):
        zh = heads(z, DIFF_HEADS, 2 * DIFF_DK)
        return zh[..., :DIFF_DK], zh[..., DIFF_DK:]

    def finish(o):
        return merge_heads(rmsnorm(o, p["diff_subln_g"]) * (1.0 - lam_init))

    k1c, k2c = split_qk(kc)
    vhc = heads(vc, DIFF_HEADS, DIFF_DV)
    q1, q2 = (apply_rope(t, *rope) for t in split_qk(q))
    k1, k2 = (apply_rope(t, *rope) for t in split_qk(k))
    kk1 = jnp.concatenate([k1c, k1], axis=2)
    kk2 = jnp.concatenate([k2c, k2], axis=2)
    vv = jnp.concatenate([vhc, heads(v, DIFF_HEADS, DIFF_DV)], axis=2)
    o = sweep_query_blocks(lambda a, b: diff_attend(a, b, kk1, kk2, vv, lam, scale),
                           q1[:, :, None], q2[:, :, None])
    out = finish(o[:, :, 0])
    out_c = None
    if need_ctx:
        q1c, q2c = split_qk(qc)
        out_c = finish(diff_attend(q1c[:, :, None], q2c[:, :, None], k1c, k2c, vhc, lam, scale)[:, :, 0])
    return out, out_c


def gqa_mixer(q, k, v, qc, kc, vc, p, rope, need_ctx):
    rep = GQA_HEADS // GQA_KV_HEADS
    scale = GQA_DH ** -0.5

    def q_heads(z):
        t = rmsnorm(heads(z, GQA_HEADS, GQA_DH), p["gqa_qnorm_g"])
        return t.reshape(t.shape[0], GQA_KV_HEADS, rep, t.shape[2], GQA_DH)

    def k_heads(z):
        return rmsnorm(heads(z, GQA_KV_HEADS, GQA_DH), p["gqa_knorm_g"])

    def finish(o):
        return merge_heads(o.reshape(o.shape[0], GQA_HEADS, o.shape[3], GQA_DH))

    khc = k_heads(kc)
    vhc = heads(vc, GQA_KV_HEADS, GQA_DH)
    kk = jnp.concatenate([khc, apply_rope(k_heads(k), *rope)], axis=2)
    vv = jnp.concatenate([vhc, heads(v, GQA_KV_HEADS, GQA_DH)], axis=2)
    ql = apply_rope(q_heads(q), *rope)
    out = finish(sweep_query_blocks(lambda a: attend(a, kk, vv, scale), ql))
    out_c = finish(attend(q_heads(qc), khc, vhc, scale)) if need_ctx else None
    return out, out_c


def mla_mixer(cq, ckv, kr, cqc, ckvc, krc, p, rope, need_ctx):
    scale = (MLA_NOPE + MLA_ROPE) ** -0.5

    def q_of(z, rot):
        qf = heads(rmsnorm(z, p["mla_qnorm_g"]) @ p["mla_w_uq"], MLA_HEADS, MLA_NOPE + MLA_ROPE)
        q_nope, q_rope = qf[..., :MLA_NOPE], qf[..., MLA_NOPE:]
        if rot:
            q_rope = apply_rope(q_rope, *rope)
        return jnp.concatenate([q_nope, q_rope], axis=-1)[:, :, None]

    def kv_of(zc, zr, rot):
        kvf = heads(rmsnorm(zc, p["mla_kvnorm_g"]) @ p["mla_w_ukv"], MLA_HEADS, MLA_NOPE + MLA_DV)
        k_nope, vh = kvf[..., :MLA_NOPE], kvf[..., MLA_NOPE:]
        k_rope = zr[:, None]
        if rot:
            k_rope = apply_rope(k_rope, *rope)
        k_rope = jnp.broadcast_to(k_rope, k_nope.shape[:-1] + (MLA_ROPE,))
        return jnp.concatenate([k_nope, k_rope], axis=-1), vh

    kc_, vc_ = kv_of(ckvc, krc, False)
    kl, vl = kv_of(ckv, kr, True)
    kk = jnp.concatenate([kc_, kl], axis=2)
    vv = jnp.concatenate([vc_, vl], axis=2)
    out = merge_heads(sweep_query_blocks(lambda a: attend(a, kk, vv, scale), q_of(cq, True))[:, :, 0])
    out_c = merge_heads(attend(q_of(cqc, False), kc_, vc_, scale)[:, :, 0]) if need_ctx else None
    return out, out_c


def merge_branches(outs, zg, w_branch, w_out):
    o = jnp.stack(outs, axis=-2)
    gates = jax.nn.sigmoid(zg.reshape(zg.shape[:-1] + (N_BRANCH, D_MODEL)))
    merged = jnp.sum(jnp.einsum("bskc,kcd->bskd", o, w_branch) * gates, axis=-2)
    return merged @ w_out


def token_mixers(h, hc, p, rope_diff, rope_gqa, rope_mla, lam_init, need_ctx):
    (xa, ya, qb, kb, vb, qg, kg, vg, cq, ckv, kr, zg) = split_cols(h @ p["w_in"])
    (xac, yac, qbc, kbc, vbc, qgc, kgc, vgc, cqc, ckvc, krc, zgc) = split_cols(hc @ p["w_in"])
    oa, oa_c = rglru_mixer(xa, ya, xac, yac, p, need_ctx)
    ob, ob_c = diff_mixer(qb, kb, vb, qbc, kbc, vbc, p, rope_diff, lam_init, need_ctx)
    og, og_c = gqa_mixer(qg, kg, vg, qgc, kgc, vgc, p, rope_gqa, need_ctx)
    om, om_c = mla_mixer(cq, ckv, kr, cqc, ckvc, krc, p, rope_mla, need_ctx)
    out = merge_branches((oa, ob, og, om), zg, p["w_branch"], p["w_out"])
    out_c = None
    if need_ctx:
        out_c = merge_branches((oa_c, ob_c, og_c, om_c), zgc, p["w_branch"], p["w_out"])
    return out, out_c


def peer_ffn(x, w_q, keys, u_tab, v_tab):
    shape = x.shape
    xt = x.reshape(-1, PEER_CHUNK, shape[-1])
    kf = keys.astype(jnp.float32)

    def chunk(t):
        q = (t @ w_q).astype(jnp.float32).reshape(PEER_CHUNK, PEER_HEADS, 2, PEER_DK // 2)
        s = jnp.einsum("thpc,hpnc->thpn", q, kf)
        sv, si = lax.top_k(s, PEER_TOPK)
        cand = (sv[:, :, 0, :, None] + sv[:, :, 1, None, :]).reshape(PEER_CHUNK, PEER_HEADS, -1)
        cidx = (si[:, :, 0, :, None] * PEER_NKEYS + si[:, :, 1, None, :]).reshape(PEER_CHUNK, PEER_HEADS, -1)
        best, pos = lax.top_k(cand, PEER_TOPK)
        eidx = jnp.take_along_axis(cidx, pos, axis=-1)
        g = jax.nn.softmax(best, axis=-1)
        act = jax.nn.gelu(jnp.einsum("thkd,td->thk", jnp.take(u_tab, eidx, axis=0), t).astype(jnp.float32))
        w = (g * act).astype(t.dtype)
        return jnp.einsum("thk,thkd->td", w, jnp.take(v_tab, eidx, axis=0))

    return lax.map(chunk, xt).reshape(shape)


def setup_inputs(seed: int = 0) -> dict:
    key = jax.random.key(seed)
    ks = jax.random.split(key, 40)
    L, D = DEPTH, D_MODEL

    def nrm(i, shape, scale):
        return jax.random.normal(ks[i], shape, jnp.float32) * scale

    def gain(i, shape):
        return 1.0 + nrm(i, shape, 0.02)

    u = jax.random.uniform(ks[39], (L, 2, D_RNN), jnp.float32, 0.9, 0.999)
    a_base = u ** (1.0 / LRU_C)
    lru_lambda = jnp.log(a_base) - jnp.log1p(-a_base)
    return {
        "x": nrm(0, (BATCH, SEQ, D), 1.0),
        "c": nrm(1, (BATCH, D), 1.0),
        "ctx": nrm(2, (BATCH, CTX_LEN, D), 1.0),
        "c_ctx": nrm(3, (D,), 1.0),
        "w_mod": nrm(4, (L, D, N_MOD * D), 0.5 * D ** -0.5),
        "b_mod": nrm(5, (L, N_MOD * D), 0.02),
        "norm1_g": gain(6, (L, D)),
        "norm2_g": gain(7, (L, D)),
        "w_in": nrm(8, (L, D, IN_COLS), D ** -0.5),
        "conv_w": nrm(9, (L, CONV_W, D_RNN), CONV_W ** -0.5),
        "conv_b": nrm(10, (L, D_RNN), 0.02),
        "lru_wa": nrm(11, (L, 2, RNN_BLOCKS, RNN_BW, RNN_BW), RNN_BW ** -0.5),
        "lru_ba": nrm(12, (L, 2, D_RNN), 0.02),
        "lru_wi": nrm(13, (L, 2, RNN_BLOCKS, RNN_BW, RNN_BW), RNN_BW ** -0.5),
        "lru_bi": nrm(14, (L, 2, D_RNN), 0.02),
        "lru_lambda": lru_lambda,
        "diff_lam": nrm(15, (L, 4, DIFF_DK), 0.1),
        "diff_subln_g": gain(16, (L, DIFF_DV)),
        "gqa_qnorm_g": gain(17, (L, GQA_DH)),
        "gqa_knorm_g": gain(18, (L, GQA_DH)),
        "mla_qnorm_g": gain(19, (L, MLA_Q_RANK)),
        "mla_w_uq": nrm(20, (L, MLA_Q_RANK, MLA_HEADS * (MLA_NOPE + MLA_ROPE)), MLA_Q_RANK ** -0.5),
        "mla_kvnorm_g": gain(21, (L, MLA_KV_RANK)),
        "mla_w_ukv": nrm(22, (L, MLA_KV_RANK, MLA_HEADS * (MLA_NOPE + MLA_DV)), MLA_KV_RANK ** -0.5),
        "w_branch": nrm(23, (L, N_BRANCH, BRANCH_W, D), BRANCH_W ** -0.5),
        "w_out": nrm(24, (L, D, D), D ** -0.5),
        "peer_wq": nrm(25, (L, D, PEER_HEADS * PEER_DK), D ** -0.5),
        "peer_keys": nrm(26, (L, PEER_HEADS, 2, PEER_NKEYS, PEER_DK // 2), (PEER_DK // 2) ** -0.5),
        "peer_u": nrm(27, (L, PEER_EXPERTS, D), D ** -0.5),
        "peer_v": nrm(28, (L, PEER_EXPERTS, D), PEER_HEADS ** -0.5),
        "final_norm_g": gain(29, (D,)),
    }


def reference(x, c, ctx, c_ctx, w_mod, b_mod, norm1_g, norm2_g, w_in, conv_w, conv_b,
              lru_wa, lru_ba, lru_wi, lru_bi, lru_lambda, diff_lam, diff_subln_g,
              gqa_qnorm_g, gqa_knorm_g, mla_qnorm_g, mla_w_uq, mla_kvnorm_g, mla_w_ukv,
              w_branch, w_out, peer_wq, peer_keys, peer_u, peer_v, final_norm_g):
    rows = x.shape[1] // GRID_W
    rope_diff = axial_rope(rows, DIFF_DK)
    rope_gqa = axial_rope(rows, GQA_DH)
    rope_mla = axial_rope(rows, MLA_ROPE)
    sc = jax.nn.silu(c)
    scc = jax.nn.silu(c_ctx)
    xc = ctx
    for l in range(DEPTH):
        need_ctx = l < DEPTH - 1
        lam_init = 0.8 - 0.6 * math.exp(-0.3 * l)
        p = {
            "w_in": w_in[l], "conv_w": conv_w[l], "conv_b": conv_b[l],
            "lru_wa": lru_wa[l], "lru_ba": lru_ba[l], "lru_wi": lru_wi[l], "lru_bi": lru_bi[l],
            "lru_lambda": lru_lambda[l], "diff_lam": diff_lam[l], "diff_subln_g": diff_subln_g[l],
            "gqa_qnorm_g": gqa_qnorm_g[l], "gqa_knorm_g": gqa_knorm_g[l],
            "mla_qnorm_g": mla_qnorm_g[l], "mla_w_uq": mla_w_uq[l],
            "mla_kvnorm_g": mla_kvnorm_g[l], "mla_w_ukv": mla_w_ukv[l],
            "w_branch": w_branch[l], "w_out": w_out[l],
        }
        mod = (sc @ w_mod[l] + b_mod[l]).reshape(-1, N_MOD, 1, D_MODEL)
        modc = (scc @ w_mod[l] + b_mod[l]).reshape(N_MOD, D_MODEL)
        h = adaln(x, norm1_g[l], mod[:, 0], mod[:, 1])
        hc = adaln(xc, norm1_g[l], modc[0], modc[1])
        mix, mix_c = token_mixers(h, hc, p, rope_diff, rope_gqa, rope_mla, lam_init, need_ctx)
        x = x + mod[:, 2] * mix
        h = adaln(x, norm2_g[l], mod[:, 3], mod[:, 4])
        x = x + mod[:, 5] * peer_ffn(h, peer_wq[l], peer_keys[l], peer_u[l], peer_v[l])
        if need_ctx:
            xc = xc + modc[2] * mix_c
            hc = adaln(xc, norm2_g[l], modc[3], modc[4])
            xc = xc + modc[5] * peer_ffn(hc, peer_wq[l], peer_keys[l], peer_u[l], peer_v[l])
    return rmsnorm(x, final_norm_g)
```

```python
import math
import numpy as np
from contextlib import ExitStack
import concourse.bass as bass
import concourse.mybir as mybir
from concourse.bass_utils import run_bass_kernel_spmd

F32 = mybir.dt.float32
BF16 = mybir.dt.bfloat16
U32 = mybir.dt.uint32
AF = mybir.ActivationFunctionType
ALU = mybir.AluOpType
AX = mybir.AxisListType

D = 1024
NCTX = 256
NLAT = 4096
T = NCTX + NLAT
DEPTH = 2
EPS = 1e-6
IN_COLS = 8096
NKB = T // 128
TILES = [(0, 256)] + [(256 + 512 * i, 512) for i in range(8)]
PT = 256
PTILES = [(i * PT, PT) for i in range(T // PT)]
CG = 2


class Sched:
    def __init__(self, nc, es, n_dma=40):
        self.nc = nc
        self.E = {"pe": nc.tensor, "act": nc.scalar, "dve": nc.vector, "pool": nc.gpsimd, "sp": nc.sync}
        self.sem = {k: es.enter_context(nc.semaphore("s_" + k)) for k in self.E}
        self.cnt = {k: 0 for k in self.E}
        self.seen = {k: {} for k in self.E}
        self.nd = n_dma
        self.dsem = [es.enter_context(nc.semaphore("d%d" % i)) for i in range(n_dma)]
        self.dval = [0] * n_dma
        self.dnext = 0
        self.lastw = {}
        self.readers = {}
        self.same_wait = {"pe": False, "act": True, "dve": True, "pool": True, "sp": True}

    def _wait(self, ek, key, val):
        if self.seen[ek].get(key, 0) >= val:
            return
        kind, idx = key
        if kind == "e" and idx == ek and not self.same_wait[ek]:
            return
        sem = self.sem[idx] if kind == "e" else self.dsem[idx]
        self.E[ek].wait_ge(sem, val)
        self.seen[ek][key] = val

    def op(self, ek, fn, reads=(), writes=(), dma=False):
        deps = {}
        for r in reads:
            t = self.lastw.get(r)
            if t is not None and deps.get(t[0], 0) < t[1]:
                deps[t[0]] = t[1]
        for w in writes:
            t = self.lastw.get(w)
            if t is not None and deps.get(t[0], 0) < t[1]:
                deps[t[0]] = t[1]
            for k, v in self.readers.get(w, {}).items():
                if deps.get(k, 0) < v:
                    deps[k] = v
        for k, v in deps.items():
            self._wait(ek, k, v)
        if dma:
            i = self.dnext
            self.dnext = (i + 1) % self.nd
            if self.dval[i] > 0:
                self._wait(ek, ("d", i), self.dval[i])
            ins = fn(self.E[ek])
            self.dval[i] += 16
            ins.then_inc(self.dsem[i], 16)
            tok = (("d", i), self.dval[i])
        else:
            ins = fn(self.E[ek])
            self.cnt[ek] += 1
            ins.then_inc(self.sem[ek], 1)
            tok = (("e", ek), self.cnt[ek])
        for w in writes:
            self.lastw[w] = tok
            self.readers[w] = {}
        for r in reads:
            d = self.readers.setdefault(r, {})
            if d.get(tok[0], 0) < tok[1]:
                d[tok[0]] = tok[1]
        return tok

    def barrier(self):
        for ek in self.E:
            for k2 in self.E:
                if k2 != ek and self.cnt[k2] > 0:
                    self._wait(ek, ("e", k2), self.cnt[k2])
            for i in range(self.nd):
                if self.dval[i] > 0:
                    self._wait(ek, ("d", i), self.dval[i])
        self.lastw.clear()
        self.readers.clear()


class Ring:
    def __init__(self, tensors, name):
        self.t = tensors
        self.name = name
        self.i = 0

    def next(self):
        j = self.i % len(self.t)
        self.i += 1
        return self.t[j], (self.name, j)


class Prog:
    def __init__(self, dbg=None):
        self.dbg = dbg
        nc = self.nc = bass.Bass("TRN2", target_bir_lowering=False)
        self.uid = 0

        def din(name, shape, dt=F32):
            return nc.dram_tensor(name, list(shape), dt, kind="ExternalInput").ap()

        def dscr(name, shape, dt):
            kind = "ExternalOutput" if dbg else "Internal"
            return nc.dram_tensor(name, list(shape), dt, kind=kind).ap()

        self.xT_in = din("xT_in", [D, T])
        self.cvec = din("cvec", [128, 8, 2])
        self.w_mod = din("w_mod", [DEPTH, D, 6 * D])
        self.b_mod = din("b_mod_lay", [DEPTH, 128, 48])
        self.ng = din("norm_g_lay", [128, 5, 8])
        self.w_in = din("w_in", [DEPTH, D, IN_COLS])
        self.convw = din("convw_lay", [DEPTH, 128, 4, 5])
        self.lru_bd = din("lru_bd", [DEPTH, 2, 2, 4, 128, 128])
        self.lru_vec = din("lru_vec", [DEPTH, 128, 2, 3, 4])
        self.diff_lam = din("diff_lam_rep", [DEPTH, 128, 4, 64])
        self.smallg = din("smallg", [DEPTH, 128, 8])
        self.w_uq = din("mla_w_uq", [DEPTH, 384, 768])
        self.w_ukv = din("mla_w_ukv", [DEPTH, 256, 1024])
        self.w_br = din("w_branch", [DEPTH, 4, 512, 1024])
        self.w_out = din("w_out", [DEPTH, D, D])
        self.w_pq = din("peer_wq", [DEPTH, D, D])
        self.keysbd = din("keysbd", [DEPTH, 8, 128, 256])
        self.UT = din("peer_uT", [DEPTH, D, 16384])
        self.VP = din("peer_vP", [DEPTH, 16384, D])
        self.rope64 = din("rope64", [2, 128, NLAT])
        self.ropem = din("ropem", [2, 96, NLAT])
        self.rmats = din("rmats", [3, 128, 128])
        self.consts = din("consts", [3, 128, 128])
        self.iota_in = din("iota_in", [128, 128])
        self.outT = nc.dram_tensor("outT", [D, NLAT], F32, kind="ExternalOutput").ap()

        self.xT_a = dscr("xT_a", [D, T], F32)
        self.xaT = dscr("xaT", [512, T], F32)
        self.yaT = dscr("yaT", [512, T], BF16)
        self.qbT = dscr("qbT", [512, T], BF16)
        self.kbT = dscr("kbT", [512, T], BF16)
        self.vb = dscr("vb", [T, 512], BF16)
        self.qgT = dscr("qgT", [512, T], BF16)
        self.kgT = dscr("kgT", [128, T], BF16)
        self.vg = dscr("vg", [T, 128], BF16)
        self.qmT = dscr("qmT", [8, 96, T], BF16)
        self.kmT = dscr("kmT", [8, 96, T], BF16)
        self.vm = dscr("vm", [T, 512], BF16)
        self.gT = dscr("gT", [4096, T], BF16)
        self.oT = dscr("oT", [4, 512, T], BF16)
        self.h2T = dscr("h2T", [D, T], BF16)
        self.ubf = [nc.dram_tensor("ubf%d" % l, [D, 16384], BF16, kind="Internal").ap() for l in range(DEPTH)]
        self.vbf = [nc.dram_tensor("vbf%d" % l, [16384, D], BF16, kind="Internal").ap() for l in range(DEPTH)]

    def sb(self, st, name, shape, dt):
        self.uid += 1
        return st.enter_context(self.nc.sbuf_tensor("%s_%d" % (name, self.uid), list(shape), dt))

    def ring(self, st, name, shape, dt, n):
        self.uid += 1
        return Ring([self.sb(st, "%s%d" % (name, i), shape, dt) for i in range(n)], "%s_%d" % (name, self.uid))

    def dma(self, q, out, in_, reads=(), writes=()):
        return self.S.op(q, lambda e: e.dma_start(out=out, in_=in_), reads=reads, writes=writes, dma=True)

    def build(self):
        nc = self.nc
        es = ExitStack()
        with es:
            S = self.S = Sched(nc, es)
            self.PS = [es.enter_context(nc.psum_tensor("ps%d" % i, [128, 512], F32)) for i in range(8)]
            sb, dma = self.sb, self.dma
            self.ident = sb(es, "ident", [128, 128], F32)
            self.ones_f = sb(es, "ones_f", [128, 128], F32)
            self.bd64_f = sb(es, "bd64_f", [128, 128], F32)
            self.ones_b = sb(es, "ones_b", [128, 128], BF16)
            self.r64 = sb(es, "r64", [128, 128], BF16)
            self.r96 = sb(es, "r96", [128, 128], BF16)
            self.r32 = sb(es, "r32", [128, 128], BF16)
            self.epsc = sb(es, "epsc", [128, 1], F32)
            self.iota_f = sb(es, "iota_f", [128, 128], F32)
            self.cv = sb(es, "cv", [128, 8, 2], F32)
            self.ngs = sb(es, "ngs", [128, 5, 8], F32)
            self.modT = sb(es, "modT", [128, 48, 2], F32)
            self.A1 = sb(es, "A1", [128, 8, 2], F32)
            dma("sp", self.ident[:], self.consts[0], writes=["ident"])
            dma("sp", self.ones_f[:], self.consts[1], writes=["ones_f"])
            dma("sp", self.bd64_f[:], self.consts[2], writes=["bd64_f"])
            dma("pool", self.ones_b[:], self.consts[1], writes=["ones_b"])
            dma("pool", self.r64[:], self.rmats[0], writes=["r64"])
            dma("pool", self.r96[:], self.rmats[1], writes=["r96"])
            dma("pool", self.r32[:], self.rmats[2], writes=["r32"])
            dma("sp", self.iota_f[:], self.iota_in, writes=["iota_f"])
            dma("sp", self.cv[:], self.cvec, writes=["cv"])
            dma("sp", self.ngs[:], self.ng, writes=["ngs"])
            S.op("dve", lambda e: e.memset(self.epsc[:], EPS), writes=["epsc"])
            S.op("act", lambda e: e.activation(out=self.cv[:], in_=self.cv[:], func=AF.Silu), reads=["cv"], writes=["cv"])
            if self.dbg in (None, "peer"):
                for l in range(DEPTH):
                    for i in range(8):
                        dma("pool", self.ubf[l][i * 128:(i + 1) * 128, :], self.UT[l, i * 128:(i + 1) * 128, :])
                    for i in range(16):
                        dma("pool", self.vbf[l][i * 1024:(i + 1) * 1024, :], self.VP[l, i * 1024:(i + 1) * 1024, :])
            xcur = self.xT_in
            stop = False
            for l in range(DEPTH):
                need_ctx = l < DEPTH - 1
                lam_init = 0.8 - 0.6 * math.exp(-0.3 * l)
                S.barrier()
                self.mod_phase(l)
                if self.dbg == "mod":
                    dm = nc.dram_tensor("dbg_mod", [128, 96], F32, kind="ExternalOutput").ap()
                    dma("sp", dm, self.modT[:].rearrange("p j s -> p (j s)"), reads=["modT"])
                    break
                S.barrier()
                self.inproj_phase(l, xcur)
                if self.dbg == "inproj":
                    break
                S.barrier()
                self.lru_phase(l)
                if self.dbg == "lru":
                    break
                S.barrier()
                self.attn_phase(l, need_ctx, lam_init)
                if self.dbg == "attn":
                    break
                S.barrier()
                self.merge_phase(l, need_ctx, xcur)
                xcur = self.xT_a
                if self.dbg == "merge":
                    break
                S.barrier()
                self.peer_phase(l, need_ctx, final=(l == DEPTH - 1))
                if self.dbg == "peer":
                    break
            S.barrier()
        return nc

    def mod_phase(self, l):
        S, PS, dma = self.S, self.PS, self.dma
        with ExitStack() as st:
            wm = self.ring(st, "wm", [128, 8, 512], F32, 2)
            bm = self.sb(st, "bm", [128, 48], F32)
            dma("sp", bm[:], self.b_mod[l], writes=["bm"])
            wsrc = self.w_mod[l].rearrange("(k p) c -> p k c", p=128)
            for g in range(12):
                wt, wk = wm.next()
                dma("sp", wt[:], wsrc[:, :, g * 512:(g + 1) * 512], writes=[wk])
                for jj in range(4):
                    j = g * 4 + jj
                    for k in range(8):
                        S.op("pe", lambda e, wt=wt, jj=jj, k=k, j=j: e.matmul(
                            PS[0][:, 2 * j:2 * j + 2], lhsT=wt[:, k, jj * 128:(jj + 1) * 128], rhs=self.cv[:, k, :],
                            start=(k == 0), stop=(k == 7)), reads=[wk, "cv"], writes=[("ps", 0)])
            for s in range(2):
                S.op("dve", lambda e, s=s: e.tensor_tensor(
                    out=self.modT[:, :, s], in0=PS[0][:, 0:96].rearrange("p (j s) -> p j s", s=2)[:, :, s],
                    in1=bm[:], op=ALU.add), reads=[("ps", 0), "bm"], writes=["modT"])
            S.barrier()

    def make_A(self, gidx, scale_idx):
        for s in range(2):
            self.S.op("dve", lambda e, s=s: e.scalar_tensor_tensor(
                out=self.A1[:, :, s], in0=self.modT[:, scale_idx * 8:(scale_idx + 1) * 8, s], scalar=1.0,
                in1=self.ngs[:, gidx, :], op0=ALU.add, op1=ALU.mult), reads=["modT", "ngs"], writes=["A1"])

    def adaln_tile(self, xt, xkey, n, acol_fn, bias_fn, hout_fn, hkey, sq, rs, tmpr, psb):
        S, PS = self.S, self.PS
        for k in range(8):
            S.op("act", lambda e, k=k: e.activation(out=sq[:, k, :n], in_=xt[:, k, :n], func=AF.Square),
                 reads=[xkey], writes=["sq"])
        for k in range(8):
            S.op("pe", lambda e, k=k: e.matmul(PS[psb][:, :n], lhsT=self.ones_f[:], rhs=sq[:, k, :n],
                                               start=(k == 0), stop=(k == 7)),
                 reads=["sq", "ones_f"], writes=[("ps", psb)])
        S.op("act", lambda e: e.activation(out=rs[:, :n], in_=PS[psb][:, :n], func=AF.Sqrt,
                                           bias=self.epsc[:], scale=1.0 / D),
             reads=[("ps", psb), "epsc"], writes=["rs"])
        S.op("dve", lambda e: e.reciprocal(out=rs[:, :n], in_=rs[:, :n]), reads=["rs"], writes=["rs"])
        for k in range(8):
            tt, tk = tmpr.next()
            S.op("dve", lambda e, k=k, tt=tt: e.scalar_tensor_tensor(
                out=tt[:, :n], in0=xt[:, k, :n], scalar=acol_fn(k), in1=rs[:, :n],
                op0=ALU.mult, op1=ALU.mult), reads=[xkey, "A1", "rs"], writes=[tk])
            b = bias_fn(k)
            if b is None:
                S.op("act", lambda e, k=k, tt=tt: e.copy(out=hout_fn(k), in_=tt[:, :n]), reads=[tk], writes=[hkey])
            else:
                S.op("act", lambda e, k=k, tt=tt, b=b: e.activation(
                    out=hout_fn(k), in_=tt[:, :n], func=AF.Identity, bias=b, scale=1.0),
                    reads=[tk, "modT"], writes=[hkey])

    def rope(self, P, n, xb, xkey, Rm, Rkey, Ct, St, tabkey, out_ap, outkey, bank, stg_f):
        S, PS = self.S, self.PS
        S.op("pe", lambda e: e.matmul(PS[bank][:P, :n], lhsT=Rm[:P, :P], rhs=xb[:P, :n], start=True, stop=True),
             reads=[xkey, Rkey], writes=[("ps", bank)])
        t1, t1k = stg_f.next()
        S.op("dve", lambda e: e.tensor_tensor(out=t1[:P, :n], in0=xb[:P, :n], in1=Ct[:P, :n], op=ALU.mult),
             reads=[xkey, tabkey], writes=[t1k])
        t2, t2k = stg_f.next()
        S.op("dve", lambda e: e.tensor_tensor(out=t2[:P, :n], in0=PS[bank][:P, :n], in1=St[:P, :n], op=ALU.mult),
             reads=[("ps", bank), tabkey], writes=[t2k])
        S.op("pool", lambda e: e.tensor_tensor(out=out_ap, in0=t1[:P, :n], in1=t2[:P, :n], op=ALU.add),
             reads=[t1k, t2k], writes=[outkey])

    def inproj_phase(self, l, xcur):
        S, PS, dma, nc = self.S, self.PS, self.dma, self.nc
        with ExitStack() as st:
            hT = self.sb(st, "hT", [128, 8, T], BF16)
            self.make_A(l, 1)
            with ExitStack() as st1:
                xr = self.ring(st1, "xr", [128, 8, 512], F32, 2)
                sq = self.sb(st1, "sq", [128, 8, 512], F32)
                rs = self.sb(st1, "rs", [128, 512], F32)
                tmpr = self.ring(st1, "tmpr", [128, 512], F32, 2)
                xsrc = xcur.rearrange("(k p) t -> p k t", p=128)
                for ti, (t0, n) in enumerate(TILES):
                    xt, xk = xr.next()
                    dma("sp", xt[:, :, :n], xsrc[:, :, t0:t0 + n], writes=[xk])
                    s = 1 if ti == 0 else 0
                    self.adaln_tile(xt, xk, n, lambda k, s=s: self.A1[:, k, s:s + 1],
                                    lambda k, s=s: self.modT[:, k, s:s + 1],
                                    lambda k, t0=t0, n=n: hT[:, k, t0:t0 + n], ("hT", ti), sq, rs, tmpr, 7)
                S.barrier()
            if self.dbg == "adaln":
                dh = nc.dram_tensor("dbg_h", [D, T], BF16, kind="ExternalOutput").ap()
                dma("sp", dh.rearrange("(k p) t -> p k t", p=128), hT[:])
                return
            wr = self.ring(st, "wsl", [128, 8, 512], BF16, 2)
            wmla = self.sb(st, "wmla", [128, 8, 672], BF16)
            wuq = self.sb(st, "wuq", [128, 3, 768], BF16)
            wkk = self.sb(st, "wkk", [128, 2, 8, 64], BF16)
            wkv = self.sb(st, "wkv", [128, 2, 8, 64], BF16)
            sg = self.sb(st, "sg", [128, 8], F32)
            stg_f = self.ring(st, "stgf", [128, 512], F32, 4)
            stg_b = self.ring(st, "stgb", [128, 512], BF16, 4)
            xbr = self.ring(st, "xbr", [128, 512], BF16, 3)
            tabC = self.ring(st, "tabC", [128, 512], F32, 2)
            tabS = self.ring(st, "tabS", [128, 512], F32, 2)
            sqn = self.sb(st, "sqn", [128, 512], F32)
            rsn = self.sb(st, "rsn", [128, 512], F32)
            zc = self.sb(st, "zc", [128, 5, 512], F32)
            sqm = self.sb(st, "sqm", [128, 5, 512], F32)
            cqn = self.sb(st, "cqn", [128, 3, 512], BF16)
            ckvn = self.sb(st, "ckvn", [128, 2, 512], BF16)
            wsrc = self.w_in[l].rearrange("(k p) c -> p k c", p=128)
            dma("sp", sg[:], self.smallg[l], writes=["sg"])
            bankc = [0]

            def nbank():
                b = bankc[0] % 3
                bankc[0] += 1
                return b

            def mm_chunk(bank, wt, wk, c0, m, ti):
                t0, n = TILES[ti]
                for k in range(8):
                    S.op("pe", lambda e, k=k: e.matmul(PS[bank][:m, :n], lhsT=wt[:, k, c0:c0 + m],
                                                       rhs=hT[:, k, t0:t0 + n], start=(k == 0), stop=(k == 7)),
                         reads=[wk], writes=[("ps", bank)])

            def load_tabs(src, P, ti, prow0=0):
                t0, n = TILES[ti]
                Ct, ck = tabC.next()
                St, sk = tabS.next()
                dma("sp", Ct[:P, :n], src[0, prow0:prow0 + P, t0 - NCTX:t0 - NCTX + n], writes=[ck])
                dma("sp", St[:P, :n], src[1, prow0:prow0 + P, t0 - NCTX:t0 - NCTX + n], writes=[sk])
                return Ct, St, ck, sk

            def store(dst_ap, src_ap, key):
                dma("sp", dst_ap, src_ap, reads=[key])

            def evac_act(bank, m, n, func, dt_ring, **kw):
                o, ok = dt_ring.next()
                S.op("act", lambda e: e.activation(out=o[:m, :n], in_=PS[bank][:m, :n], func=func, **kw),
                     reads=[("ps", bank)], writes=[ok])
                return o, ok

            def tokmajor(wt, wk, c0, w, dst, ti):
                t0, n = TILES[ti]
                for sub in range(n // 128):
                    bank = nbank()
                    for k in range(8):
                        S.op("pe", lambda e, k=k, sub=sub: e.matmul(
                            PS[bank][:, :w], lhsT=hT[:, k, t0 + sub * 128:t0 + (sub + 1) * 128],
                            rhs=wt[:, k, c0:c0 + w], start=(k == 0), stop=(k == 7)),
                            reads=[wk], writes=[("ps", bank)])
                    o, ok = evac_act(bank, 128, w, AF.Copy, stg_b)
                    store(dst[t0 + sub * 128:t0 + (sub + 1) * 128, :], o[:, :w], ok)

            def rope_store(P, n, ti, xb, xk, Rm, Rkey, tabs, dst_ap, bank=3):
                if ti == 0:
                    store(dst_ap, xb[:P, :n], xk)
                else:
                    Ct, St, ck, sk = tabs
                    o, ok = stg_b.next()
                    S.op("pe", lambda e: e.matmul(PS[bank][:P, :n], lhsT=Rm[:P, :P], rhs=xb[:P, :n],
                                                  start=True, stop=True),
                         reads=[xk, Rkey], writes=[("ps", bank)])
                    t1, t1k = stg_f.next()
                    S.op("dve", lambda e: e.tensor_tensor(out=t1[:P, :n], in0=xb[:P, :n], in1=Ct[:P, :n],
                                                          op=ALU.mult), reads=[xk, ck], writes=[t1k])
                    t2, t2k = stg_f.next()
                    S.op("dve", lambda e: e.tensor_tensor(out=t2[:P, :n], in0=PS[bank][:P, :n], in1=St[:P, :n],
                                                          op=ALU.mult), reads=[("ps", bank), sk], writes=[t2k])
                    S.op("pool", lambda e: e.tensor_tensor(out=o[:P, :n], in0=t1[:P, :n], in1=t2[:P, :n],
                                                           op=ALU.add), reads=[t1k, t2k], writes=[ok])
                    store(dst_ap, o[:P, :n], ok)

            def headnorm(bank, n, gcol, inv_dim, ones_mat, ones_key):
                S.op("act", lambda e: e.activation(out=sqn[:, :n], in_=PS[bank][:, :n], func=AF.Square),
                     reads=[("ps", bank)], writes=["sqn"])
                S.op("pe", lambda e: e.matmul(PS[4][:, :n], lhsT=ones_mat[:], rhs=sqn[:, :n], start=True, stop=True),
                     reads=["sqn", ones_key], writes=[("ps", 4)])
                S.op("act", lambda e: e.activation(out=rsn[:, :n], in_=PS[4][:, :n], func=AF.Sqrt,
                                                   bias=self.epsc[:], scale=inv_dim),
                     reads=[("ps", 4), "epsc"], writes=["rsn"])
                S.op("dve", lambda e: e.reciprocal(out=rsn[:, :n], in_=rsn[:, :n]), reads=["rsn"], writes=["rsn"])
                xb, xk = xbr.next()
                S.op("dve", lambda e: e.scalar_tensor_tensor(out=xb[:, :n], in0=PS[bank][:, :n], scalar=gcol,
                                                             in1=rsn[:, :n], op0=ALU.mult, op1=ALU.mult),
                     reads=[("ps", bank), "rsn", "sg"], writes=[xk])
                return xb, xk

            groups = [("xa", 0, 512), ("ya", 512, 512), ("qb", 1024, 512), ("kb", 1536, 512), ("vb", 2048, 512),
                      ("qg", 2560, 512), ("kvg", 3072, 256)] + [("zg%d" % i, 4000 + 512 * i, 512) for i in range(8)]
            for gname, c0, w in groups:
                wt, wk = wr.next()
                dma("pool", wt[:, :, :w], wsrc[:, :, c0:c0 + w], writes=[wk])
                for ti, (t0, n) in enumerate(TILES):
                    if gname == "vb":
                        tokmajor(wt, wk, 0, 512, self.vb, ti)
                        continue
                    if gname == "kvg":
                        tokmajor(wt, wk, 128, 128, self.vg, ti)
                    tabs = None
                    if gname in ("qb", "kb", "qg", "kvg") and ti > 0:
                        tabs = load_tabs(self.rope64, 128, ti)
                    nch = 1 if gname == "kvg" else w // 128
                    for ch in range(nch):
                        bank = nbank()
                        mm_chunk(bank, wt, wk, ch * 128, 128, ti)
                        if gname == "xa":
                            o, ok = evac_act(bank, 128, n, AF.Copy, stg_f)
                            store(self.xaT[ch * 128:(ch + 1) * 128, t0:t0 + n], o[:, :n], ok)
                        elif gname == "ya":
                            o, ok = evac_act(bank, 128, n, AF.Gelu_apprx_tanh, stg_b)
                            store(self.yaT[ch * 128:(ch + 1) * 128, t0:t0 + n], o[:, :n], ok)
                        elif gname.startswith("zg"):
                            o, ok = evac_act(bank, 128, n, AF.Sigmoid, stg_b)
                            r0 = (c0 - 4000) + ch * 128
                            store(self.gT[r0:r0 + 128, t0:t0 + n], o[:, :n], ok)
                        elif gname in ("qb", "kb"):
                            xb, xk = evac_act(bank, 128, n, AF.Copy, xbr)
                            dst = self.qbT if gname == "qb" else self.kbT
                            rope_store(128, n, ti, xb, xk, self.r64, "r64", tabs,
                                       dst[ch * 128:(ch + 1) * 128, t0:t0 + n])
                        elif gname == "qg":
                            xb, xk = headnorm(bank, n, sg[:, 1:2], 1.0 / 64, self.bd64_f, "bd64_f")
                            rope_store(128, n, ti, xb, xk, self.r64, "r64", tabs,
                                       self.qgT[ch * 128:(ch + 1) * 128, t0:t0 + n])
                        elif gname == "kvg":
                            xb, xk = headnorm(bank, n, sg[:, 2:3], 1.0 / 64, self.bd64_f, "bd64_f")
                            rope_store(128, n, ti, xb, xk, self.r64, "r64", tabs, self.kgT[:, t0:t0 + n])
            dma("pool", wmla[:], wsrc[:, :, 3328:4000], writes=["wmla"])
            dma("pool", wuq[:], self.w_uq[l].rearrange("(j p) c -> p j c", p=128), writes=["wuq"])
            ukv5 = self.w_ukv[l].rearrange("(j p) (h t d) -> p j h t d", p=128, h=8, t=2)
            for j in range(2):
                dma("pool", wkk[:, j], ukv5[:, j, :, 0, :], writes=["wkk"])
                dma("pool", wkv[:, j], ukv5[:, j, :, 1, :], writes=["wkv"])
            for ti, (t0, n) in enumerate(TILES):
                tq = tk_ = None
                if ti > 0:
                    tq = load_tabs(self.ropem, 96, ti)
                    tk_ = load_tabs(self.ropem, 32, ti, prow0=64)
                for j in range(5):
                    bank = nbank()
                    mm_chunk(bank, wmla, "wmla", j * 128, 128, ti)
                    S.op("act", lambda e, j=j: e.copy(out=zc[:, j, :n], in_=PS[bank][:, :n]),
                         reads=[("ps", bank)], writes=["zc"])
                    S.op("act", lambda e, j=j: e.activation(out=sqm[:, j, :n], in_=PS[bank][:, :n], func=AF.Square),
                         reads=[("ps", bank)], writes=["sqm"])
                for (j0, nj, inv, gc0, dstn, dkey) in ((0, 3, 1.0 / 384, 3, cqn, "cqn"), (3, 2, 1.0 / 256, 6, ckvn, "ckvn")):
                    for jj in range(nj):
                        S.op("pe", lambda e, jj=jj: e.matmul(PS[4][:, :n], lhsT=self.ones_f[:], rhs=sqm[:, j0 + jj, :n],
                                                             start=(jj == 0), stop=(jj == nj - 1)),
                             reads=["sqm", "ones_f"], writes=[("ps", 4)])
                    S.op("act", lambda e: e.activation(out=rsn[:, :n], in_=PS[4][:, :n], func=AF.Sqrt,
                                                       bias=self.epsc[:], scale=inv),
                         reads=[("ps", 4), "epsc"], writes=["rsn"])
                    S.op("dve", lambda e: e.reciprocal(out=rsn[:, :n], in_=rsn[:, :n]), reads=["rsn"], writes=["rsn"])
                    for jj in range(nj):
                        S.op("dve", lambda e, jj=jj: e.scalar_tensor_tensor(
                            out=dstn[:, jj, :n], in0=zc[:, j0 + jj, :n], scalar=sg[:, gc0 + jj:gc0 + jj + 1],
                            in1=rsn[:, :n], op0=ALU.mult, op1=ALU.mult), reads=["zc", "rsn", "sg"], writes=[dkey])
                for h in range(8):
                    bank = nbank()
                    for j in range(3):
                        S.op("pe", lambda e, j=j, h=h: e.matmul(PS[bank][:96, :n], lhsT=wuq[:, j, h * 96:(h + 1) * 96],
                                                                rhs=cqn[:, j, :n], start=(j == 0), stop=(j == 2)),
                             reads=["wuq", "cqn"], writes=[("ps", bank)])
                    xb, xk = evac_act(bank, 96, n, AF.Copy, xbr)
                    rope_store(96, n, ti, xb, xk, self.r96, "r96", tq, self.qmT[h, :, t0:t0 + n])
                for h in range(8):
                    bank = nbank()
                    for j in range(2):
                        S.op("pe", lambda e, j=j, h=h: e.matmul(PS[bank][:64, :n], lhsT=wkk[:, j, h, :],
                                                                rhs=ckvn[:, j, :n], start=(j == 0), stop=(j == 1)),
                             reads=["wkk", "ckvn"], writes=[("ps", bank)])
                    o, ok = evac_act(bank, 64, n, AF.Copy, stg_b)
                    store(self.kmT[h, 0:64, t0:t0 + n], o[:64, :n], ok)
                for sub in range(n // 128):
                    bank = nbank()
                    for j in range(2):
                        S.op("pe", lambda e, j=j, sub=sub: e.matmul(
                            PS[bank][:, :512], lhsT=ckvn[:, j, sub * 128:(sub + 1) * 128],
                            rhs=wkv[:, j].rearrange("p h d -> p (h d)"), start=(j == 0), stop=(j == 1)),
                            reads=["wkv", "ckvn"], writes=[("ps", bank)])
                    o, ok = evac_act(bank, 128, 512, AF.Copy, stg_b)
                    store(self.vm[t0 + sub * 128:t0 + (sub + 1) * 128, :], o[:, :], ok)
                bank = nbank()
                mm_chunk(bank, wmla, "wmla", 640, 32, ti)
                xb, xk = evac_act(bank, 32, n, AF.Copy, xbr)
                if ti == 0:
                    for h in range(8):
                        store(self.kmT[h, 64:96, t0:t0 + n], xb[:32, :n], xk)
                else:
                    Ct, St, ck, sk = tk_
                    o, ok = stg_b.next()
                    S.op("pe", lambda e: e.matmul(PS[3][:32, :n], lhsT=self.r32[:32, :32], rhs=xb[:32, :n],
                                                  start=True, stop=True), reads=[xk, "r32"], writes=[("ps", 3)])
                    t1, t1k = stg_f.next()
                    S.op("dve", lambda e: e.tensor_tensor(out=t1[:32, :n], in0=xb[:32, :n], in1=Ct[:32, :n],
                                                          op=ALU.mult), reads=[xk, ck], writes=[t1k])
                    t2, t2k = stg_f.next()
                    S.op("dve", lambda e: e.tensor_tensor(out=t2[:32, :n], in0=PS[3][:32, :n], in1=St[:32, :n],
                                                          op=ALU.mult), reads=[("ps", 3), sk], writes=[t2k])
                    S.op("pool", lambda e: e.tensor_tensor(out=o[:32, :n], in0=t1[:32, :n], in1=t2[:32, :n],
                                                           op=ALU.add), reads=[t1k, t2k], writes=[ok])
                    for h in range(8):
                        store(self.kmT[h, 64:96, t0:t0 + n], o[:32, :n], ok)
            S.barrier()

    def lru_phase(self, l):
        S, PS, dma = self.S, self.PS, self.dma
        with ExitStack() as st:
            xa = self.sb(st, "xa", [128, T], F32)
            u = self.sb(st, "u", [128, T], F32)
            ub = self.sb(st, "ub", [128, T], BF16)
            gy = self.sb(st, "gy", [128, T], BF16)
            at = self.sb(st, "at", [128, T], F32)
            bt = self.sb(st, "bt", [128, T], F32)
            tm = self.sb(st, "tm", [128, T], F32)
            hf = self.sb(st, "hf", [128, T], F32)
            hr = self.sb(st, "hr", [128, T], F32)
            ho = self.sb(st, "ho", [128, T], BF16)
            bdw = self.ring(st, "bdw", [128, 128], BF16, 4)
            cw = self.sb(st, "cw", [128, 4, 5], F32)
            lv = self.sb(st, "lv", [128, 2, 3, 4], F32)
            negk = self.sb(st, "negk", [128, 2, 4], F32)
            dma("sp", cw[:], self.convw[l], writes=["cw"])
            dma("sp", lv[:], self.lru_vec[l], writes=["lv"])
            for d in range(2):
                S.op("act", lambda e, d=d: e.activation(out=negk[:, d, :], in_=lv[:, d, 2, :], func=AF.Exp, scale=-1.0),
                     reads=["lv"], writes=["negk"])
                S.op("act", lambda e, d=d: e.activation(out=negk[:, d, :], in_=negk[:, d, :], func=AF.Ln, bias=1.0),
                     reads=["negk"], writes=["negk"])
                S.op("dve", lambda e, d=d: e.tensor_scalar(out=negk[:, d, :], in0=negk[:, d, :], scalar1=-8.0,
                                                           scalar2=None, op0=ALU.mult), reads=["negk"], writes=["negk"])
            segs = [(0, NCTX), (NCTX, T)]
            for cc in range(4):
                dma("sp", xa[:], self.xaT[cc * 128:(cc + 1) * 128, :], writes=["xa"])
                dma("sp", gy[:], self.yaT[cc * 128:(cc + 1) * 128, :], writes=["gy"])
                S.op("act", lambda e: e.activation(out=u[:], in_=xa[:], func=AF.Identity, scale=cw[:, cc, 1:2],
                                                   bias=cw[:, cc, 4:5]), reads=["xa", "cw"], writes=["u"])
                for (s0, e0) in segs:
                    for (tap, osl, isl) in ((0, (s0 + 1, e0), (s0, e0 - 1)), (2, (s0, e0 - 1), (s0 + 1, e0)),
                                            (3, (s0, e0 - 2), (s0 + 2, e0))):
                        S.op("dve", lambda e, tap=tap, osl=osl, isl=isl: e.scalar_tensor_tensor(
                            out=u[:, osl[0]:osl[1]], in0=xa[:, isl[0]:isl[1]], scalar=cw[:, cc, tap:tap + 1],
                            in1=u[:, osl[0]:osl[1]], op0=ALU.mult, op1=ALU.add), reads=["xa", "u", "cw"], writes=["u"])
                S.op("pool", lambda e: e.tensor_copy(out=ub[:], in_=u[:]), reads=["u"], writes=["ub"])
                for d in range(2):
                    wts = []
                    for kind in range(2):
                        wt, wk = bdw.next()
                        dma("pool", wt[:], self.lru_bd[l, d, kind, cc], writes=[wk])
                        wts.append((wt, wk))
                    for ti, (t0, n) in enumerate(TILES):
                        for kind in range(2):
                            bank = (ti * 2 + kind) % 4
                            wt, wk = wts[kind]
                            S.op("pe", lambda e, wt=wt: e.matmul(PS[bank][:, :n], lhsT=wt[:], rhs=ub[:, t0:t0 + n],
                                                                 start=True, stop=True),
                                 reads=[wk, "ub"], writes=[("ps", bank)])
                            dst, dk = (at, "at") if kind == 0 else (bt, "bt")
                            S.op("act", lambda e, dst=dst, kind=kind: e.activation(
                                out=dst[:, t0:t0 + n], in_=PS[bank][:, :n], func=AF.Sigmoid,
                                bias=lv[:, d, kind, cc:cc + 1], scale=1.0), reads=[("ps", bank), "lv"], writes=[dk])
                    S.op("act", lambda e: e.activation(out=at[:], in_=at[:], func=AF.Exp, scale=negk[:, d, cc:cc + 1]),
                         reads=["at", "negk"], writes=["at"])
                    S.op("pool", lambda e: e.tensor_tensor(out=tm[:], in0=at[:], in1=at[:], op=ALU.mult),
                         reads=["at"], writes=["tm"])
                    S.op("act", lambda e: e.activation(out=tm[:], in_=tm[:], func=AF.Sqrt, scale=-1.0, bias=1.0),
                         reads=["tm"], writes=["tm"])
                    S.op("dve", lambda e: e.tensor_tensor(out=bt[:], in0=bt[:], in1=u[:], op=ALU.mult),
                         reads=["bt", "u"], writes=["bt"])
                    S.op("dve", lambda e: e.tensor_tensor(out=bt[:], in0=bt[:], in1=tm[:], op=ALU.mult),
                         reads=["bt", "tm"], writes=["bt"])
                    if d == 0:
                        pieces = [(0, 1088), (1088, 2176), (2176, 3264), (3264, T)]
                        for pi, (p0, p1) in enumerate(pieces):
                            init = 0.0 if pi == 0 else hf[:, p0 - 1:p0]
                            S.op("dve", lambda e, p0=p0, p1=p1, init=init: e.tensor_tensor_scan(
                                out=hf[:, p0:p1], data0=at[:, p0:p1], data1=bt[:, p0:p1], initial=init,
                                op0=ALU.mult, op1=ALU.add), reads=["at", "bt", "hf"], writes=["hf"])
                    else:
                        S.op("dve", lambda e: e.tensor_tensor_scan(
                            out=hr[:, 0:NCTX][:, ::-1], data0=at[:, 0:NCTX][:, ::-1], data1=bt[:, 0:NCTX][:, ::-1],
                            initial=0.0, op0=ALU.mult, op1=ALU.add), reads=["at", "bt"], writes=["hr"])
                        pieces = [(3328, T), (2304, 3328), (1280, 2304), (NCTX, 1280)]
                        for pi, (p0, p1) in enumerate(pieces):
                            init = hr[:, 0:1] if pi == 0 else hr[:, p1:p1 + 1]
                            S.op("dve", lambda e, p0=p0, p1=p1, init=init: e.tensor_tensor_scan(
                                out=hr[:, p0:p1][:, ::-1], data0=at[:, p0:p1][:, ::-1], data1=bt[:, p0:p1][:, ::-1],
                                initial=init, op0=ALU.mult, op1=ALU.add), reads=["at", "bt", "hr"], writes=["hr"])
                S.op("pool", lambda e: e.tensor_tensor(out=hf[:], in0=hf[:], in1=hr[:], op=ALU.add),
                     reads=["hf", "hr"], writes=["hf"])
                S.op("dve", lambda e: e.tensor_tensor(out=ho[:], in0=hf[:], in1=gy[:], op=ALU.mult),
                     reads=["hf", "gy"], writes=["ho"])
                dma("sp", self.oT[0, cc * 128:(cc + 1) * 128, :], ho[:], reads=["ho"])
            S.barrier()

    def attn_phase(self, l, need_ctx, lam_init):
        S, PS, dma = self.S, self.PS, self.dma
        with ExitStack() as st:
            kT = self.ring(st, "kT", [128, T], BF16, 2)
            vt = self.ring(st, "vt", [128, NKB, 128], BF16, 2)
            qt = self.ring(st, "qt", [128, 512], BF16, 2)
            pT = self.ring(st, "pT", [128, 512], BF16, 4)
            rec = self.ring(st, "rec", [128, 512], F32, 2)
            of = self.ring(st, "of", [128, 512], F32, 3)
            ob = self.ring(st, "ob", [128, 512], BF16, 2)
            sqd = self.sb(st, "sqd", [128, 512], F32)
            rsd = self.sb(st, "rsd", [128, 512], F32)
            sg = self.sb(st, "sga", [128, 8], F32)
            dl = self.sb(st, "dl", [128, 4, 64], F32)
            lam = self.sb(st, "lam", [128, 4], F32)
            dma("sp", sg[:], self.smallg[l], writes=["sga"])
            dma("sp", dl[:], self.diff_lam[l], writes=["dl"])
            for i in range(2):
                S.op("dve", lambda e, i=i: e.tensor_tensor(out=dl[:, 2 * i, :], in0=dl[:, 2 * i, :],
                                                           in1=dl[:, 2 * i + 1, :], op=ALU.mult), reads=["dl"], writes=["dl"])
                S.op("dve", lambda e, i=i: e.tensor_reduce(out=lam[:, i:i + 1], in_=dl[:, 2 * i, :], axis=AX.X, op=ALU.add),
                     reads=["dl"], writes=["lam"])
            S.op("act", lambda e: e.activation(out=lam[:, 0:2], in_=lam[:, 0:2], func=AF.Exp), reads=["lam"], writes=["lam"])
            S.op("dve", lambda e: e.tensor_tensor(out=lam[:, 2:3], in0=lam[:, 1:2], in1=lam[:, 0:1], op=ALU.subtract),
                 reads=["lam"], writes=["lam"])
            S.op("dve", lambda e: e.tensor_scalar(out=lam[:, 2:3], in0=lam[:, 2:3], scalar1=-lam_init, scalar2=None,
                                                  op0=ALU.add), reads=["lam"], writes=["lam"])
            S.op("dve", lambda e: e.tensor_scalar(out=lam[:, 3:4], in0=sg[:, 0:1], scalar1=1.0 - lam_init, scalar2=None,
                                                  op0=ALU.mult), reads=["sga", "lam"], writes=["lam"])

            qgroups = ([(0, 256, 2)] if need_ctx else []) + [(256 + 512 * i, 512, NKB) for i in range(8)]

            def run_job(kt, kk, vtile, vk, dv, qsrc_fn, variants, scale, finish):
                for (q0, n, nkb) in qgroups:
                    q, qk = qt.next()
                    qsrc_fn(q, qk, q0, n)
                    nv = len(variants)
                    sbanks = [0, 1, 2, 7]
                    seq = [(kb, v) for kb in range(nkb) for v in range(nv)]

                    def emit_qk(idx):
                        kb, v = seq[idx]
                        p0, pn = variants[v]
                        bank = sbanks[idx % 4]
                        S.op("pe", lambda e: e.matmul(PS[bank][:, :n], lhsT=kt[p0:p0 + pn, kb * 128:(kb + 1) * 128],
                                                      rhs=q[p0:p0 + pn, :n], start=True, stop=True),
                             reads=[kk, qk], writes=[("ps", bank)])
                        return bank

                    bank_of = {0: emit_qk(0)}
                    for idx in range(len(seq)):
                        kb, v = seq[idx]
                        bank = bank_of.pop(idx)
                        p, pk = pT.next()
                        S.op("act", lambda e, p=p, bank=bank: e.activation(out=p[:, :n], in_=PS[bank][:, :n],
                                                                           func=AF.Exp, scale=scale),
                             reads=[("ps", bank)], writes=[pk])
                        if idx + 1 < len(seq):
                            bank_of[idx + 1] = emit_qk(idx + 1)
                        S.op("pe", lambda e, p=p, kb=kb, v=v: e.matmul(PS[3 + v][:dv, :n], lhsT=vtile[:, kb, :dv],
                                                                      rhs=p[:, :n], start=(kb == 0), stop=(kb == nkb - 1)),
                             reads=[pk, vk], writes=[("ps", 3 + v)])
                        S.op("pe", lambda e, p=p, kb=kb, v=v: e.matmul(PS[5 + v][:dv, :n], lhsT=self.ones_b[:, :dv],
                                                                      rhs=p[:, :n], start=(kb == 0), stop=(kb == nkb - 1)),
                             reads=[pk, "ones_b"], writes=[("ps", 5 + v)])
                    finish(q0, n)

            def load_v(src_ap, dv):
                vtile, vk = vt.next()
                dma("sp", vtile[:, :, :dv], src_ap.rearrange("(kb p) d -> p kb d", p=128), writes=[vk])
                return vtile, vk

            def normalized(v, dv, n):
                r, rk = rec.next()
                S.op("dve", lambda e: e.reciprocal(out=r[:dv, :n], in_=PS[5 + v][:dv, :n]), reads=[("ps", 5 + v)], writes=[rk])
                o, ok = of.next()
                S.op("dve", lambda e: e.tensor_tensor(out=o[:dv, :n], in0=PS[3 + v][:dv, :n], in1=r[:dv, :n], op=ALU.mult),
                     reads=[("ps", 3 + v), rk], writes=[ok])
                return o, ok

            for h in range(4):
                kt, kk = kT.next()
                dma("sp", kt[:], self.kbT[h * 128:(h + 1) * 128, :], writes=[kk])
                vtile, vk = load_v(self.vb[:, h * 128:(h + 1) * 128], 128)

                def qsrc(q, qk, q0, n, h=h):
                    dma("sp", q[:, :n], self.qbT[h * 128:(h + 1) * 128, q0:q0 + n], writes=[qk])

                def finish(q0, n, h=h):
                    o1, o1k = normalized(0, 128, n)
                    o2, o2k = normalized(1, 128, n)
                    od, odk = of.next()
                    S.op("dve", lambda e: e.scalar_tensor_tensor(out=od[:, :n], in0=o2[:, :n], scalar=lam[:, 2:3],
                                                                 in1=o1[:, :n], op0=ALU.mult, op1=ALU.add),
                         reads=[o1k, o2k, "lam"], writes=[odk])
                    S.op("act", lambda e: e.activation(out=sqd[:, :n], in_=od[:, :n], func=AF.Square),
                         reads=[odk], writes=["sqd"])
                    S.op("pe", lambda e: e.matmul(PS[7][:, :n], lhsT=self.ones_f[:], rhs=sqd[:, :n], start=True, stop=True),
                         reads=["sqd", "ones_f"], writes=[("ps", 7)])
                    S.op("act", lambda e: e.activation(out=rsd[:, :n], in_=PS[7][:, :n], func=AF.Sqrt,
                                                       bias=self.epsc[:], scale=1.0 / 128),
                         reads=[("ps", 7), "epsc"], writes=["rsd"])
                    S.op("dve", lambda e: e.reciprocal(out=rsd[:, :n], in_=rsd[:, :n]), reads=["rsd"], writes=["rsd"])
                    o, ok = ob.next()
                    S.op("dve", lambda e: e.scalar_tensor_tensor(out=o[:, :n], in0=od[:, :n], scalar=lam[:, 3:4],
                                                                 in1=rsd[:, :n], op0=ALU.mult, op1=ALU.mult),
                         reads=[odk, "rsd", "lam"], writes=[ok])
                    dma("sp", self.oT[1, h * 128:(h + 1) * 128, q0:q0 + n], o[:, :n], reads=[ok])

                run_job(kt, kk, vtile, vk, 128, qsrc, [(0, 64), (64, 64)], 0.125, finish)

            def simple_finish(branch, h):
                def fin(q0, n):
                    o1, o1k = normalized(0, 64, n)
                    o, ok = ob.next()
                    S.op("act", lambda e: e.copy(out=o[:64, :n], in_=o1[:64, :n]), reads=[o1k], writes=[ok])
                    dma("sp", self.oT[branch, h * 64:(h + 1) * 64, q0:q0 + n], o[:64, :n], reads=[ok])
                return fin

            for h in range(8):
                g = h // 4
                kt, kk = kT.next()
                dma("sp", kt[0:64, :], self.kgT[g * 64:(g + 1) * 64, :], writes=[kk])
                vtile, vk = load_v(self.vg[:, g * 64:(g + 1) * 64], 64)

                def qsrc(q, qk, q0, n, h=h):
                    dma("sp", q[0:64, :n], self.qgT[h * 64:(h + 1) * 64, q0:q0 + n], writes=[qk])

                run_job(kt, kk, vtile, vk, 64, qsrc, [(0, 64)], 0.125, simple_finish(2, h))
            for h in range(8):
                kt, kk = kT.next()
                dma("sp", kt[0:96, :], self.kmT[h], writes=[kk])
                vtile, vk = load_v(self.vm[:, h * 64:(h + 1) * 64], 64)

                def qsrc(q, qk, q0, n, h=h):
                    dma("sp", q[0:96, :n], self.qmT[h, :, q0:q0 + n], writes=[qk])

                run_job(kt, kk, vtile, vk, 64, qsrc, [(0, 96)], 96.0 ** -0.5, simple_finish(3, h))
            S.barrier()

    def merge_phase(self, l, need_ctx, xcur):
        S, PS, dma = self.S, self.PS, self.dma
        with ExitStack() as st:
            wb = self.sb(st, "wb", [128, 16, 1024], BF16)
            wo = self.sb(st, "wo", [128, 8, 1024], BF16)
            otr = self.ring(st, "otr", [128, 16, 512], BF16, 2)
            gtr = self.ring(st, "gtr", [128, 512], BF16, 4)
            macc = self.sb(st, "macc", [128, 8, 512], F32)
            mtmp = self.ring(st, "mtmp", [128, 512], F32, 2)
            mb = self.sb(st, "mb", [128, 8, 512], BF16)
            xr = self.ring(st, "xr2", [128, 8, 512], F32, 2)
            sq = self.sb(st, "sq2", [128, 8, 512], F32)
            rs = self.sb(st, "rs2", [128, 512], F32)
            tmpr = self.ring(st, "tmpr2", [128, 512], F32, 2)
            hst = self.ring(st, "hst", [128, 8, 512], BF16, 2)
            for k in range(4):
                dma("pool", wb[:, k * 4:(k + 1) * 4, :], self.w_br[l, k].rearrange("(c p) d -> p c d", p=128), writes=["wb"])
            dma("pool", wo[:], self.w_out[l].rearrange("(k p) d -> p k d", p=128), writes=["wo"])
            self.make_A(2 + l, 4)
            osrc = self.oT.rearrange("k (c p) t -> p k c t", p=128)
            xsrc = xcur.rearrange("(k p) t -> p k t", p=128)
            xdst = self.xT_a.rearrange("(k p) t -> p k t", p=128)
            hdst = self.h2T.rearrange("(k p) t -> p k t", p=128)
            tiles = list(enumerate(TILES)) if need_ctx else list(enumerate(TILES))[1:]
            bc = 0
            for ti, (t0, n) in tiles:
                s = 1 if ti == 0 else 0
                ot, otk = otr.next()
                for k in range(4):
                    dma("sp", ot[:, k * 4:(k + 1) * 4, :n], osrc[:, k, :, t0:t0 + n], writes=[otk])
                xt, xk = xr.next()
                dma("sp", xt[:, :, :n], xsrc[:, :, t0:t0 + n], writes=[xk])
                for j in range(8):
                    for k in range(4):
                        gt, gk = gtr.next()
                        r0 = k * 1024 + j * 128
                        dma("sp", gt[:, :n], self.gT[r0:r0 + 128, t0:t0 + n], writes=[gk])
                        bank = bc % 3
                        bc += 1
                        for c in range(4):
                            S.op("pe", lambda e, c=c, k=k, j=j: e.matmul(
                                PS[bank][:, :n], lhsT=wb[:, k * 4 + c, j * 128:(j + 1) * 128], rhs=ot[:, k * 4 + c, :n],
                                start=(c == 0), stop=(c == 3)), reads=["wb", otk], writes=[("ps", bank)])
                        if k == 0:
                            S.op("dve", lambda e, j=j, gt=gt: e.tensor_tensor(out=macc[:, j, :n], in0=PS[bank][:, :n],
                                                                              in1=gt[:, :n], op=ALU.mult),
                                 reads=[("ps", bank), gk], writes=[("macc", j)])
                        else:
                            mt, mk = mtmp.next()
                            S.op("dve", lambda e, gt=gt, mt=mt: e.tensor_tensor(out=mt[:, :n], in0=PS[bank][:, :n],
                                                                                in1=gt[:, :n], op=ALU.mult),
                                 reads=[("ps", bank), gk], writes=[mk])
                            S.op("pool", lambda e, j=j, mt=mt: e.tensor_tensor(out=macc[:, j, :n], in0=macc[:, j, :n],
                                                                               in1=mt[:, :n], op=ALU.add),
                                 reads=[mk, ("macc", j)], writes=[("macc", j)])
                    S.op("act", lambda e, j=j: e.copy(out=mb[:, j, :n], in_=macc[:, j, :n]),
                         reads=[("macc", j)], writes=[("mb", j)])
                for jo in range(8):
                    bank = 3 + (jo % 3)
                    for j in range(8):
                        S.op("pe", lambda e, j=j, jo=jo: e.matmul(PS[bank][:, :n], lhsT=wo[:, j, jo * 128:(jo + 1) * 128],
                                                                  rhs=mb[:, j, :n], start=(j == 0), stop=(j == 7)),
                             reads=["wo"] + [("mb", jj) for jj in range(8)], writes=[("ps", bank)])
                    S.op("dve", lambda e, jo=jo: e.scalar_tensor_tensor(
                        out=xt[:, jo, :n], in0=PS[bank][:, :n], scalar=self.modT[:, 16 + jo, s:s + 1],
                        in1=xt[:, jo, :n], op0=ALU.mult, op1=ALU.add), reads=[("ps", bank), xk, "modT"], writes=[xk])
                dma("sp", xdst[:, :, t0:t0 + n], xt[:, :, :n], reads=[xk])
                hs, hk = hst.next()
                self.adaln_tile(xt, xk, n, lambda k, s=s: self.A1[:, k, s:s + 1],
                                lambda k, s=s: self.modT[:, 24 + k, s:s + 1],
                                lambda k, hs=hs, n=n: hs[:, k, :n], hk, sq, rs, tmpr, 7)
                dma("sp", hdst[:, :, t0:t0 + n], hs[:, :, :n], reads=[hk])
            S.barrier()

    def peer_phase(self, l, need_ctx, final):
        S, PS, dma = self.S, self.PS, self.dma
        with ExitStack() as st:
            sb = lambda name, shape, dt: self.sb(st, name, shape, dt)
            wq = sb("wq", [128, 8, 1024], BF16)
            kbd = sb("kbd", [128, 8, 256], F32)
            h2r = self.ring(st, "h2r", [128, 8, PT], BF16, 2)
            qTh = sb("qTh", [128, 8, PT], F32)
            sc = sb("sc", [128, 8, 256], F32)
            v8 = sb("v8", [128, 8, 2, 16], F32)
            i8u = sb("i8u", [128, 8, 2, 16], U32)
            i8f = sb("i8f", [128, 8, 2, 16], F32)
            cand = sb("cand", [128, 8, 256], F32)
            cand2 = sb("cand2", [128, 8, 256], F32)
            wk1 = cand2
            best = sb("best", [128, 8, 16], F32)
            posu = sb("posu", [128, 8, 16], U32)
            posf = sb("posf", [128, 8, 16], F32)
            af = sb("af", [128, 8, 16], F32)
            bf = sb("bf", [128, 8, 16], F32)
            oh = sb("oh", [128, 128, 16], F32)
            iw = sb("iw", [128, 128], F32)
            jw = sb("jw", [128, 128], F32)
            gw = sb("gw", [128, 8, 16], F32)
            zs = sb("zs", [128, 8], F32)
            iT = sb("iT", [128, PT], F32)
            jT = sb("jT", [128, PT], F32)
            gTt = sb("gTt", [128, PT], F32)
            lhr = self.ring(st, "lhr", [128, 128], BF16, 4)
            rhr = self.ring(st, "rhr", [128, 128], BF16, 4)
            GT = sb("GT", [128, 128, PT], BF16)
            usr = self.ring(st, "usr", [128, 8, CG, 128], BF16, 2)
            vsr = self.ring(st, "vsr", [128, CG, 1024], BF16, 2)
            agr = self.ring(st, "agr", [128, PT], BF16, 3)
            mgr = self.ring(st, "mgr", [128, PT], BF16, 3)
            xt = sb("xtp", [128, 8, PT], F32)
            sq = sb("sq3", [128, 8, PT], F32)
            rs = sb("rs3", [128, PT], F32)
            tmpr = self.ring(st, "tmpr3", [128, PT], F32, 2)
            ost = sb("ost", [128, 8, PT], F32)
            dma("pool", wq[:], self.w_pq[l].rearrange("(k p) d -> p k d", p=128), writes=["wq"])
            dma("sp", kbd[:], self.keysbd[l].rearrange("h p n -> p h n"), writes=["kbd"])
            usrc = self.ubf[l].rearrange("(k p) (c i) -> p k c i", p=128, i=128)
            vsrc = self.vbf[l].rearrange("(c i) d -> i c d", i=128)
            hsrc = self.h2T.rearrange("(k p) t -> p k t", p=128)
            xsrc = self.xT_a.rearrange("(k p) t -> p k t", p=128)
            tiles = PTILES if need_ctx else PTILES[1:]
            iota16 = self.iota_f[:, 0:16]
            for (t0, n) in tiles:
                s = 1 if t0 < NCTX else 0
                h2, h2k = h2r.next()
                dma("sp", h2[:], hsrc[:, :, t0:t0 + n], writes=[h2k])
                dma("sp", xt[:], xsrc[:, :, t0:t0 + n], writes=["xtp"])
                for h in range(8):
                    bank = 6 + (h % 2)
                    for k in range(8):
                        S.op("pe", lambda e, h=h, k=k: e.matmul(PS[bank][:, :n], lhsT=wq[:, k, h * 128:(h + 1) * 128],
                                                                rhs=h2[:, k, :], start=(k == 0), stop=(k == 7)),
                             reads=["wq", h2k], writes=[("ps", bank)])
                    S.op("act", lambda e, h=h: e.copy(out=qTh[:, h, :], in_=PS[bank][:, :n]),
                         reads=[("ps", bank)], writes=[("qTh", h)])
                for sub in range(n // 128):
                    tsl = slice(sub * 128, (sub + 1) * 128)
                    for h in range(8):
                        bank = 4 + (h % 2)
                        S.op("pe", lambda e, h=h: e.matmul(PS[bank][:, :256], lhsT=qTh[:, h, tsl], rhs=kbd[:, h, :],
                                                           start=True, stop=True),
                             reads=[("qTh", h), "kbd"], writes=[("ps", bank)])
                        S.op("act", lambda e, h=h: e.copy(out=sc[:, h, :], in_=PS[bank][:, :256]),
                             reads=[("ps", bank)], writes=[("sc", h)])
                    for h in range(8):
                        for p in range(2):
                            src = sc[:, h, p * 128:(p + 1) * 128]
                            wks = wk1[:, h, p * 128:(p + 1) * 128]
                            S.op("dve", lambda e, h=h, p=p, src=src: e.max(out=v8[:, h, p, 0:8], in_=src),
                                 reads=[("sc", h)], writes=["v8"])
                            S.op("dve", lambda e, h=h, p=p, src=src: e.max_index(out=i8u[:, h, p, 0:8], in_max=v8[:, h, p, 0:8],
                                                                                  in_values=src),
                                 reads=[("sc", h), "v8"], writes=["i8u"])
                            S.op("dve", lambda e, h=h, p=p, src=src, wks=wks: e.match_replace(
                                out=wks, in_to_replace=v8[:, h, p, 0:8], in_values=src, imm_value=-1e30),
                                reads=[("sc", h), "v8"], writes=["cand2"])
                            S.op("dve", lambda e, h=h, p=p, wks=wks: e.max(out=v8[:, h, p, 8:16], in_=wks),
                                 reads=["cand2"], writes=["v8"])
                            S.op("dve", lambda e, h=h, p=p, wks=wks: e.max_index(out=i8u[:, h, p, 8:16],
                                                                                 in_max=v8[:, h, p, 8:16], in_values=wks),
                                 reads=["cand2", "v8"], writes=["i8u"])
                    S.op("dve", lambda e: e.tensor_copy(out=i8f[:], in_=i8u[:]), reads=["i8u"], writes=["i8f"])
                    S.op("dve", lambda e: e.tensor_tensor(
                        out=cand[:].rearrange("p h (a b) -> p h a b", b=16),
                        in0=v8[:, :, 0, :].unsqueeze(3).to_broadcast([128, 8, 16, 16]),
                        in1=v8[:, :, 1, :].unsqueeze(2).to_broadcast([128, 8, 16, 16]), op=ALU.add),
                        reads=["v8"], writes=["cand"])
                    for h in range(8):
                        S.op("dve", lambda e, h=h: e.max(out=best[:, h, 0:8], in_=cand[:, h, :]), reads=["cand"], writes=["best"])
                        S.op("dve", lambda e, h=h: e.max_index(out=posu[:, h, 0:8], in_max=best[:, h, 0:8], in_values=cand[:, h, :]),
                             reads=["cand", "best"], writes=["posu"])
                        S.op("dve", lambda e, h=h: e.match_replace(out=cand2[:, h, :], in_to_replace=best[:, h, 0:8],
                                                                   in_values=cand[:, h, :], imm_value=-1e30),
                             reads=["cand", "best"], writes=["cand2"])
                        S.op("dve", lambda e, h=h: e.max(out=best[:, h, 8:16], in_=cand2[:, h, :]), reads=["cand2"], writes=["best"])
                        S.op("dve", lambda e, h=h: e.max_index(out=posu[:, h, 8:16], in_max=best[:, h, 8:16], in_values=cand2[:, h, :]),
                             reads=["cand2", "best"], writes=["posu"])
                    S.op("dve", lambda e: e.tensor_single_scalar(out=posf[:].bitcast(U32), in_=posu[:], scalar=4,
                                                                 op=ALU.logical_shift_right), reads=["posu"], writes=["posf"])
                    S.op("dve", lambda e: e.tensor_copy(out=af[:], in_=posf[:].bitcast(U32)), reads=["posf"], writes=["af"])
                    S.op("dve", lambda e: e.tensor_single_scalar(out=posf[:].bitcast(U32), in_=posu[:], scalar=15,
                                                                 op=ALU.bitwise_and), reads=["posu", "af"], writes=["posf"])
                    S.op("dve", lambda e: e.tensor_copy(out=bf[:], in_=posf[:].bitcast(U32)), reads=["posf"], writes=["bf"])
                    for (src, p, dst, dk) in ((af, 0, iw, "iw"), (bf, 1, jw, "jw")):
                        S.op("dve", lambda e, src=src: e.tensor_tensor(
                            out=oh[:], in0=src[:].rearrange("p h k -> p (h k)").unsqueeze(2).to_broadcast([128, 128, 16]),
                            in1=iota16.unsqueeze(1).to_broadcast([128, 128, 16]), op=ALU.is_equal),
                            reads=["af", "bf", "iota_f"], writes=["oh"])
                        S.op("dve", lambda e, p=p: e.tensor_tensor(
                            out=oh[:].rearrange("p (h k) a -> p h k a", k=16),
                            in0=oh[:].rearrange("p (h k) a -> p h k a", k=16),
                            in1=i8f[:, :, p, :].unsqueeze(2).to_broadcast([128, 8, 16, 16]), op=ALU.mult),
                            reads=["oh", "i8f"], writes=["oh"])
                        S.op("dve", lambda e, dst=dst: e.tensor_reduce(out=dst[:], in_=oh[:], axis=AX.X, op=ALU.add),
                             reads=["oh"], writes=[dk])
                    S.op("dve", lambda e: e.tensor_tensor(out=gw[:], in0=best[:],
                                                          in1=best[:, :, 0:1].to_broadcast([128, 8, 16]), op=ALU.subtract),
                         reads=["best"], writes=["gw"])
                    S.op("act", lambda e: e.activation(out=gw[:], in_=gw[:], func=AF.Exp), reads=["gw"], writes=["gw"])
                    S.op("dve", lambda e: e.tensor_reduce(out=zs[:], in_=gw[:], axis=AX.X, op=ALU.add), reads=["gw"], writes=["zs"])
                    S.op("dve", lambda e: e.reciprocal(out=zs[:], in_=zs[:]), reads=["zs"], writes=["zs"])
                    S.op("dve", lambda e: e.tensor_tensor(out=gw[:], in0=gw[:], in1=zs[:].unsqueeze(2).to_broadcast([128, 8, 16]),
                                                          op=ALU.mult), reads=["gw", "zs"], writes=["gw"])
                    for (src_ap, sk, dstT, dk, bank) in ((iw[:], "iw", iT, "iT", 4), (jw[:], "jw", jT, "jT", 5),
                                                         (gw[:].rearrange("p h k -> p (h k)"), "gw", gTt, "gTt", 6)):
                        S.op("pe", lambda e, src_ap=src_ap, bank=bank: e.transpose(out=PS[bank][:, :128], in_=src_ap,
                                                                                   identity=self.ident[:]),
                             reads=[sk, "ident"], writes=[("ps", bank)])
                        S.op("act", lambda e, dstT=dstT, bank=bank: e.copy(out=dstT[:, tsl], in_=PS[bank][:, :128]),
                             reads=[("ps", bank)], writes=[dk])
                for tg in range(n // 4):
                    bank = 4 + (tg % 2)
                    for tt in range(4):
                        tok = tg * 4 + tt
                        lh, lk = lhr.next()
                        rh, rk = rhr.next()
                        S.op("dve", lambda e, lh=lh, tok=tok: e.tensor_scalar(
                            out=lh[:], in0=self.iota_f[:], scalar1=iT[:, tok:tok + 1], scalar2=gTt[:, tok:tok + 1],
                            op0=ALU.is_equal, op1=ALU.mult), reads=["iT", "gTt", "iota_f"], writes=[lk])
                        S.op("pool", lambda e, rh=rh, tok=tok: e.tensor_scalar(
                            out=rh[:], in0=self.iota_f[:], scalar1=jT[:, tok:tok + 1], scalar2=None,
                            op0=ALU.is_equal), reads=["jT", "iota_f"], writes=[rk])
                        S.op("pe", lambda e, lh=lh, rh=rh, tt=tt: e.matmul(PS[bank][:, tt * 128:(tt + 1) * 128], lhsT=lh[:],
                                                                           rhs=rh[:], start=True, stop=True),
                             reads=[lk, rk], writes=[("ps", bank)])
                    S.op("act", lambda e, tg=tg, bank=bank: e.copy(
                        out=GT[:, :, tg * 4:(tg + 1) * 4].rearrange("p j t -> p t j"),
                        in_=PS[bank][:, :].rearrange("p (t j) -> p t j", j=128)), reads=[("ps", bank)], writes=["GT"])
                for cg in range(128 // CG):
                    us, uk = usr.next()
                    vs, vk = vsr.next()
                    for k in range(8):
                        dma("sp", us[:, k], usrc[:, k, cg * CG:(cg + 1) * CG, :], writes=[uk])
                    dma("sp", vs[:], vsrc[:, cg * CG:(cg + 1) * CG, :], writes=[vk])
                    for cl in range(CG):
                        c = cg * CG + cl
                        bank = 6 + (c % 2)
                        for k in range(8):
                            S.op("pe", lambda e, k=k, cl=cl: e.matmul(PS[bank][:, :n], lhsT=us[:, k, cl, :], rhs=h2[:, k, :],
                                                                      start=(k == 0), stop=(k == 7)),
                                 reads=[uk, h2k], writes=[("ps", bank)])
                        ag, agk = agr.next()
                        S.op("act", lambda e, ag=ag, bank=bank: e.activation(out=ag[:], in_=PS[bank][:, :n],
                                                                             func=AF.Gelu_apprx_tanh),
                             reads=[("ps", bank)], writes=[agk])
                        mg, mgk = mgr.next()
                        S.op("dve", lambda e, ag=ag, mg=mg, c=c: e.tensor_tensor(out=mg[:], in0=ag[:], in1=GT[:, c, :],
                                                                                 op=ALU.mult), reads=[agk, "GT"], writes=[mgk])
                        for dc in range(8):
                            S.op("pe", lambda e, dc=dc, cl=cl, mg=mg, c=c: e.matmul(
                                PS[dc // 2][:, (dc % 2) * PT:(dc % 2 + 1) * PT], lhsT=vs[:, cl, dc * 128:(dc + 1) * 128],
                                rhs=mg[:], start=(c == 0), stop=(c == 127)), reads=[vk, mgk], writes=[("pso", dc)])
                for dc in range(8):
                    S.op("dve", lambda e, dc=dc: e.scalar_tensor_tensor(
                        out=xt[:, dc, :], in0=PS[dc // 2][:, (dc % 2) * PT:(dc % 2 + 1) * PT],
                        scalar=self.modT[:, 40 + dc, s:s + 1], in1=xt[:, dc, :], op0=ALU.mult, op1=ALU.add),
                        reads=[("pso", dc), "xtp", "modT"], writes=["xtp"])
                if not final:
                    dma("sp", xsrc[:, :, t0:t0 + n], xt[:], reads=["xtp"])
                else:
                    self.adaln_tile(xt, "xtp", n, lambda k: self.ngs[:, 4, k:k + 1], lambda k: None,
                                    lambda k: ost[:, k, :], "ost", sq, rs, tmpr, 6)
                    dma("sp", self.outT.rearrange("(k p) t -> p k t", p=128)[:, :, t0 - NCTX:t0 - NCTX + n], ost[:],
                        reads=["ost"])
            S.barrier()


def _lay(v):
    return np.ascontiguousarray(np.asarray(v, np.float32).reshape(-1, 128).T)


def _rope_tables():
    t = np.arange(NLAT)
    r = (t // 64).astype(np.float32)
    c = (t % 64).astype(np.float32)

    def tab(dim):
        nf = dim // 4
        inv = (np.float32(10000.0) ** (-np.arange(nf, dtype=np.float32) / np.float32(nf))).astype(np.float32)
        ang = np.concatenate([r[:, None] * inv[None, :], c[:, None] * inv[None, :]], axis=1).astype(np.float32)
        return np.cos(ang).astype(np.float32), np.sin(ang).astype(np.float32)

    c64, s64 = tab(64)
    c32, s32 = tab(32)
    rope64 = np.zeros((2, 128, NLAT), np.float32)
    for row in range(128):
        rope64[0, row] = c64[:, row % 32]
        rope64[1, row] = s64[:, row % 32]
    ropem = np.zeros((2, 96, NLAT), np.float32)
    ropem[0, :64] = 1.0
    for row in range(32):
        ropem[0, 64 + row] = c32[:, row % 16]
        ropem[1, 64 + row] = s32[:, row % 16]
    rm = np.zeros((3, 128, 128), np.float32)
    for B in (0, 64):
        for m in range(32):
            rm[0, B + m + 32, B + m] = -1.0
            rm[0, B + m, B + m + 32] = 1.0
    for m in range(16):
        rm[1, 64 + m + 16, 64 + m] = -1.0
        rm[1, 64 + m, 64 + m + 16] = 1.0
        rm[2, m + 16, m] = -1.0
        rm[2, m, m + 16] = 1.0
    return rope64, ropem, rm


def prep_inputs(inp):
    g = {k: np.asarray(v) for k, v in inp.items()}
    L = DEPTH
    shared = {}
    shared["w_mod"] = g["w_mod"]
    shared["b_mod_lay"] = np.ascontiguousarray(g["b_mod"].reshape(L, 48, 128).transpose(0, 2, 1))
    ngl = np.zeros((128, 5, 8), np.float32)
    ngl[:, 0] = _lay(g["norm1_g"][0]); ngl[:, 1] = _lay(g["norm1_g"][1])
    ngl[:, 2] = _lay(g["norm2_g"][0]); ngl[:, 3] = _lay(g["norm2_g"][1])
    ngl[:, 4] = _lay(g["final_norm_g"])
    shared["norm_g_lay"] = ngl
    shared["w_in"] = g["w_in"]
    cw = np.zeros((L, 128, 4, 5), np.float32)
    for l in range(L):
        for tap in range(4):
            cw[l, :, :, tap] = g["conv_w"][l, tap].reshape(4, 128).T
        cw[l, :, :, 4] = g["conv_b"][l].reshape(4, 128).T
    shared["convw_lay"] = cw
    bd = np.zeros((L, 2, 2, 4, 128, 128), np.float32)
    for l in range(L):
        for d in range(2):
            for kind, wname in enumerate(("lru_wa", "lru_wi")):
                w = g[wname][l, d]
                for cc in range(4):
                    bd[l, d, kind, cc, 0:64, 0:64] = w[2 * cc]
                    bd[l, d, kind, cc, 64:128, 64:128] = w[2 * cc + 1]
    shared["lru_bd"] = bd
    lv = np.zeros((L, 128, 2, 3, 4), np.float32)
    for l in range(L):
        for d in range(2):
            for kind, nm in enumerate(("lru_ba", "lru_bi", "lru_lambda")):
                lv[l, :, d, kind, :] = g[nm][l, d].reshape(4, 128).T
    shared["lru_vec"] = lv
    shared["diff_lam_rep"] = np.ascontiguousarray(np.broadcast_to(g["diff_lam"][:, None], (L, 128, 4, 64)))
    sgm = np.zeros((L, 128, 8), np.float32)
    for l in range(L):
        sgm[l, :, 0] = g["diff_subln_g"][l]
        sgm[l, :, 1] = np.tile(g["gqa_qnorm_g"][l], 2)
        sgm[l, :, 2] = np.tile(g["gqa_knorm_g"][l], 2)
        sgm[l, :, 3:6] = g["mla_qnorm_g"][l].reshape(3, 128).T
        sgm[l, :, 6:8] = g["mla_kvnorm_g"][l].reshape(2, 128).T
    shared["smallg"] = sgm
    shared["mla_w_uq"] = g["mla_w_uq"]
    shared["mla_w_ukv"] = g["mla_w_ukv"]
    shared["w_branch"] = g["w_branch"]
    shared["w_out"] = g["w_out"]
    shared["peer_wq"] = g["peer_wq"]
    kb = np.zeros((L, 8, 128, 256), np.float32)
    for p in range(2):
        kb[:, :, p * 64:(p + 1) * 64, p * 128:(p + 1) * 128] = g["peer_keys"][:, :, p].transpose(0, 1, 3, 2)
    shared["keysbd"] = kb
    shared["peer_uT"] = np.ascontiguousarray(
        g["peer_u"].reshape(L, 128, 128, D).transpose(0, 3, 2, 1).reshape(L, D, 16384))
    shared["peer_vP"] = np.ascontiguousarray(
        g["peer_v"].reshape(L, 128, 128, D).transpose(0, 2, 1, 3).reshape(L, 16384, D))
    rope64, ropem, rm = _rope_tables()
    shared["rope64"] = rope64
    shared["ropem"] = ropem
    shared["rmats"] = rm
    cst = np.zeros((3, 128, 128), np.float32)
    cst[0] = np.eye(128, dtype=np.float32)
    cst[1] = 1.0
    cst[2, :64, :64] = 1.0
    cst[2, 64:, 64:] = 1.0
    shared["consts"] = cst
    shared["iota_in"] = np.ascontiguousarray(np.broadcast_to(np.arange(128, dtype=np.float32)[None, :], (128, 128)))
    maps = []
    for b in range(8):
        m = dict(shared)
        xall = np.concatenate([g["ctx"][b], g["x"][b]], axis=0)
        m["xT_in"] = np.ascontiguousarray(xall.T)
        cvv = np.zeros((128, 8, 2), np.float32)
        cvv[:, :, 0] = _lay(g["c"][b])
        cvv[:, :, 1] = _lay(g["c_ctx"])
        m["cvec"] = cvv
        maps.append(m)
    return maps


def kernel(**inputs):
    maps = prep_inputs(inputs)
    nc = Prog().build()
    res = run_bass_kernel_spmd(nc, maps, core_ids=list(range(8)))
    out = np.stack([np.ascontiguousarray(res.results[b]["outT"].T) for b in range(8)], axis=0)
    return out.astype(np.float32)
```

```python
import math
import numpy as np
from contextlib import ExitStack
import concourse.bass as bass
import concourse.mybir as mybir
from concourse.bass_utils import run_bass_kernel_spmd

F32 = mybir.dt.float32
BF16 = mybir.dt.bfloat16
U32 = mybir.dt.uint32
AF = mybir.ActivationFunctionType
ALU = mybir.AluOpType
AX = mybir.AxisListType

D = 1024
NCTX = 256
NLAT = 4096
T = NCTX + NLAT
DEPTH = 2
EPS = 1e-6
IN_COLS = 8096
NKB = T // 128
TILES = [(0, 256)] + [(256 + 512 * i, 512) for i in range(8)]
PT = 256
PTILES = [(i * PT, PT) for i in range(T // PT)]
CG = 4


class Sched:
    def __init__(self, nc, es, n_dma=40):
        self.nc = nc
        self.E = {"pe": nc.tensor, "act": nc.scalar, "dve": nc.vector, "pool": nc.gpsimd, "sp": nc.sync}
        self.sem = {k: es.enter_context(nc.semaphore("s_" + k)) for k in self.E}
        self.cnt = {k: 0 for k in self.E}
        self.seen = {k: {} for k in self.E}
        self.nd = n_dma
        self.dsem = [es.enter_context(nc.semaphore("d%d" % i)) for i in range(n_dma)]
        self.dval = [0] * n_dma
        self.dnext = 0
        self.lastw = {}
        self.readers = {}
        self.same_wait = {"pe": False, "act": True, "dve": True, "pool": True, "sp": True}

    def _wait(self, ek, key, val):
        if self.seen[ek].get(key, 0) >= val:
            return
        kind, idx = key
        if kind == "e" and idx == ek and not self.same_wait[ek]:
            return
        sem = self.sem[idx] if kind == "e" else self.dsem[idx]
        self.E[ek].wait_ge(sem, val)
        self.seen[ek][key] = val

    def op(self, ek, fn, reads=(), writes=(), dma=False):
        deps = {}
        for r in reads:
            t = self.lastw.get(r)
            if t is not None and deps.get(t[0], 0) < t[1]:
                deps[t[0]] = t[1]
        for w in writes:
            t = self.lastw.get(w)
            if t is not None and deps.get(t[0], 0) < t[1]:
                deps[t[0]] = t[1]
            for k, v in self.readers.get(w, {}).items():
                if deps.get(k, 0) < v:
                    deps[k] = v
        for k, v in deps.items():
            self._wait(ek, k, v)
        if dma:
            i = self.dnext
            self.dnext = (i + 1) % self.nd
            if self.dval[i] > 0:
                self._wait(ek, ("d", i), self.dval[i])
            ins = fn(self.E[ek])
            self.dval[i] += 16
            ins.then_inc(self.dsem[i], 16)
            tok = (("d", i), self.dval[i])
        else:
            ins = fn(self.E[ek])
            self.cnt[ek] += 1
            ins.then_inc(self.sem[ek], 1)
            tok = (("e", ek), self.cnt[ek])
        for w in writes:
            self.lastw[w] = tok
            self.readers[w] = {}
        for r in reads:
            d = self.readers.setdefault(r, {})
            if d.get(tok[0], 0) < tok[1]:
                d[tok[0]] = tok[1]
        return tok

    def barrier(self):
        for ek in self.E:
            for k2 in self.E:
                if k2 != ek and self.cnt[k2] > 0:
                    self._wait(ek, ("e", k2), self.cnt[k2])
            for i in range(self.nd):
                if self.dval[i] > 0:
                    self._wait(ek, ("d", i), self.dval[i])
        self.lastw.clear()
        self.readers.clear()


class Ring:
    def __init__(self, tensors, name):
        self.t = tensors
        self.name = name
        self.i = 0

    def next(self):
        j = self.i % len(self.t)
        self.i += 1
        return self.t[j], (self.name, j)


class Prog:
    def __init__(self, dbg=None):
        self.dbg = dbg
        nc = self.nc = bass.Bass("TRN2", target_bir_lowering=False)
        self.uid = 0

        def din(name, shape, dt=F32):
            return nc.dram_tensor(name, list(shape), dt, kind="ExternalInput").ap()

        def dscr(name, shape, dt):
            kind = "ExternalOutput" if dbg else "Internal"
            return nc.dram_tensor(name, list(shape), dt, kind=kind).ap()

        self.xT_in = din("xT_in", [D, T])
        self.cvec = din("cvec", [128, 8, 2])
        self.w_mod = din("w_mod", [DEPTH, D, 6 * D])
        self.b_mod = din("b_mod_lay", [DEPTH, 128, 48])
        self.ng = din("norm_g_lay", [128, 5, 8])
        self.w_in = din("w_in", [DEPTH, D, IN_COLS])
        self.convw = din("convw_lay", [DEPTH, 128, 4, 5])
        self.lru_bd = din("lru_bd", [DEPTH, 2, 2, 4, 128, 128])
        self.lru_vec = din("lru_vec", [DEPTH, 128, 2, 3, 4])
        self.diff_lam = din("diff_lam_rep", [DEPTH, 128, 4, 64])
        self.smallg = din("smallg", [DEPTH, 128, 8])
        self.w_uq = din("mla_w_uq", [DEPTH, 384, 768])
        self.w_ukv = din("mla_w_ukv", [DEPTH, 256, 1024])
        self.w_br = din("w_branch", [DEPTH, 4, 512, 1024])
        self.w_out = din("w_out", [DEPTH, D, D])
        self.w_pq = din("peer_wq", [DEPTH, D, D])
        self.keysbd = din("keysbd", [DEPTH, 8, 128, 256])
        self.UT = din("peer_uT", [DEPTH, 4096, 4096])
        self.VP = din("peer_vP", [DEPTH, 16384, D])
        self.rope64 = din("rope64", [2, 128, NLAT])
        self.ropem = din("ropem", [2, 96, NLAT])
        self.rmats = din("rmats", [3, 128, 128])
        self.consts = din("consts", [3, 128, 128])
        self.iota_in = din("iota_in", [128, 128])
        self.outT = nc.dram_tensor("outT", [D, NLAT], F32, kind="ExternalOutput").ap()

        self.xT_a = dscr("xT_a", [D, T], F32)
        self.xaT = dscr("xaT", [512, T], F32)
        self.yaT = dscr("yaT", [512, T], BF16)
        self.qbT = dscr("qbT", [512, T], BF16)
        self.kbT = dscr("kbT", [512, T], BF16)
        self.vb = dscr("vb", [T, 512], BF16)
        self.qgT = dscr("qgT", [512, T], BF16)
        self.kgT = dscr("kgT", [128, T], BF16)
        self.vg = dscr("vg", [T, 128], BF16)
        self.qmT = dscr("qmT", [8, 96, T], BF16)
        self.kmT = dscr("kmT", [8, 96, T], BF16)
        self.vm = dscr("vm", [T, 512], BF16)
        self.gT = dscr("gT", [4096, T], BF16)
        self.oT = dscr("oT", [4, 512, T], BF16)
        self.h2T = dscr("h2T", [D, T], BF16)
        self.ubf = [nc.dram_tensor("ubf%d" % l, [4096, 4096], BF16, kind="Internal").ap() for l in range(DEPTH)]
        self.vbf = [nc.dram_tensor("vbf%d" % l, [16384, D], BF16, kind="Internal").ap() for l in range(DEPTH)]

    def sb(self, st, name, shape, dt):
        self.uid += 1
        return st.enter_context(self.nc.sbuf_tensor("%s_%d" % (name, self.uid), list(shape), dt))

    def ring(self, st, name, shape, dt, n):
        self.uid += 1
        return Ring([self.sb(st, "%s%d" % (name, i), shape, dt) for i in range(n)], "%s_%d" % (name, self.uid))

    def dma(self, q, out, in_, reads=(), writes=()):
        return self.S.op(q, lambda e: e.dma_start(out=out, in_=in_), reads=reads, writes=writes, dma=True)

    def build(self):
        nc = self.nc
        es = ExitStack()
        with es:
            S = self.S = Sched(nc, es)
            self.PS = [es.enter_context(nc.psum_tensor("ps%d" % i, [128, 512], F32)) for i in range(8)]
            sb, dma = self.sb, self.dma
            self.ident = sb(es, "ident", [128, 128], F32)
            self.ones_f = sb(es, "ones_f", [128, 128], F32)
            self.bd64_f = sb(es, "bd64_f", [128, 128], F32)
            self.ones_b = sb(es, "ones_b", [128, 128], BF16)
            self.r64 = sb(es, "r64", [128, 128], BF16)
            self.r96 = sb(es, "r96", [128, 128], BF16)
            self.r32 = sb(es, "r32", [128, 128], BF16)
            self.epsc = sb(es, "epsc", [128, 1], F32)
            self.iota_f = sb(es, "iota_f", [128, 128], F32)
            self.cv = sb(es, "cv", [128, 8, 2], F32)
            self.ngs = sb(es, "ngs", [128, 5, 8], F32)
            self.modT = sb(es, "modT", [128, 48, 2], F32)
            self.A1 = sb(es, "A1", [128, 8, 2], F32)
            dma("sp", self.ident[:], self.consts[0], writes=["ident"])
            dma("sp", self.ones_f[:], self.consts[1], writes=["ones_f"])
            dma("sp", self.bd64_f[:], self.consts[2], writes=["bd64_f"])
            dma("pool", self.ones_b[:], self.consts[1], writes=["ones_b"])
            dma("pool", self.r64[:], self.rmats[0], writes=["r64"])
            dma("pool", self.r96[:], self.rmats[1], writes=["r96"])
            dma("pool", self.r32[:], self.rmats[2], writes=["r32"])
            dma("sp", self.iota_f[:], self.iota_in, writes=["iota_f"])
            dma("sp", self.cv[:], self.cvec, writes=["cv"])
            dma("sp", self.ngs[:], self.ng, writes=["ngs"])
            S.op("dve", lambda e: e.memset(self.epsc[:], EPS), writes=["epsc"])
            S.op("act", lambda e: e.activation(out=self.cv[:], in_=self.cv[:], func=AF.Silu), reads=["cv"], writes=["cv"])
            if self.dbg in (None, "peer"):
                for l in range(DEPTH):
                    for i in range(8):
                        dma("pool", self.ubf[l][i * 512:(i + 1) * 512, :], self.UT[l, i * 512:(i + 1) * 512, :])
                    for i in range(16):
                        dma("pool", self.vbf[l][i * 1024:(i + 1) * 1024, :], self.VP[l, i * 1024:(i + 1) * 1024, :])
            xcur = self.xT_in
            stop = False
            for l in range(DEPTH):
                need_ctx = l < DEPTH - 1
                lam_init = 0.8 - 0.6 * math.exp(-0.3 * l)
                S.barrier()
                self.mod_phase(l)
                if self.dbg == "mod":
                    dm = nc.dram_tensor("dbg_mod", [128, 96], F32, kind="ExternalOutput").ap()
                    dma("sp", dm, self.modT[:].rearrange("p j s -> p (j s)"), reads=["modT"])
                    break
                S.barrier()
                self.inproj_phase(l, xcur)
                if self.dbg == "inproj":
                    break
                S.barrier()
                self.lru_phase(l)
                if self.dbg == "lru":
                    break
                S.barrier()
                self.attn_phase(l, need_ctx, lam_init)
                if self.dbg == "attn":
                    break
                S.barrier()
                self.merge_phase(l, need_ctx, xcur)
                xcur = self.xT_a
                if self.dbg == "merge":
                    break
                S.barrier()
                self.peer_phase(l, need_ctx, final=(l == DEPTH - 1))
                if self.dbg == "peer":
                    break
            S.barrier()
        return nc

    def mod_phase(self, l):
        S, PS, dma = self.S, self.PS, self.dma
        with ExitStack() as st:
            wm = self.ring(st, "wm", [128, 8, 512], F32, 2)
            bm = self.sb(st, "bm", [128, 48], F32)
            dma("sp", bm[:], self.b_mod[l], writes=["bm"])
            wsrc = self.w_mod[l].rearrange("(k p) c -> p k c", p=128)
            for g in range(12):
                wt, wk = wm.next()
                dma("sp", wt[:], wsrc[:, :, g * 512:(g + 1) * 512], writes=[wk])
                for jj in range(4):
                    j = g * 4 + jj
                    for k in range(8):
                        S.op("pe", lambda e, wt=wt, jj=jj, k=k, j=j: e.matmul(
                            PS[0][:, 2 * j:2 * j + 2], lhsT=wt[:, k, jj * 128:(jj + 1) * 128], rhs=self.cv[:, k, :],
                            start=(k == 0), stop=(k == 7)), reads=[wk, "cv"], writes=[("ps", 0)])
            for s in range(2):
                S.op("dve", lambda e, s=s: e.tensor_tensor(
                    out=self.modT[:, :, s], in0=PS[0][:, 0:96].rearrange("p (j s) -> p j s", s=2)[:, :, s],
                    in1=bm[:], op=ALU.add), reads=[("ps", 0), "bm"], writes=["modT"])
            S.barrier()

    def make_A(self, gidx, scale_idx):
        for s in range(2):
            self.S.op("dve", lambda e, s=s: e.scalar_tensor_tensor(
                out=self.A1[:, :, s], in0=self.modT[:, scale_idx * 8:(scale_idx + 1) * 8, s], scalar=1.0,
                in1=self.ngs[:, gidx, :], op0=ALU.add, op1=ALU.mult), reads=["modT", "ngs"], writes=["A1"])

    def adaln_tile(self, xt, xkey, n, acol_fn, bias_fn, hout_fn, hkey, sq, rs, tmpr, psb, sqkey="sq"):
        S, PS = self.S, self.PS
        for k in range(8):
            S.op("act", lambda e, k=k: e.activation(out=sq[:, k, :n], in_=xt[:, k, :n], func=AF.Square),
                 reads=[xkey], writes=[sqkey])
        for k in range(8):
            S.op("pe", lambda e, k=k: e.matmul(PS[psb][:, :n], lhsT=self.ones_f[:], rhs=sq[:, k, :n],
                                               start=(k == 0), stop=(k == 7)),
                 reads=[sqkey, "ones_f"], writes=[("ps", psb)])
        S.op("act", lambda e: e.activation(out=rs[:, :n], in_=PS[psb][:, :n], func=AF.Sqrt,
                                           bias=self.epsc[:], scale=1.0 / D),
             reads=[("ps", psb), "epsc"], writes=["rs"])
        S.op("dve", lambda e: e.reciprocal(out=rs[:, :n], in_=rs[:, :n]), reads=["rs"], writes=["rs"])
        for k in range(8):
            tt, tk = tmpr.next()
            S.op("dve", lambda e, k=k, tt=tt: e.scalar_tensor_tensor(
                out=tt[:, :n], in0=xt[:, k, :n], scalar=acol_fn(k), in1=rs[:, :n],
                op0=ALU.mult, op1=ALU.mult), reads=[xkey, "A1", "rs"], writes=[tk])
            b = bias_fn(k)
            if b is None:
                S.op("act", lambda e, k=k, tt=tt: e.copy(out=hout_fn(k), in_=tt[:, :n]), reads=[tk], writes=[hkey])
            else:
                S.op("act", lambda e, k=k, tt=tt, b=b: e.activation(
                    out=hout_fn(k), in_=tt[:, :n], func=AF.Identity, bias=b, scale=1.0),
                    reads=[tk, "modT"], writes=[hkey])

    def rope(self, P, n, xb, xkey, Rm, Rkey, Ct, St, tabkey, out_ap, outkey, bank, stg_f):
        S, PS = self.S, self.PS
        S.op("pe", lambda e: e.matmul(PS[bank][:P, :n], lhsT=Rm[:P, :P], rhs=xb[:P, :n], start=True, stop=True),
             reads=[xkey, Rkey], writes=[("ps", bank)])
        t1, t1k = stg_f.next()
        S.op("dve", lambda e: e.tensor_tensor(out=t1[:P, :n], in0=xb[:P, :n], in1=Ct[:P, :n], op=ALU.mult),
             reads=[xkey, tabkey], writes=[t1k])
        t2, t2k = stg_f.next()
        S.op("dve", lambda e: e.tensor_tensor(out=t2[:P, :n], in0=PS[bank][:P, :n], in1=St[:P, :n], op=ALU.mult),
             reads=[("ps", bank), tabkey], writes=[t2k])
        S.op("pool", lambda e: e.tensor_tensor(out=out_ap, in0=t1[:P, :n], in1=t2[:P, :n], op=ALU.add),
             reads=[t1k, t2k], writes=[outkey])

    def inproj_phase(self, l, xcur):
        S, PS, dma, nc = self.S, self.PS, self.dma, self.nc
        with ExitStack() as st:
            hT = self.sb(st, "hT", [128, 8, T], BF16)
            self.make_A(l, 1)
            with ExitStack() as st1:
                xr = self.ring(st1, "xr", [128, 8, 512], F32, 2)
                sq = self.sb(st1, "sq", [128, 8, 512], F32)
                rs = self.sb(st1, "rs", [128, 512], F32)
                tmpr = self.ring(st1, "tmpr", [128, 512], F32, 2)
                xsrc = xcur.rearrange("(k p) t -> p k t", p=128)
                for ti, (t0, n) in enumerate(TILES):
                    xt, xk = xr.next()
                    dma("sp", xt[:, :, :n], xsrc[:, :, t0:t0 + n], writes=[xk])
                    s = 1 if ti == 0 else 0
                    self.adaln_tile(xt, xk, n, lambda k, s=s: self.A1[:, k, s:s + 1],
                                    lambda k, s=s: self.modT[:, k, s:s + 1],
                                    lambda k, t0=t0, n=n: hT[:, k, t0:t0 + n], ("hT", ti), sq, rs, tmpr, 7)
                S.barrier()
            if self.dbg == "adaln":
                dh = nc.dram_tensor("dbg_h", [D, T], BF16, kind="ExternalOutput").ap()
                dma("sp", dh.rearrange("(k p) t -> p k t", p=128), hT[:])
                return
            wr = self.ring(st, "wsl", [128, 8, 512], BF16, 2)
            wmla = self.sb(st, "wmla", [128, 8, 672], BF16)
            wuq = self.sb(st, "wuq", [128, 3, 768], BF16)
            wkk = self.sb(st, "wkk", [128, 2, 8, 64], BF16)
            wkv = self.sb(st, "wkv", [128, 2, 8, 64], BF16)
            sg = self.sb(st, "sg", [128, 8], F32)
            stg_f = self.ring(st, "stgf", [128, 512], F32, 4)
            stg_b = self.ring(st, "stgb", [128, 512], BF16, 4)
            xbr = self.ring(st, "xbr", [128, 512], BF16, 3)
            tabC = self.ring(st, "tabC", [128, 512], F32, 2)
            tabS = self.ring(st, "tabS", [128, 512], F32, 2)
            sqn = self.sb(st, "sqn", [128, 512], F32)
            rsn = self.sb(st, "rsn", [128, 512], F32)
            zc = self.sb(st, "zc", [128, 5, 512], F32)
            sqm = self.sb(st, "sqm", [128, 5, 512], F32)
            cqn = self.sb(st, "cqn", [128, 3, 512], BF16)
            ckvn = self.sb(st, "ckvn", [128, 2, 512], BF16)
            wsrc = self.w_in[l].rearrange("(k p) c -> p k c", p=128)
            dma("sp", sg[:], self.smallg[l], writes=["sg"])
            bankc = [0]

            def nbank():
                b = bankc[0] % 3
                bankc[0] += 1
                return b

            def mm_chunk(bank, wt, wk, c0, m, ti):
                t0, n = TILES[ti]
                for k in range(8):
                    S.op("pe", lambda e, k=k: e.matmul(PS[bank][:m, :n], lhsT=wt[:, k, c0:c0 + m],
                                                       rhs=hT[:, k, t0:t0 + n], start=(k == 0), stop=(k == 7)),
                         reads=[wk], writes=[("ps", bank)])

            def load_tabs(src, P, ti, prow0=0):
                t0, n = TILES[ti]
                Ct, ck = tabC.next()
                St, sk = tabS.next()
                dma("sp", Ct[:P, :n], src[0, prow0:prow0 + P, t0 - NCTX:t0 - NCTX + n], writes=[ck])
                dma("sp", St[:P, :n], src[1, prow0:prow0 + P, t0 - NCTX:t0 - NCTX + n], writes=[sk])
                return Ct, St, ck, sk

            def store(dst_ap, src_ap, key):
                dma("sp", dst_ap, src_ap, reads=[key])

            def evac_act(bank, m, n, func, dt_ring, **kw):
                o, ok = dt_ring.next()
                S.op("act", lambda e: e.activation(out=o[:m, :n], in_=PS[bank][:m, :n], func=func, **kw),
                     reads=[("ps", bank)], writes=[ok])
                return o, ok

            def tokmajor(wt, wk, c0, w, dst, ti):
                t0, n = TILES[ti]
                for sub in range(n // 128):
                    bank = nbank()
                    for k in range(8):
                        S.op("pe", lambda e, k=k, sub=sub: e.matmul(
                            PS[bank][:, :w], lhsT=hT[:, k, t0 + sub * 128:t0 + (sub + 1) * 128],
                            rhs=wt[:, k, c0:c0 + w], start=(k == 0), stop=(k == 7)),
                            reads=[wk], writes=[("ps", bank)])
                    o, ok = evac_act(bank, 128, w, AF.Copy, stg_b)
                    store(dst[t0 + sub * 128:t0 + (sub + 1) * 128, :], o[:, :w], ok)

            def rope_store(P, n, ti, xb, xk, Rm, Rkey, tabs, dst_ap, bank=3):
                if ti == 0:
                    store(dst_ap, xb[:P, :n], xk)
                else:
                    Ct, St, ck, sk = tabs
                    o, ok = stg_b.next()
                    S.op("pe", lambda e: e.matmul(PS[bank][:P, :n], lhsT=Rm[:P, :P], rhs=xb[:P, :n],
                                                  start=True, stop=True),
                         reads=[xk, Rkey], writes=[("ps", bank)])
                    t1, t1k = stg_f.next()
                    S.op("dve", lambda e: e.tensor_tensor(out=t1[:P, :n], in0=xb[:P, :n], in1=Ct[:P, :n],
                                                          op=ALU.mult), reads=[xk, ck], writes=[t1k])
                    t2, t2k = stg_f.next()
                    S.op("dve", lambda e: e.tensor_tensor(out=t2[:P, :n], in0=PS[bank][:P, :n], in1=St[:P, :n],
                                                          op=ALU.mult), reads=[("ps", bank), sk], writes=[t2k])
                    S.op("pool", lambda e: e.tensor_tensor(out=o[:P, :n], in0=t1[:P, :n], in1=t2[:P, :n],
                                                           op=ALU.add), reads=[t1k, t2k], writes=[ok])
                    store(dst_ap, o[:P, :n], ok)

            def headnorm(bank, n, gcol, inv_dim, ones_mat, ones_key):
                S.op("act", lambda e: e.activation(out=sqn[:, :n], in_=PS[bank][:, :n], func=AF.Square),
                     reads=[("ps", bank)], writes=["sqn"])
                S.op("pe", lambda e: e.matmul(PS[4][:, :n], lhsT=ones_mat[:], rhs=sqn[:, :n], start=True, stop=True),
                     reads=["sqn", ones_key], writes=[("ps", 4)])
                S.op("act", lambda e: e.activation(out=rsn[:, :n], in_=PS[4][:, :n], func=AF.Sqrt,
                                                   bias=self.epsc[:], scale=inv_dim),
                     reads=[("ps", 4), "epsc"], writes=["rsn"])
                S.op("dve", lambda e: e.reciprocal(out=rsn[:, :n], in_=rsn[:, :n]), reads=["rsn"], writes=["rsn"])
                xb, xk = xbr.next()
                S.op("dve", lambda e: e.scalar_tensor_tensor(out=xb[:, :n], in0=PS[bank][:, :n], scalar=gcol,
                                                             in1=rsn[:, :n], op0=ALU.mult, op1=ALU.mult),
                     reads=[("ps", bank), "rsn", "sg"], writes=[xk])
                return xb, xk

            groups = [("xa", 0, 512), ("ya", 512, 512), ("qb", 1024, 512), ("kb", 1536, 512), ("vb", 2048, 512),
                      ("qg", 2560, 512), ("kvg", 3072, 256)] + [("zg%d" % i, 4000 + 512 * i, 512) for i in range(8)]
            for gname, c0, w in groups:
                wt, wk = wr.next()
                dma("pool", wt[:, :, :w], wsrc[:, :, c0:c0 + w], writes=[wk])
                for ti, (t0, n) in enumerate(TILES):
                    if gname == "vb":
                        tokmajor(wt, wk, 0, 512, self.vb, ti)
                        continue
                    if gname == "kvg":
                        tokmajor(wt, wk, 128, 128, self.vg, ti)
                    tabs = None
                    if gname in ("qb", "kb", "qg", "kvg") and ti > 0:
                        tabs = load_tabs(self.rope64, 128, ti)
                    nch = 1 if gname == "kvg" else w // 128
                    for ch in range(nch):
                        bank = nbank()
                        mm_chunk(bank, wt, wk, ch * 128, 128, ti)
                        if gname == "xa":
                            o, ok = evac_act(bank, 128, n, AF.Copy, stg_f)
                            store(self.xaT[ch * 128:(ch + 1) * 128, t0:t0 + n], o[:, :n], ok)
                        elif gname == "ya":
                            o, ok = evac_act(bank, 128, n, AF.Gelu_apprx_tanh, stg_b)
                            store(self.yaT[ch * 128:(ch + 1) * 128, t0:t0 + n], o[:, :n], ok)
                        elif gname.startswith("zg"):
                            o, ok = evac_act(bank, 128, n, AF.Sigmoid, stg_b)
                            r0 = (c0 - 4000) + ch * 128
                            store(self.gT[r0:r0 + 128, t0:t0 + n], o[:, :n], ok)
                        elif gname in ("qb", "kb"):
                            xb, xk = evac_act(bank, 128, n, AF.Copy, xbr)
                            dst = self.qbT if gname == "qb" else self.kbT
                            rope_store(128, n, ti, xb, xk, self.r64, "r64", tabs,
                                       dst[ch * 128:(ch + 1) * 128, t0:t0 + n])
                        elif gname == "qg":
                            xb, xk = headnorm(bank, n, sg[:, 1:2], 1.0 / 64, self.bd64_f, "bd64_f")
                            rope_store(128, n, ti, xb, xk, self.r64, "r64", tabs,
                                       self.qgT[ch * 128:(ch + 1) * 128, t0:t0 + n])
                        elif gname == "kvg":
                            xb, xk = headnorm(bank, n, sg[:, 2:3], 1.0 / 64, self.bd64_f, "bd64_f")
                            rope_store(128, n, ti, xb, xk, self.r64, "r64", tabs, self.kgT[:, t0:t0 + n])
            dma("pool", wmla[:], wsrc[:, :, 3328:4000], writes=["wmla"])
            dma("pool", wuq[:], self.w_uq[l].rearrange("(j p) c -> p j c", p=128), writes=["wuq"])
            ukv5 = self.w_ukv[l].rearrange("(j p) (h t d) -> p j h t d", p=128, h=8, t=2)
            for j in range(2):
                dma("pool", wkk[:, j], ukv5[:, j, :, 0, :], writes=["wkk"])
                dma("pool", wkv[:, j], ukv5[:, j, :, 1, :], writes=["wkv"])
            for ti, (t0, n) in enumerate(TILES):
                tq = tk_ = None
                if ti > 0:
                    tq = load_tabs(self.ropem, 96, ti)
                    tk_ = load_tabs(self.ropem, 32, ti, prow0=64)
                for j in range(5):
                    bank = nbank()
                    mm_chunk(bank, wmla, "wmla", j * 128, 128, ti)
                    S.op("act", lambda e, j=j: e.copy(out=zc[:, j, :n], in_=PS[bank][:, :n]),
                         reads=[("ps", bank)], writes=["zc"])
                    S.op("act", lambda e, j=j: e.activation(out=sqm[:, j, :n], in_=PS[bank][:, :n], func=AF.Square),
                         reads=[("ps", bank)], writes=["sqm"])
                for (j0, nj, inv, gc0, dstn, dkey) in ((0, 3, 1.0 / 384, 3, cqn, "cqn"), (3, 2, 1.0 / 256, 6, ckvn, "ckvn")):
                    for jj in range(nj):
                        S.op("pe", lambda e, jj=jj: e.matmul(PS[4][:, :n], lhsT=self.ones_f[:], rhs=sqm[:, j0 + jj, :n],
                                                             start=(jj == 0), stop=(jj == nj - 1)),
                             reads=["sqm", "ones_f"], writes=[("ps", 4)])
                    S.op("act", lambda e: e.activation(out=rsn[:, :n], in_=PS[4][:, :n], func=AF.Sqrt,
                                                       bias=self.epsc[:], scale=inv),
                         reads=[("ps", 4), "epsc"], writes=["rsn"])
                    S.op("dve", lambda e: e.reciprocal(out=rsn[:, :n], in_=rsn[:, :n]), reads=["rsn"], writes=["rsn"])
                    for jj in range(nj):
                        S.op("dve", lambda e, jj=jj: e.scalar_tensor_tensor(
                            out=dstn[:, jj, :n], in0=zc[:, j0 + jj, :n], scalar=sg[:, gc0 + jj:gc0 + jj + 1],
                            in1=rsn[:, :n], op0=ALU.mult, op1=ALU.mult), reads=["zc", "rsn", "sg"], writes=[dkey])
                for h in range(8):
                    bank = nbank()
                    for j in range(3):
                        S.op("pe", lambda e, j=j, h=h: e.matmul(PS[bank][:96, :n], lhsT=wuq[:, j, h * 96:(h + 1) * 96],
                                                                rhs=cqn[:, j, :n], start=(j == 0), stop=(j == 2)),
                             reads=["wuq", "cqn"], writes=[("ps", bank)])
                    xb, xk = evac_act(bank, 96, n, AF.Copy, xbr)
                    rope_store(96, n, ti, xb, xk, self.r96, "r96", tq, self.qmT[h, :, t0:t0 + n])
                for h in range(8):
                    bank = nbank()
                    for j in range(2):
                        S.op("pe", lambda e, j=j, h=h: e.matmul(PS[bank][:64, :n], lhsT=wkk[:, j, h, :],
                                                                rhs=ckvn[:, j, :n], start=(j == 0), stop=(j == 1)),
                             reads=["wkk", "ckvn"], writes=[("ps", bank)])
                    o, ok = evac_act(bank, 64, n, AF.Copy, stg_b)
                    store(self.kmT[h, 0:64, t0:t0 + n], o[:64, :n], ok)
                for sub in range(n // 128):
                    bank = nbank()
                    for j in range(2):
                        S.op("pe", lambda e, j=j, sub=sub: e.matmul(
                            PS[bank][:, :512], lhsT=ckvn[:, j, sub * 128:(sub + 1) * 128],
                            rhs=wkv[:, j].rearrange("p h d -> p (h d)"), start=(j == 0), stop=(j == 1)),
                            reads=["wkv", "ckvn"], writes=[("ps", bank)])
                    o, ok = evac_act(bank, 128, 512, AF.Copy, stg_b)
                    store(self.vm[t0 + sub * 128:t0 + (sub + 1) * 128, :], o[:, :], ok)
                bank = nbank()
                mm_chunk(bank, wmla, "wmla", 640, 32, ti)
                xb, xk = evac_act(bank, 32, n, AF.Copy, xbr)
                if ti == 0:
                    for h in range(8):
                        store(self.kmT[h, 64:96, t0:t0 + n], xb[:32, :n], xk)
                else:
                    Ct, St, ck, sk = tk_
                    o, ok = stg_b.next()
                    S.op("pe", lambda e: e.matmul(PS[3][:32, :n], lhsT=self.r32[:32, :32], rhs=xb[:32, :n],
                                                  start=True, stop=True), reads=[xk, "r32"], writes=[("ps", 3)])
                    t1, t1k = stg_f.next()
                    S.op("dve", lambda e: e.tensor_tensor(out=t1[:32, :n], in0=xb[:32, :n], in1=Ct[:32, :n],
                                                          op=ALU.mult), reads=[xk, ck], writes=[t1k])
                    t2, t2k = stg_f.next()
                    S.op("dve", lambda e: e.tensor_tensor(out=t2[:32, :n], in0=PS[3][:32, :n], in1=St[:32, :n],
                                                          op=ALU.mult), reads=[("ps", 3), sk], writes=[t2k])
                    S.op("pool", lambda e: e.tensor_tensor(out=o[:32, :n], in0=t1[:32, :n], in1=t2[:32, :n],
                                                           op=ALU.add), reads=[t1k, t2k], writes=[ok])
                    for h in range(8):
                        store(self.kmT[h, 64:96, t0:t0 + n], o[:32, :n], ok)
            S.barrier()

    def lru_phase(self, l):
        S, PS, dma = self.S, self.PS, self.dma
        with ExitStack() as st:
            xa = self.sb(st, "xa", [128, T], F32)
            u = self.sb(st, "u", [128, T], F32)
            ub = self.sb(st, "ub", [128, T], BF16)
            gy = self.sb(st, "gy", [128, T], BF16)
            at = self.sb(st, "at", [128, T], F32)
            bt = self.sb(st, "bt", [128, T], F32)
            tm = self.sb(st, "tm", [128, T], F32)
            hf = self.sb(st, "hf", [128, T], F32)
            hr = self.sb(st, "hr", [128, T], F32)
            ho = self.sb(st, "ho", [128, T], BF16)
            bdw = self.ring(st, "bdw", [128, 128], BF16, 4)
            cw = self.sb(st, "cw", [128, 4, 5], F32)
            lv = self.sb(st, "lv", [128, 2, 3, 4], F32)
            negk = self.sb(st, "negk", [128, 2, 4], F32)
            dma("sp", cw[:], self.convw[l], writes=["cw"])
            dma("sp", lv[:], self.lru_vec[l], writes=["lv"])
            for d in range(2):
                S.op("act", lambda e, d=d: e.activation(out=negk[:, d, :], in_=lv[:, d, 2, :], func=AF.Exp, scale=-1.0),
                     reads=["lv"], writes=["negk"])
                S.op("act", lambda e, d=d: e.activation(out=negk[:, d, :], in_=negk[:, d, :], func=AF.Ln, bias=1.0),
                     reads=["negk"], writes=["negk"])
                S.op("dve", lambda e, d=d: e.tensor_scalar(out=negk[:, d, :], in0=negk[:, d, :], scalar1=-8.0,
                                                           scalar2=None, op0=ALU.mult), reads=["negk"], writes=["negk"])
            segs = [(0, NCTX), (NCTX, T)]
            for cc in range(4):
                dma("sp", xa[:], self.xaT[cc * 128:(cc + 1) * 128, :], writes=["xa"])
                dma("sp", gy[:], self.yaT[cc * 128:(cc + 1) * 128, :], writes=["gy"])
                S.op("act", lambda e: e.activation(out=u[:], in_=xa[:], func=AF.Identity, scale=cw[:, cc, 1:2],
                                                   bias=cw[:, cc, 4:5]), reads=["xa", "cw"], writes=["u"])
                for (s0, e0) in segs:
                    for (tap, osl, isl) in ((0, (s0 + 1, e0), (s0, e0 - 1)), (2, (s0, e0 - 1), (s0 + 1, e0)),
                                            (3, (s0, e0 - 2), (s0 + 2, e0))):
                        S.op("dve", lambda e, tap=tap, osl=osl, isl=isl: e.scalar_tensor_tensor(
                            out=u[:, osl[0]:osl[1]], in0=xa[:, isl[0]:isl[1]], scalar=cw[:, cc, tap:tap + 1],
                            in1=u[:, osl[0]:osl[1]], op0=ALU.mult, op1=ALU.add), reads=["xa", "u", "cw"], writes=["u"])
                S.op("pool", lambda e: e.tensor_copy(out=ub[:], in_=u[:]), reads=["u"], writes=["ub"])
                for d in range(2):
                    wts = []
                    for kind in range(2):
                        wt, wk = bdw.next()
                        dma("pool", wt[:], self.lru_bd[l, d, kind, cc], writes=[wk])
                        wts.append((wt, wk))
                    for ti, (t0, n) in enumerate(TILES):
                        for kind in range(2):
                            bank = (ti * 2 + kind) % 4
                            wt, wk = wts[kind]
                            S.op("pe", lambda e, wt=wt: e.matmul(PS[bank][:, :n], lhsT=wt[:], rhs=ub[:, t0:t0 + n],
                                                                 start=True, stop=True),
                                 reads=[wk, "ub"], writes=[("ps", bank)])
                            dst, dk = (at, "at") if kind == 0 else (bt, "bt")
                            S.op("act", lambda e, dst=dst, kind=kind: e.activation(
                                out=dst[:, t0:t0 + n], in_=PS[bank][:, :n], func=AF.Sigmoid,
                                bias=lv[:, d, kind, cc:cc + 1], scale=1.0), reads=[("ps", bank), "lv"], writes=[dk])
                    S.op("act", lambda e: e.activation(out=at[:], in_=at[:], func=AF.Exp, scale=negk[:, d, cc:cc + 1]),
                         reads=["at", "negk"], writes=["at"])
                    S.op("pool", lambda e: e.tensor_tensor(out=tm[:], in0=at[:], in1=at[:], op=ALU.mult),
                         reads=["at"], writes=["tm"])
                    S.op("act", lambda e: e.activation(out=tm[:], in_=tm[:], func=AF.Sqrt, scale=-1.0, bias=1.0),
                         reads=["tm"], writes=["tm"])
                    S.op("dve", lambda e: e.tensor_tensor(out=bt[:], in0=bt[:], in1=u[:], op=ALU.mult),
                         reads=["bt", "u"], writes=["bt"])
                    S.op("dve", lambda e: e.tensor_tensor(out=bt[:], in0=bt[:], in1=tm[:], op=ALU.mult),
                         reads=["bt", "tm"], writes=["bt"])
                    if d == 0:
                        pieces = [(0, 1088), (1088, 2176), (2176, 3264), (3264, T)]
                        for pi, (p0, p1) in enumerate(pieces):
                            init = 0.0 if pi == 0 else hf[:, p0 - 1:p0]
                            S.op("dve", lambda e, p0=p0, p1=p1, init=init: e.tensor_tensor_scan(
                                out=hf[:, p0:p1], data0=at[:, p0:p1], data1=bt[:, p0:p1], initial=init,
                                op0=ALU.mult, op1=ALU.add), reads=["at", "bt", "hf"], writes=["hf"])
                    else:
                        S.op("dve", lambda e: e.tensor_tensor_scan(
                            out=hr[:, 0:NCTX][:, ::-1], data0=at[:, 0:NCTX][:, ::-1], data1=bt[:, 0:NCTX][:, ::-1],
                            initial=0.0, op0=ALU.mult, op1=ALU.add), reads=["at", "bt"], writes=["hr"])
                        pieces = [(3328, T), (2304, 3328), (1280, 2304), (NCTX, 1280)]
                        for pi, (p0, p1) in enumerate(pieces):
                            init = hr[:, 0:1] if pi == 0 else hr[:, p1:p1 + 1]
                            S.op("dve", lambda e, p0=p0, p1=p1, init=init: e.tensor_tensor_scan(
                                out=hr[:, p0:p1][:, ::-1], data0=at[:, p0:p1][:, ::-1], data1=bt[:, p0:p1][:, ::-1],
                                initial=init, op0=ALU.mult, op1=ALU.add), reads=["at", "bt", "hr"], writes=["hr"])
                S.op("pool", lambda e: e.tensor_tensor(out=hf[:], in0=hf[:], in1=hr[:], op=ALU.add),
                     reads=["hf", "hr"], writes=["hf"])
                S.op("dve", lambda e: e.tensor_tensor(out=ho[:], in0=hf[:], in1=gy[:], op=ALU.mult),
                     reads=["hf", "gy"], writes=["ho"])
                dma("sp", self.oT[0, cc * 128:(cc + 1) * 128, :], ho[:], reads=["ho"])
            S.barrier()

    def attn_phase(self, l, need_ctx, lam_init):
        S, PS, dma = self.S, self.PS, self.dma
        with ExitStack() as st:
            kT = self.ring(st, "kT", [128, T], BF16, 2)
            vt = self.ring(st, "vt", [128, NKB, 128], BF16, 2)
            qt = self.ring(st, "qt", [128, 512], BF16, 2)
            pT = self.ring(st, "pT", [128, 512], BF16, 4)
            rec = self.ring(st, "rec", [128, 512], F32, 2)
            of = self.ring(st, "of", [128, 512], F32, 3)
            ob = self.ring(st, "ob", [128, 512], BF16, 2)
            sqd = self.sb(st, "sqd", [128, 512], F32)
            rsd = self.sb(st, "rsd", [128, 512], F32)
            sg = self.sb(st, "sga", [128, 8], F32)
            dl = self.sb(st, "dl", [128, 4, 64], F32)
            lam = self.sb(st, "lam", [128, 4], F32)
            dma("sp", sg[:], self.smallg[l], writes=["sga"])
            dma("sp", dl[:], self.diff_lam[l], writes=["dl"])
            for i in range(2):
                S.op("dve", lambda e, i=i: e.tensor_tensor(out=dl[:, 2 * i, :], in0=dl[:, 2 * i, :],
                                                           in1=dl[:, 2 * i + 1, :], op=ALU.mult), reads=["dl"], writes=["dl"])
                S.op("dve", lambda e, i=i: e.tensor_reduce(out=lam[:, i:i + 1], in_=dl[:, 2 * i, :], axis=AX.X, op=ALU.add),
                     reads=["dl"], writes=["lam"])
            S.op("act", lambda e: e.activation(out=lam[:, 0:2], in_=lam[:, 0:2], func=AF.Exp), reads=["lam"], writes=["lam"])
            S.op("dve", lambda e: e.tensor_tensor(out=lam[:, 2:3], in0=lam[:, 1:2], in1=lam[:, 0:1], op=ALU.subtract),
                 reads=["lam"], writes=["lam"])
            S.op("dve", lambda e: e.tensor_scalar(out=lam[:, 2:3], in0=lam[:, 2:3], scalar1=-lam_init, scalar2=None,
                                                  op0=ALU.add), reads=["lam"], writes=["lam"])
            S.op("dve", lambda e: e.tensor_scalar(out=lam[:, 3:4], in0=sg[:, 0:1], scalar1=1.0 - lam_init, scalar2=None,
                                                  op0=ALU.mult), reads=["sga", "lam"], writes=["lam"])

            qgroups = ([(0, 256, 2)] if need_ctx else []) + [(256 + 512 * i, 512, NKB) for i in range(8)]

            def run_job(kt, kk, vtile, vk, dv, qsrc_fn, variants, scale, finish):
                for (q0, n, nkb) in qgroups:
                    q, qk = qt.next()
                    qsrc_fn(q, qk, q0, n)
                    nv = len(variants)
                    sbanks = [0, 1, 2, 7]
                    seq = [(kb, v) for kb in range(nkb) for v in range(nv)]

                    def emit_qk(idx):
                        kb, v = seq[idx]
                        p0, pn = variants[v]
                        bank = sbanks[idx % 4]
                        S.op("pe", lambda e: e.matmul(PS[bank][:, :n], lhsT=kt[p0:p0 + pn, kb * 128:(kb + 1) * 128],
                                                      rhs=q[p0:p0 + pn, :n], start=True, stop=True),
                             reads=[kk, qk], writes=[("ps", bank)])
                        return bank

                    bank_of = {0: emit_qk(0)}
                    for idx in range(len(seq)):
                        kb, v = seq[idx]
                        bank = bank_of.pop(idx)
                        p, pk = pT.next()
                        S.op("act", lambda e, p=p, bank=bank: e.activation(out=p[:, :n], in_=PS[bank][:, :n],
                                                                           func=AF.Exp, scale=scale),
                             reads=[("ps", bank)], writes=[pk])
                        if idx + 1 < len(seq):
                            bank_of[idx + 1] = emit_qk(idx + 1)
                        S.op("pe", lambda e, p=p, kb=kb, v=v: e.matmul(PS[3 + v][:, :n], lhsT=vtile[:, kb, :],
                                                                      rhs=p[:, :n], start=(kb == 0), stop=(kb == nkb - 1)),
                             reads=[pk, vk], writes=[("ps", 3 + v)])
                        if dv == 128:
                            S.op("pe", lambda e, p=p, kb=kb, v=v: e.matmul(PS[5 + v][:, :n], lhsT=self.ones_b[:, :],
                                                                          rhs=p[:, :n], start=(kb == 0), stop=(kb == nkb - 1)),
                                 reads=[pk, "ones_b"], writes=[("ps", 5 + v)])
                    finish(q0, n)

            def load_v(src_ap, dv):
                vtile, vk = vt.next()
                if dv == 64:
                    S.op("pool", lambda e: e.memset(vtile[:, :, 64:128], 1.0), writes=[vk])
                dma("sp", vtile[:, :, :dv], src_ap.rearrange("(kb p) d -> p kb d", p=128), reads=[vk], writes=[vk])
                return vtile, vk

            def normalized(v, dv, n):
                r, rk = rec.next()
                S.op("dve", lambda e: e.reciprocal(out=r[:dv, :n], in_=PS[5 + v][:dv, :n]), reads=[("ps", 5 + v)], writes=[rk])
                o, ok = of.next()
                S.op("dve", lambda e: e.tensor_tensor(out=o[:dv, :n], in0=PS[3 + v][:dv, :n], in1=r[:dv, :n], op=ALU.mult),
                     reads=[("ps", 3 + v), rk], writes=[ok])
                return o, ok

            for h in range(4):
                kt, kk = kT.next()
                dma("sp", kt[:], self.kbT[h * 128:(h + 1) * 128, :], writes=[kk])
                vtile, vk = load_v(self.vb[:, h * 128:(h + 1) * 128], 128)

                def qsrc(q, qk, q0, n, h=h):
                    dma("sp", q[:, :n], self.qbT[h * 128:(h + 1) * 128, q0:q0 + n], writes=[qk])

                def finish(q0, n, h=h):
                    o1, o1k = normalized(0, 128, n)
                    o2, o2k = normalized(1, 128, n)
                    od, odk = of.next()
                    S.op("dve", lambda e: e.scalar_tensor_tensor(out=od[:, :n], in0=o2[:, :n], scalar=lam[:, 2:3],
                                                                 in1=o1[:, :n], op0=ALU.mult, op1=ALU.add),
                         reads=[o1k, o2k, "lam"], writes=[odk])
                    S.op("act", lambda e: e.activation(out=sqd[:, :n], in_=od[:, :n], func=AF.Square),
                         reads=[odk], writes=["sqd"])
                    S.op("pe", lambda e: e.matmul(PS[7][:, :n], lhsT=self.ones_f[:], rhs=sqd[:, :n], start=True, stop=True),
                         reads=["sqd", "ones_f"], writes=[("ps", 7)])
                    S.op("act", lambda e: e.activation(out=rsd[:, :n], in_=PS[7][:, :n], func=AF.Sqrt,
                                                       bias=self.epsc[:], scale=1.0 / 128),
                         reads=[("ps", 7), "epsc"], writes=["rsd"])
                    S.op("dve", lambda e: e.reciprocal(out=rsd[:, :n], in_=rsd[:, :n]), reads=["rsd"], writes=["rsd"])
                    o, ok = ob.next()
                    S.op("dve", lambda e: e.scalar_tensor_tensor(out=o[:, :n], in0=od[:, :n], scalar=lam[:, 3:4],
                                                                 in1=rsd[:, :n], op0=ALU.mult, op1=ALU.mult),
                         reads=[odk, "rsd", "lam"], writes=[ok])
                    dma("sp", self.oT[1, h * 128:(h + 1) * 128, q0:q0 + n], o[:, :n], reads=[ok])

                run_job(kt, kk, vtile, vk, 128, qsrc, [(0, 64), (64, 64)], 0.125, finish)

            def simple_finish(branch, h):
                def fin(q0, n):
                    r, rk = rec.next()
                    S.op("dve", lambda e: e.reciprocal(out=r[64:128, :n], in_=PS[3][64:128, :n]), reads=[("ps", 3)], writes=[rk])
                    o, ok = ob.next()
                    S.op("dve", lambda e: e.tensor_tensor(out=o[:64, :n], in0=PS[3][0:64, :n], in1=r[64:128, :n], op=ALU.mult),
                         reads=[("ps", 3), rk], writes=[ok])
                    dma("sp", self.oT[branch, h * 64:(h + 1) * 64, q0:q0 + n], o[:64, :n], reads=[ok])
                return fin

            for h in range(8):
                g = h // 4
                kt, kk = kT.next()
                dma("sp", kt[0:64, :], self.kgT[g * 64:(g + 1) * 64, :], writes=[kk])
                vtile, vk = load_v(self.vg[:, g * 64:(g + 1) * 64], 64)

                def qsrc(q, qk, q0, n, h=h):
                    dma("sp", q[0:64, :n], self.qgT[h * 64:(h + 1) * 64, q0:q0 + n], writes=[qk])

                run_job(kt, kk, vtile, vk, 64, qsrc, [(0, 64)], 0.125, simple_finish(2, h))
            for h in range(8):
                kt, kk = kT.next()
                dma("sp", kt[0:96, :], self.kmT[h], writes=[kk])
                vtile, vk = load_v(self.vm[:, h * 64:(h + 1) * 64], 64)

                def qsrc(q, qk, q0, n, h=h):
                    dma("sp", q[0:96, :n], self.qmT[h, :, q0:q0 + n], writes=[qk])

                run_job(kt, kk, vtile, vk, 64, qsrc, [(0, 96)], 96.0 ** -0.5, simple_finish(3, h))
            S.barrier()

    def merge_phase(self, l, need_ctx, xcur):
        S, PS, dma = self.S, self.PS, self.dma
        with ExitStack() as st:
            wb = self.sb(st, "wb", [128, 16, 1024], BF16)
            wo = self.sb(st, "wo", [128, 8, 1024], BF16)
            otr = self.ring(st, "otr", [128, 16, 512], BF16, 2)
            gtr = self.ring(st, "gtr", [128, 512], BF16, 4)
            macc = self.sb(st, "macc", [128, 8, 512], F32)
            mtmp = self.ring(st, "mtmp", [128, 512], F32, 2)
            mb = self.sb(st, "mb", [128, 8, 512], BF16)
            xr = self.ring(st, "xr2", [128, 8, 512], F32, 2)
            sq = self.sb(st, "sq2", [128, 8, 512], F32)
            rs = self.sb(st, "rs2", [128, 512], F32)
            tmpr = self.ring(st, "tmpr2", [128, 512], F32, 2)
            hst = self.ring(st, "hst", [128, 8, 512], BF16, 2)
            for k in range(4):
                dma("pool", wb[:, k * 4:(k + 1) * 4, :], self.w_br[l, k].rearrange("(c p) d -> p c d", p=128), writes=["wb"])
            dma("pool", wo[:], self.w_out[l].rearrange("(k p) d -> p k d", p=128), writes=["wo"])
            self.make_A(2 + l, 4)
            osrc = self.oT.rearrange("k (c p) t -> p k c t", p=128)
            xsrc = xcur.rearrange("(k p) t -> p k t", p=128)
            xdst = self.xT_a.rearrange("(k p) t -> p k t", p=128)
            hdst = self.h2T.rearrange("(k p) t -> p k t", p=128)
            tiles = list(enumerate(TILES)) if need_ctx else list(enumerate(TILES))[1:]
            bc = 0
            for ti, (t0, n) in tiles:
                s = 1 if ti == 0 else 0
                ot, otk = otr.next()
                for k in range(4):
                    dma("sp", ot[:, k * 4:(k + 1) * 4, :n], osrc[:, k, :, t0:t0 + n], writes=[otk])
                xt, xk = xr.next()
                dma("sp", xt[:, :, :n], xsrc[:, :, t0:t0 + n], writes=[xk])
                for j in range(8):
                    for k in range(4):
                        gt, gk = gtr.next()
                        r0 = k * 1024 + j * 128
                        dma("sp", gt[:, :n], self.gT[r0:r0 + 128, t0:t0 + n], writes=[gk])
                        bank = bc % 3
                        bc += 1
                        for c in range(4):
                            S.op("pe", lambda e, c=c, k=k, j=j: e.matmul(
                                PS[bank][:, :n], lhsT=wb[:, k * 4 + c, j * 128:(j + 1) * 128], rhs=ot[:, k * 4 + c, :n],
                                start=(c == 0), stop=(c == 3)), reads=["wb", otk], writes=[("ps", bank)])
                        if k == 0:
                            S.op("dve", lambda e, j=j, gt=gt: e.tensor_tensor(out=macc[:, j, :n], in0=PS[bank][:, :n],
                                                                              in1=gt[:, :n], op=ALU.mult),
                                 reads=[("ps", bank), gk], writes=[("macc", j)])
                        else:
                            mt, mk = mtmp.next()
                            S.op("dve", lambda e, gt=gt, mt=mt: e.tensor_tensor(out=mt[:, :n], in0=PS[bank][:, :n],
                                                                                in1=gt[:, :n], op=ALU.mult),
                                 reads=[("ps", bank), gk], writes=[mk])
                            S.op("pool", lambda e, j=j, mt=mt: e.tensor_tensor(out=macc[:, j, :n], in0=macc[:, j, :n],
                                                                               in1=mt[:, :n], op=ALU.add),
                                 reads=[mk, ("macc", j)], writes=[("macc", j)])
                    S.op("act", lambda e, j=j: e.copy(out=mb[:, j, :n], in_=macc[:, j, :n]),
                         reads=[("macc", j)], writes=[("mb", j)])
                for jo in range(8):
                    bank = 3 + (jo % 3)
                    for j in range(8):
                        S.op("pe", lambda e, j=j, jo=jo: e.matmul(PS[bank][:, :n], lhsT=wo[:, j, jo * 128:(jo + 1) * 128],
                                                                  rhs=mb[:, j, :n], start=(j == 0), stop=(j == 7)),
                             reads=["wo"] + [("mb", jj) for jj in range(8)], writes=[("ps", bank)])
                    S.op("dve", lambda e, jo=jo: e.scalar_tensor_tensor(
                        out=xt[:, jo, :n], in0=PS[bank][:, :n], scalar=self.modT[:, 16 + jo, s:s + 1],
                        in1=xt[:, jo, :n], op0=ALU.mult, op1=ALU.add), reads=[("ps", bank), xk, "modT"], writes=[xk])
                dma("sp", xdst[:, :, t0:t0 + n], xt[:, :, :n], reads=[xk])
                hs, hk = hst.next()
                self.adaln_tile(xt, xk, n, lambda k, s=s: self.A1[:, k, s:s + 1],
                                lambda k, s=s: self.modT[:, 24 + k, s:s + 1],
                                lambda k, hs=hs, n=n: hs[:, k, :n], hk, sq, rs, tmpr, 7)
                dma("sp", hdst[:, :, t0:t0 + n], hs[:, :, :n], reads=[hk])
            S.barrier()

    def peer_phase(self, l, need_ctx, final):
        S, PS, dma = self.S, self.PS, self.dma
        with ExitStack() as st:
            sb = lambda name, shape, dt: self.sb(st, name, shape, dt)
            wq = sb("wq", [128, 8, 1024], BF16)
            kbd = sb("kbd", [128, 8, 256], F32)
            h2r = self.ring(st, "h2r", [128, 8, PT], BF16, 2)
            qTh = sb("qTh", [128, 8, PT], F32)
            sc = sb("sc", [128, 8, 256], F32)
            v8 = sb("v8", [128, 8, 2, 16], F32)
            i8u = sb("i8u", [128, 8, 2, 16], U32)
            i8f = sb("i8f", [128, 8, 2, 16], F32)
            cand = sb("cand", [128, 8, 256], F32)
            cand2 = sb("cand2", [128, 8, 256], F32)
            wk1 = cand2
            best = sb("best", [128, 8, 16], F32)
            posu = sb("posu", [128, 8, 16], U32)
            posf = sb("posf", [128, 8, 16], F32)
            af = sb("af", [128, 8, 16], F32)
            bf = sb("bf", [128, 8, 16], F32)
            oh = sb("oh", [128, 128, 16], F32)
            iw = sb("iw", [128, 128], F32)
            jw = sb("jw", [128, 128], F32)
            gw = sb("gw", [128, 8, 16], F32)
            zs = sb("zs", [128, 8], F32)
            iT = sb("iT", [128, PT], F32)
            jT = sb("jT", [128, PT], F32)
            gTt = sb("gTt", [128, PT], F32)
            lhr = self.ring(st, "lhr", [128, 128], BF16, 4)
            rhr = self.ring(st, "rhr", [128, 128], BF16, 4)
            GT = sb("GT", [128, 128, PT], BF16)
            usr = self.ring(st, "usr", [128, 8, CG, 128], BF16, 2)
            vsr = self.ring(st, "vsr", [128, CG, 1024], BF16, 2)
            agr = self.ring(st, "agr", [128, PT], BF16, 3)
            mgr = self.ring(st, "mgr", [128, PT], BF16, 3)
            xt = sb("xtp", [128, 8, PT], F32)
            sq = cand2
            rs = sb("rs3", [128, PT], F32)
            tmpr = self.ring(st, "tmpr3", [128, PT], F32, 2)
            ost = cand
            dma("pool", wq[:], self.w_pq[l].rearrange("(k p) d -> p k d", p=128), writes=["wq"])
            dma("sp", kbd[:], self.keysbd[l].rearrange("h p n -> p h n"), writes=["kbd"])
            usrc = self.ubf[l].rearrange("(g p) (k c i) -> g p k c i", p=128, k=8, c=CG)
            vsrc = self.vbf[l].rearrange("(c i) d -> i c d", i=128)
            hsrc = self.h2T.rearrange("(k p) t -> p k t", p=128)
            xsrc = self.xT_a.rearrange("(k p) t -> p k t", p=128)
            tiles = PTILES if need_ctx else PTILES[1:]
            iota16 = self.iota_f[:, 0:16]
            iT2 = [iT, sb("iTb", [128, PT], F32)]
            jT2 = [jT, sb("jTb", [128, PT], F32)]
            gT2 = [gTt, sb("gTtb", [128, PT], F32)]
            h2of = {}

            def scores_topk(idx, sub):
                t0, n = tiles[idx]
                tsl = slice(sub * 128, (sub + 1) * 128)
                for h in range(8):
                    bank = 4 + (h % 2)
                    S.op("pe", lambda e, h=h: e.matmul(PS[bank][:, :256], lhsT=qTh[:, h, tsl], rhs=kbd[:, h, :],
                                                       start=True, stop=True),
                         reads=[("qTh", h), "kbd"], writes=[("ps", bank)])
                    S.op("act", lambda e, h=h: e.copy(out=sc[:, h, :], in_=PS[bank][:, :256]),
                         reads=[("ps", bank)], writes=[("sc", h)])
                for h in range(8):
                    for p in range(2):
                        src = sc[:, h, p * 128:(p + 1) * 128]
                        wks = wk1[:, h, p * 128:(p + 1) * 128]
                        S.op("dve", lambda e, h=h, p=p, src=src: e.max(out=v8[:, h, p, 0:8], in_=src),
                             reads=[("sc", h)], writes=["v8"])
                        S.op("dve", lambda e, h=h, p=p, src=src: e.max_index(out=i8u[:, h, p, 0:8], in_max=v8[:, h, p, 0:8],
                                                                              in_values=src),
                             reads=[("sc", h), "v8"], writes=["i8u"])
                        S.op("dve", lambda e, h=h, p=p, src=src, wks=wks: e.match_replace(
                            out=wks, in_to_replace=v8[:, h, p, 0:8], in_values=src, imm_value=-1e30),
                            reads=[("sc", h), "v8"], writes=["cand2"])
                        S.op("dve", lambda e, h=h, p=p, wks=wks: e.max(out=v8[:, h, p, 8:16], in_=wks),
                             reads=["cand2"], writes=["v8"])
                        S.op("dve", lambda e, h=h, p=p, wks=wks: e.max_index(out=i8u[:, h, p, 8:16],
                                                                             in_max=v8[:, h, p, 8:16], in_values=wks),
                             reads=["cand2", "v8"], writes=["i8u"])
                S.op("dve", lambda e: e.tensor_copy(out=i8f[:], in_=i8u[:]), reads=["i8u"], writes=["i8f"])
                S.op("dve", lambda e: e.tensor_tensor(
                    out=cand[:].rearrange("p h (a b) -> p h a b", b=16),
                    in0=v8[:, :, 0, :].unsqueeze(3).to_broadcast([128, 8, 16, 16]),
                    in1=v8[:, :, 1, :].unsqueeze(2).to_broadcast([128, 8, 16, 16]), op=ALU.add),
                    reads=["v8"], writes=["cand"])
                for h in range(8):
                    S.op("dve", lambda e, h=h: e.max(out=best[:, h, 0:8], in_=cand[:, h, :]), reads=["cand"], writes=["best"])
                    S.op("dve", lambda e, h=h: e.max_index(out=posu[:, h, 0:8], in_max=best[:, h, 0:8], in_values=cand[:, h, :]),
                         reads=["cand", "best"], writes=["posu"])
                    S.op("dve", lambda e, h=h: e.match_replace(out=cand2[:, h, :], in_to_replace=best[:, h, 0:8],
                                                               in_values=cand[:, h, :], imm_value=-1e30),
                         reads=["cand", "best"], writes=["cand2"])
                    S.op("dve", lambda e, h=h: e.max(out=best[:, h, 8:16], in_=cand2[:, h, :]), reads=["cand2"], writes=["best"])
                    S.op("dve", lambda e, h=h: e.max_index(out=posu[:, h, 8:16], in_max=best[:, h, 8:16], in_values=cand2[:, h, :]),
                         reads=["cand2", "best"], writes=["posu"])
                S.op("dve", lambda e: e.tensor_single_scalar(out=posf[:].bitcast(U32), in_=posu[:], scalar=4,
                                                             op=ALU.logical_shift_right), reads=["posu"], writes=["posf"])
                S.op("dve", lambda e: e.tensor_copy(out=af[:], in_=posf[:].bitcast(U32)), reads=["posf"], writes=["af"])
                S.op("dve", lambda e: e.tensor_single_scalar(out=posf[:].bitcast(U32), in_=posu[:], scalar=15,
                                                             op=ALU.bitwise_and), reads=["posu", "af"], writes=["posf"])
                S.op("dve", lambda e: e.tensor_copy(out=bf[:], in_=posf[:].bitcast(U32)), reads=["posf"], writes=["bf"])
                for (src, p, dst, dk) in ((af, 0, iw, "iw"), (bf, 1, jw, "jw")):
                    S.op("dve", lambda e, src=src: e.tensor_tensor(
                        out=oh[:], in0=src[:].rearrange("p h k -> p (h k)").unsqueeze(2).to_broadcast([128, 128, 16]),
                        in1=iota16.unsqueeze(1).to_broadcast([128, 128, 16]), op=ALU.is_equal),
                        reads=["af", "bf", "iota_f"], writes=["oh"])
                    S.op("dve", lambda e, p=p: e.tensor_tensor(
                        out=oh[:].rearrange("p (h k) a -> p h k a", k=16),
                        in0=oh[:].rearrange("p (h k) a -> p h k a", k=16),
                        in1=i8f[:, :, p, :].unsqueeze(2).to_broadcast([128, 8, 16, 16]), op=ALU.mult),
                        reads=["oh", "i8f"], writes=["oh"])
                    S.op("dve", lambda e, dst=dst: e.tensor_reduce(out=dst[:], in_=oh[:], axis=AX.X, op=ALU.add),
                         reads=["oh"], writes=[dk])
                S.op("dve", lambda e: e.tensor_tensor(out=gw[:], in0=best[:],
                                                      in1=best[:, :, 0:1].to_broadcast([128, 8, 16]), op=ALU.subtract),
                     reads=["best"], writes=["gw"])
                S.op("act", lambda e: e.activation(out=gw[:], in_=gw[:], func=AF.Exp), reads=["gw"], writes=["gw"])
                S.op("dve", lambda e: e.tensor_reduce(out=zs[:], in_=gw[:], axis=AX.X, op=ALU.add), reads=["gw"], writes=["zs"])
                S.op("dve", lambda e: e.reciprocal(out=zs[:], in_=zs[:]), reads=["zs"], writes=["zs"])
                S.op("dve", lambda e: e.tensor_tensor(out=gw[:], in0=gw[:], in1=zs[:].unsqueeze(2).to_broadcast([128, 8, 16]),
                                                      op=ALU.mult), reads=["gw", "zs"], writes=["gw"])

            def transposes(idx, sub):
                par = idx % 2
                tsl = slice(sub * 128, (sub + 1) * 128)
                for (src_ap, sk, dstT, dk, bank) in ((iw[:], "iw", iT2[par], ("iT", par), 4), (jw[:], "jw", jT2[par], ("jT", par), 5),
                                                     (gw[:].rearrange("p h k -> p (h k)"), "gw", gT2[par], ("gTt", par), 4)):
                    S.op("pe", lambda e, src_ap=src_ap, bank=bank: e.transpose(out=PS[bank][:, :128], in_=src_ap,
                                                                               identity=self.ident[:]),
                         reads=[sk, "ident"], writes=[("ps", bank)])
                    S.op("act", lambda e, dstT=dstT, bank=bank: e.copy(out=dstT[:, tsl], in_=PS[bank][:, :128]),
                         reads=[("ps", bank)], writes=[dk])

            def stageA(idx, part):
                t0, n = tiles[idx]
                if part == 0:
                    h2, h2k = h2r.next()
                    h2of[idx] = (h2, h2k)
                    dma("sp", h2[:], hsrc[:, :, t0:t0 + n], writes=[h2k])
                    for h in range(8):
                        bank = 4 + (h % 2)
                        for k in range(8):
                            S.op("pe", lambda e, h=h, k=k: e.matmul(PS[bank][:, :n], lhsT=wq[:, k, h * 128:(h + 1) * 128],
                                                                    rhs=h2[:, k, :], start=(k == 0), stop=(k == 7)),
                                 reads=["wq", h2k], writes=[("ps", bank)])
                        S.op("act", lambda e, h=h: e.copy(out=qTh[:, h, :], in_=PS[bank][:, :n]),
                             reads=[("ps", bank)], writes=[("qTh", h)])
                    scores_topk(idx, 0)
                elif part == 1:
                    transposes(idx, 0)
                    scores_topk(idx, 1)
                else:
                    transposes(idx, 1)

            for part in range(3):
                stageA(0, part)
            hooks = {2: 0, 12: 1, 24: 2}
            for idx, (t0, n) in enumerate(tiles):
                s = 1 if t0 < NCTX else 0
                par = idx % 2
                iTc, jTc, gTc = iT2[par], jT2[par], gT2[par]
                h2, h2k = h2of.pop(idx)
                dma("sp", xt[:], xsrc[:, :, t0:t0 + n], writes=["xtp"])
                for tg in range(n // 4):
                    bank = 4 + (tg % 2)
                    for tt in range(4):
                        tok = tg * 4 + tt
                        lh, lk = lhr.next()
                        rh, rk = rhr.next()
                        S.op("dve", lambda e, lh=lh, tok=tok: e.tensor_scalar(
                            out=lh[:], in0=self.iota_f[:], scalar1=iTc[:, tok:tok + 1], scalar2=gTc[:, tok:tok + 1],
                            op0=ALU.is_equal, op1=ALU.mult), reads=[("iT", par), ("gTt", par), "iota_f"], writes=[lk])
                        S.op("pool" if tt % 2 == 0 else "dve", lambda e, rh=rh, tok=tok: e.tensor_scalar(
                            out=rh[:], in0=self.iota_f[:], scalar1=jTc[:, tok:tok + 1], scalar2=None,
                            op0=ALU.is_equal), reads=[("jT", par), "iota_f"], writes=[rk])
                        S.op("pe", lambda e, lh=lh, rh=rh, tt=tt: e.matmul(PS[bank][:, tt * 128:(tt + 1) * 128], lhsT=lh[:],
                                                                           rhs=rh[:], start=True, stop=True),
                             reads=[lk, rk], writes=[("ps", bank)])
                    S.op("act", lambda e, tg=tg, bank=bank: e.copy(
                        out=GT[:, :, tg * 4:(tg + 1) * 4].rearrange("p j t -> p t j"),
                        in_=PS[bank][:, :].rearrange("p (t j) -> p t j", j=128)), reads=[("ps", bank)], writes=["GT"])
                for cg in range(128 // CG):
                    if cg in hooks and idx + 1 < len(tiles):
                        stageA(idx + 1, hooks[cg])
                    us, uk = usr.next()
                    vs, vk = vsr.next()
                    dma("sp", us[:], usrc[cg], writes=[uk])
                    dma("sp", vs[:], vsrc[:, cg * CG:(cg + 1) * CG, :], writes=[vk])
                    for cl in range(CG):
                        c = cg * CG + cl
                        bank = 6 + (c % 2)
                        for k in range(8):
                            S.op("pe", lambda e, k=k, cl=cl: e.matmul(PS[bank][:, :n], lhsT=us[:, k, cl, :], rhs=h2[:, k, :],
                                                                      start=(k == 0), stop=(k == 7)),
                                 reads=[uk, h2k], writes=[("ps", bank)])
                        ag, agk = agr.next()
                        S.op("act", lambda e, ag=ag, bank=bank: e.activation(out=ag[:], in_=PS[bank][:, :n],
                                                                             func=AF.Gelu_apprx_tanh),
                             reads=[("ps", bank)], writes=[agk])
                        mg, mgk = mgr.next()
                        S.op("pool", lambda e, ag=ag, mg=mg, c=c: e.tensor_tensor(out=mg[:], in0=ag[:], in1=GT[:, c, :],
                                                                                  op=ALU.mult), reads=[agk, "GT"], writes=[mgk])
                        for dc in range(8):
                            S.op("pe", lambda e, dc=dc, cl=cl, mg=mg, c=c: e.matmul(
                                PS[dc // 2][:, (dc % 2) * PT:(dc % 2 + 1) * PT], lhsT=vs[:, cl, dc * 128:(dc + 1) * 128],
                                rhs=mg[:], start=(c == 0), stop=(c == 127)), reads=[vk, mgk], writes=[("pso", dc)])
                for dc in range(8):
                    S.op("dve", lambda e, dc=dc: e.scalar_tensor_tensor(
                        out=xt[:, dc, :], in0=PS[dc // 2][:, (dc % 2) * PT:(dc % 2 + 1) * PT],
                        scalar=self.modT[:, 40 + dc, s:s + 1], in1=xt[:, dc, :], op0=ALU.mult, op1=ALU.add),
                        reads=[("pso", dc), "xtp", "modT"], writes=["xtp"])
                if not final:
                    dma("sp", xsrc[:, :, t0:t0 + n], xt[:], reads=["xtp"])
                else:
                    self.adaln_tile(xt, "xtp", n, lambda k: self.ngs[:, 4, k:k + 1], lambda k: None,
                                    lambda k: ost[:, k, :], "cand", sq, rs, tmpr, 6, sqkey="cand2")
                    dma("sp", self.outT.rearrange("(k p) t -> p k t", p=128)[:, :, t0 - NCTX:t0 - NCTX + n], ost[:],
                        reads=["cand"])
            S.barrier()


def _lay(v):
    return np.ascontiguousarray(np.asarray(v, np.float32).reshape(-1, 128).T)


def _rope_tables():
    t = np.arange(NLAT)
    r = (t // 64).astype(np.float32)
    c = (t % 64).astype(np.float32)

    def tab(dim):
        nf = dim // 4
        inv = (np.float32(10000.0) ** (-np.arange(nf, dtype=np.float32) / np.float32(nf))).astype(np.float32)
        ang = np.concatenate([r[:, None] * inv[None, :], c[:, None] * inv[None, :]], axis=1).astype(np.float32)
        return np.cos(ang).astype(np.float32), np.sin(ang).astype(np.float32)

    c64, s64 = tab(64)
    c32, s32 = tab(32)
    rope64 = np.zeros((2, 128, NLAT), np.float32)
    for row in range(128):
        rope64[0, row] = c64[:, row % 32]
        rope64[1, row] = s64[:, row % 32]
    ropem = np.zeros((2, 96, NLAT), np.float32)
    ropem[0, :64] = 1.0
    for row in range(32):
        ropem[0, 64 + row] = c32[:, row % 16]
        ropem[1, 64 + row] = s32[:, row % 16]
    rm = np.zeros((3, 128, 128), np.float32)
    for B in (0, 64):
        for m in range(32):
            rm[0, B + m + 32, B + m] = -1.0
            rm[0, B + m, B + m + 32] = 1.0
    for m in range(16):
        rm[1, 64 + m + 16, 64 + m] = -1.0
        rm[1, 64 + m, 64 + m + 16] = 1.0
        rm[2, m + 16, m] = -1.0
        rm[2, m, m + 16] = 1.0
    return rope64, ropem, rm


def prep_inputs(inp):
    g = {k: np.asarray(v) for k, v in inp.items()}
    L = DEPTH
    shared = {}
    shared["w_mod"] = g["w_mod"]
    shared["b_mod_lay"] = np.ascontiguousarray(g["b_mod"].reshape(L, 48, 128).transpose(0, 2, 1))
    ngl = np.zeros((128, 5, 8), np.float32)
    ngl[:, 0] = _lay(g["norm1_g"][0]); ngl[:, 1] = _lay(g["norm1_g"][1])
    ngl[:, 2] = _lay(g["norm2_g"][0]); ngl[:, 3] = _lay(g["norm2_g"][1])
    ngl[:, 4] = _lay(g["final_norm_g"])
    shared["norm_g_lay"] = ngl
    shared["w_in"] = g["w_in"]
    cw = np.zeros((L, 128, 4, 5), np.float32)
    for l in range(L):
        for tap in range(4):
            cw[l, :, :, tap] = g["conv_w"][l, tap].reshape(4, 128).T
        cw[l, :, :, 4] = g["conv_b"][l].reshape(4, 128).T
    shared["convw_lay"] = cw
    bd = np.zeros((L, 2, 2, 4, 128, 128), np.float32)
    for l in range(L):
        for d in range(2):
            for kind, wname in enumerate(("lru_wa", "lru_wi")):
                w = g[wname][l, d]
                for cc in range(4):
                    bd[l, d, kind, cc, 0:64, 0:64] = w[2 * cc]
                    bd[l, d, kind, cc, 64:128, 64:128] = w[2 * cc + 1]
    shared["lru_bd"] = bd
    lv = np.zeros((L, 128, 2, 3, 4), np.float32)
    for l in range(L):
        for d in range(2):
            for kind, nm in enumerate(("lru_ba", "lru_bi", "lru_lambda")):
                lv[l, :, d, kind, :] = g[nm][l, d].reshape(4, 128).T
    shared["lru_vec"] = lv
    shared["diff_lam_rep"] = np.ascontiguousarray(np.broadcast_to(g["diff_lam"][:, None], (L, 128, 4, 64)))
    sgm = np.zeros((L, 128, 8), np.float32)
    for l in range(L):
        sgm[l, :, 0] = g["diff_subln_g"][l]
        sgm[l, :, 1] = np.tile(g["gqa_qnorm_g"][l], 2)
        sgm[l, :, 2] = np.tile(g["gqa_knorm_g"][l], 2)
        sgm[l, :, 3:6] = g["mla_qnorm_g"][l].reshape(3, 128).T
        sgm[l, :, 6:8] = g["mla_kvnorm_g"][l].reshape(2, 128).T
    shared["smallg"] = sgm
    shared["mla_w_uq"] = g["mla_w_uq"]
    shared["mla_w_ukv"] = g["mla_w_ukv"]
    shared["w_branch"] = g["w_branch"]
    shared["w_out"] = g["w_out"]
    shared["peer_wq"] = g["peer_wq"]
    kb = np.zeros((L, 8, 128, 256), np.float32)
    for p in range(2):
        kb[:, :, p * 64:(p + 1) * 64, p * 128:(p + 1) * 128] = g["peer_keys"][:, :, p].transpose(0, 1, 3, 2)
    shared["keysbd"] = kb
    shared["peer_uT"] = np.ascontiguousarray(
        g["peer_u"].reshape(L, 128, 128 // CG, CG, 8, 128).transpose(0, 2, 5, 4, 3, 1).reshape(L, 4096, 4096))
    shared["peer_vP"] = np.ascontiguousarray(
        g["peer_v"].reshape(L, 128, 128, D).transpose(0, 2, 1, 3).reshape(L, 16384, D))
    rope64, ropem, rm = _rope_tables()
    shared["rope64"] = rope64
    shared["ropem"] = ropem
    shared["rmats"] = rm
    cst = np.zeros((3, 128, 128), np.float32)
    cst[0] = np.eye(128, dtype=np.float32)
    cst[1] = 1.0
    cst[2, :64, :64] = 1.0
    cst[2, 64:, 64:] = 1.0
    shared["consts"] = cst
    shared["iota_in"] = np.ascontiguousarray(np.broadcast_to(np.arange(128, dtype=np.float32)[None, :], (128, 128)))
    maps = []
    for b in range(8):
        m = dict(shared)
        xall = np.concatenate([g["ctx"][b], g["x"][b]], axis=0)
        m["xT_in"] = np.ascontiguousarray(xall.T)
        cvv = np.zeros((128, 8, 2), np.float32)
        cvv[:, :, 0] = _lay(g["c"][b])
        cvv[:, :, 1] = _lay(g["c_ctx"])
        m["cvec"] = cvv
        maps.append(m)
    return maps


def kernel(**inputs):
    maps = prep_inputs(inputs)
    nc = Prog().build()
    res = run_bass_kernel_spmd(nc, maps, core_ids=list(range(8)))
    out = np.stack([np.ascontiguousarray(res.results[b]["outT"].T) for b in range(8)], axis=0)
    return out.astype(np.float32)
```

```python
import math
import numpy as np
from contextlib import ExitStack
import concourse.bass as bass
import concourse.mybir as mybir
from concourse.bass_utils import run_bass_kernel_spmd

F32 = mybir.dt.float32
BF16 = mybir.dt.bfloat16
U32 = mybir.dt.uint32
AF = mybir.ActivationFunctionType
ALU = mybir.AluOpType
AX = mybir.AxisListType

D = 1024
NCTX = 256
NLAT = 4096
T = NCTX + NLAT
DEPTH = 2
EPS = 1e-6
IN_COLS = 8096
NKB = T // 128
TILES = [(0, 256)] + [(256 + 512 * i, 512) for i in range(8)]
PT = 256
PTILES = [(i * PT, PT) for i in range(T // PT)]
CG = 4


class Sched:
    def __init__(self, nc, es, n_dma=40):
        self.nc = nc
        self.E = {"pe": nc.tensor, "act": nc.scalar, "dve": nc.vector, "pool": nc.gpsimd, "sp": nc.sync}
        self.sem = {k: es.enter_context(nc.semaphore("s_" + k)) for k in self.E}
        self.cnt = {k: 0 for k in self.E}
        self.seen = {k: {} for k in self.E}
        self.nd = n_dma
        self.dsem = [es.enter_context(nc.semaphore("d%d" % i)) for i in range(n_dma)]
        self.dval = [0] * n_dma
        self.dnext = 0
        self.lastw = {}
        self.readers = {}
        self.same_wait = {"pe": False, "act": True, "dve": True, "pool": True, "sp": True}

    def _wait(self, ek, key, val):
        if self.seen[ek].get(key, 0) >= val:
            return
        kind, idx = key
        if kind == "e" and idx == ek and not self.same_wait[ek]:
            return
        sem = self.sem[idx] if kind == "e" else self.dsem[idx]
        self.E[ek].wait_ge(sem, val)
        self.seen[ek][key] = val

    def op(self, ek, fn, reads=(), writes=(), dma=False):
        deps = {}
        for r in reads:
            t = self.lastw.get(r)
            if t is not None and deps.get(t[0], 0) < t[1]:
                deps[t[0]] = t[1]
        for w in writes:
            t = self.lastw.get(w)
            if t is not None and deps.get(t[0], 0) < t[1]:
                deps[t[0]] = t[1]
            for k, v in self.readers.get(w, {}).items():
                if deps.get(k, 0) < v:
                    deps[k] = v
        for k, v in deps.items():
            self._wait(ek, k, v)
        if dma:
            i = self.dnext
            self.dnext = (i + 1) % self.nd
            if self.dval[i] > 0:
                self._wait(ek, ("d", i), self.dval[i])
            ins = fn(self.E[ek])
            self.dval[i] += 16
            ins.then_inc(self.dsem[i], 16)
            tok = (("d", i), self.dval[i])
        else:
            ins = fn(self.E[ek])
            self.cnt[ek] += 1
            ins.then_inc(self.sem[ek], 1)
            tok = (("e", ek), self.cnt[ek])
        for w in writes:
            self.lastw[w] = tok
            self.readers[w] = {}
        for r in reads:
            d = self.readers.setdefault(r, {})
            if d.get(tok[0], 0) < tok[1]:
                d[tok[0]] = tok[1]
        return tok

    def barrier(self):
        for ek in self.E:
            for k2 in self.E:
                if k2 != ek and self.cnt[k2] > 0:
                    self._wait(ek, ("e", k2), self.cnt[k2])
            for i in range(self.nd):
                if self.dval[i] > 0:
                    self._wait(ek, ("d", i), self.dval[i])
        self.lastw.clear()
        self.readers.clear()


class Ring:
    def __init__(self, tensors, name):
        self.t = tensors
        self.name = name
        self.i = 0

    def next(self):
        j = self.i % len(self.t)
        self.i += 1
        return self.t[j], (self.name, j)


class Prog:
    def __init__(self, dbg=None):
        self.dbg = dbg
        nc = self.nc = bass.Bass("TRN2", target_bir_lowering=False)
        self.uid = 0

        def din(name, shape, dt=F32):
            return nc.dram_tensor(name, list(shape), dt, kind="ExternalInput").ap()

        def dscr(name, shape, dt):
            kind = "ExternalOutput" if dbg else "Internal"
            return nc.dram_tensor(name, list(shape), dt, kind=kind).ap()

        self.xT_in = din("xT_in", [D, T])
        self.cvec = din("cvec", [128, 8, 2])
        self.w_mod = din("w_mod", [DEPTH, D, 6 * D])
        self.b_mod = din("b_mod_lay", [DEPTH, 128, 48])
        self.ng = din("norm_g_lay", [128, 5, 8])
        self.w_in = din("w_in", [DEPTH, D, IN_COLS])
        self.convw = din("convw_lay", [DEPTH, 128, 4, 5])
        self.lru_bd = din("lru_bd", [DEPTH, 2, 2, 4, 128, 128])
        self.lru_vec = din("lru_vec", [DEPTH, 128, 2, 3, 4])
        self.diff_lam = din("diff_lam_rep", [DEPTH, 128, 4, 64])
        self.smallg = din("smallg", [DEPTH, 128, 8])
        self.w_uq = din("mla_w_uq", [DEPTH, 384, 768])
        self.w_ukv = din("mla_w_ukv", [DEPTH, 256, 1024])
        self.w_br = din("w_branch", [DEPTH, 4, 512, 1024])
        self.w_out = din("w_out", [DEPTH, D, D])
        self.w_pq = din("peer_wq", [DEPTH, D, D])
        self.keysbd = din("keysbd", [DEPTH, 8, 128, 256])
        self.UT = din("peer_uT", [DEPTH, 4096, 4096])
        self.VP = din("peer_vP", [DEPTH, 16384, D])
        self.rope64 = din("rope64", [2, 128, NLAT])
        self.ropem = din("ropem", [2, 96, NLAT])
        self.rmats = din("rmats", [3, 128, 128])
        self.consts = din("consts", [3, 128, 128])
        self.iota_in = din("iota_in", [128, 128])
        self.outT = nc.dram_tensor("outT", [D, NLAT], F32, kind="ExternalOutput").ap()

        self.xT_a = dscr("xT_a", [D, T], F32)
        self.xaT = dscr("xaT", [512, T], F32)
        self.yaT = dscr("yaT", [512, T], BF16)
        self.qbT = dscr("qbT", [512, T], BF16)
        self.kbT = dscr("kbT", [512, T], BF16)
        self.vb = dscr("vb", [T, 512], BF16)
        self.qgT = dscr("qgT", [512, T], BF16)
        self.kgT = dscr("kgT", [128, T], BF16)
        self.vg = dscr("vg", [T, 128], BF16)
        self.qmT = dscr("qmT", [8, 96, T], BF16)
        self.kmT = dscr("kmT", [8, 96, T], BF16)
        self.vm = dscr("vm", [T, 512], BF16)
        self.gT = dscr("gT", [4096, T], BF16)
        self.oT = dscr("oT", [4, 512, T], BF16)
        self.h2T = dscr("h2T", [D, T], BF16)
        self.ubf = [nc.dram_tensor("ubf%d" % l, [4096, 4096], BF16, kind="Internal").ap() for l in range(DEPTH)]
        self.vbf = [nc.dram_tensor("vbf%d" % l, [16384, D], BF16, kind="Internal").ap() for l in range(DEPTH)]

    def sb(self, st, name, shape, dt):
        self.uid += 1
        return st.enter_context(self.nc.sbuf_tensor("%s_%d" % (name, self.uid), list(shape), dt))

    def ring(self, st, name, shape, dt, n):
        self.uid += 1
        return Ring([self.sb(st, "%s%d" % (name, i), shape, dt) for i in range(n)], "%s_%d" % (name, self.uid))

    def dma(self, q, out, in_, reads=(), writes=()):
        return self.S.op(q, lambda e: e.dma_start(out=out, in_=in_), reads=reads, writes=writes, dma=True)

    def build(self):
        nc = self.nc
        es = ExitStack()
        with es:
            S = self.S = Sched(nc, es)
            self.PS = [es.enter_context(nc.psum_tensor("ps%d" % i, [128, 512], F32)) for i in range(8)]
            sb, dma = self.sb, self.dma
            self.ident = sb(es, "ident", [128, 128], F32)
            self.ones_f = sb(es, "ones_f", [128, 128], F32)
            self.bd64_f = sb(es, "bd64_f", [128, 128], F32)
            self.ones_b = sb(es, "ones_b", [128, 128], BF16)
            self.r64 = sb(es, "r64", [128, 128], BF16)
            self.r96 = sb(es, "r96", [128, 128], BF16)
            self.r32 = sb(es, "r32", [128, 128], BF16)
            self.epsc = sb(es, "epsc", [128, 1], F32)
            self.iota_f = sb(es, "iota_f", [128, 128], F32)
            self.cv = sb(es, "cv", [128, 8, 2], F32)
            self.ngs = sb(es, "ngs", [128, 5, 8], F32)
            self.modT = sb(es, "modT", [128, 48, 2], F32)
            self.A1 = sb(es, "A1", [128, 8, 2], F32)
            dma("sp", self.ident[:], self.consts[0], writes=["ident"])
            dma("sp", self.ones_f[:], self.consts[1], writes=["ones_f"])
            dma("sp", self.bd64_f[:], self.consts[2], writes=["bd64_f"])
            dma("pool", self.ones_b[:], self.consts[1], writes=["ones_b"])
            dma("pool", self.r64[:], self.rmats[0], writes=["r64"])
            dma("pool", self.r96[:], self.rmats[1], writes=["r96"])
            dma("pool", self.r32[:], self.rmats[2], writes=["r32"])
            dma("sp", self.iota_f[:], self.iota_in, writes=["iota_f"])
            dma("sp", self.cv[:], self.cvec, writes=["cv"])
            dma("sp", self.ngs[:], self.ng, writes=["ngs"])
            S.op("dve", lambda e: e.memset(self.epsc[:], EPS), writes=["epsc"])
            S.op("act", lambda e: e.activation(out=self.cv[:], in_=self.cv[:], func=AF.Silu), reads=["cv"], writes=["cv"])
            if self.dbg in (None, "peer"):
                for l in range(DEPTH):
                    for i in range(8):
                        dma("pool", self.ubf[l][i * 512:(i + 1) * 512, :], self.UT[l, i * 512:(i + 1) * 512, :])
                    for i in range(16):
                        dma("pool", self.vbf[l][i * 1024:(i + 1) * 1024, :], self.VP[l, i * 1024:(i + 1) * 1024, :])
            xcur = self.xT_in
            stop = False
            for l in range(DEPTH):
                need_ctx = l < DEPTH - 1
                lam_init = 0.8 - 0.6 * math.exp(-0.3 * l)
                S.barrier()
                self.mod_phase(l)
                if self.dbg == "mod":
                    dm = nc.dram_tensor("dbg_mod", [128, 96], F32, kind="ExternalOutput").ap()
                    dma("sp", dm, self.modT[:].rearrange("p j s -> p (j s)"), reads=["modT"])
                    break
                S.barrier()
                self.inproj_phase(l, xcur)
                if self.dbg == "inproj":
                    break
                S.barrier()
                self.lru_phase(l)
                if self.dbg == "lru":
                    break
                S.barrier()
                self.attn_phase(l, need_ctx, lam_init)
                if self.dbg == "attn":
                    break
                S.barrier()
                self.merge_phase(l, need_ctx, xcur)
                xcur = self.xT_a
                if self.dbg == "merge":
                    break
                S.barrier()
                self.peer_phase(l, need_ctx, final=(l == DEPTH - 1))
                if self.dbg == "peer":
                    break
            S.barrier()
        return nc

    def mod_phase(self, l):
        S, PS, dma = self.S, self.PS, self.dma
        with ExitStack() as st:
            wm = self.ring(st, "wm", [128, 8, 512], F32, 2)
            bm = self.sb(st, "bm", [128, 48], F32)
            dma("sp", bm[:], self.b_mod[l], writes=["bm"])
            wsrc = self.w_mod[l].rearrange("(k p) c -> p k c", p=128)
            for g in range(12):
                wt, wk = wm.next()
                dma("sp", wt[:], wsrc[:, :, g * 512:(g + 1) * 512], writes=[wk])
                for jj in range(4):
                    j = g * 4 + jj
                    for k in range(8):
                        S.op("pe", lambda e, wt=wt, jj=jj, k=k, j=j: e.matmul(
                            PS[0][:, 2 * j:2 * j + 2], lhsT=wt[:, k, jj * 128:(jj + 1) * 128], rhs=self.cv[:, k, :],
                            start=(k == 0), stop=(k == 7)), reads=[wk, "cv"], writes=[("ps", 0)])
            for s in range(2):
                S.op("dve", lambda e, s=s: e.tensor_tensor(
                    out=self.modT[:, :, s], in0=PS[0][:, 0:96].rearrange("p (j s) -> p j s", s=2)[:, :, s],
                    in1=bm[:], op=ALU.add), reads=[("ps", 0), "bm"], writes=["modT"])
            S.barrier()

    def make_A(self, gidx, scale_idx):
        for s in range(2):
            self.S.op("dve", lambda e, s=s: e.scalar_tensor_tensor(
                out=self.A1[:, :, s], in0=self.modT[:, scale_idx * 8:(scale_idx + 1) * 8, s], scalar=1.0,
                in1=self.ngs[:, gidx, :], op0=ALU.add, op1=ALU.mult), reads=["modT", "ngs"], writes=["A1"])

    def adaln_tile(self, xt, xkey, n, acol_fn, bias_fn, hout_fn, hkey, sq, rs, tmpr, psb, sqkey="sq"):
        S, PS = self.S, self.PS
        for k in range(8):
            S.op("act", lambda e, k=k: e.activation(out=sq[:, k, :n], in_=xt[:, k, :n], func=AF.Square),
                 reads=[xkey], writes=[sqkey])
        for k in range(8):
            S.op("pe", lambda e, k=k: e.matmul(PS[psb][:, :n], lhsT=self.ones_f[:], rhs=sq[:, k, :n],
                                               start=(k == 0), stop=(k == 7)),
                 reads=[sqkey, "ones_f"], writes=[("ps", psb)])
        S.op("act", lambda e: e.activation(out=rs[:, :n], in_=PS[psb][:, :n], func=AF.Sqrt,
                                           bias=self.epsc[:], scale=1.0 / D),
             reads=[("ps", psb), "epsc"], writes=["rs"])
        S.op("dve", lambda e: e.reciprocal(out=rs[:, :n], in_=rs[:, :n]), reads=["rs"], writes=["rs"])
        for k in range(8):
            tt, tk = tmpr.next()
            S.op("dve", lambda e, k=k, tt=tt: e.scalar_tensor_tensor(
                out=tt[:, :n], in0=xt[:, k, :n], scalar=acol_fn(k), in1=rs[:, :n],
                op0=ALU.mult, op1=ALU.mult), reads=[xkey, "A1", "rs"], writes=[tk])
            b = bias_fn(k)
            if b is None:
                S.op("act", lambda e, k=k, tt=tt: e.copy(out=hout_fn(k), in_=tt[:, :n]), reads=[tk], writes=[hkey])
            else:
                S.op("act", lambda e, k=k, tt=tt, b=b: e.activation(
                    out=hout_fn(k), in_=tt[:, :n], func=AF.Identity, bias=b, scale=1.0),
                    reads=[tk, "modT"], writes=[hkey])

    def rope(self, P, n, xb, xkey, Rm, Rkey, Ct, St, tabkey, out_ap, outkey, bank, stg_f):
        S, PS = self.S, self.PS
        S.op("pe", lambda e: e.matmul(PS[bank][:P, :n], lhsT=Rm[:P, :P], rhs=xb[:P, :n], start=True, stop=True),
             reads=[xkey, Rkey], writes=[("ps", bank)])
        t1, t1k = stg_f.next()
        S.op("dve", lambda e: e.tensor_tensor(out=t1[:P, :n], in0=xb[:P, :n], in1=Ct[:P, :n], op=ALU.mult),
             reads=[xkey, tabkey], writes=[t1k])
        t2, t2k = stg_f.next()
        S.op("dve", lambda e: e.tensor_tensor(out=t2[:P, :n], in0=PS[bank][:P, :n], in1=St[:P, :n], op=ALU.mult),
             reads=[("ps", bank), tabkey], writes=[t2k])
        S.op("pool", lambda e: e.tensor_tensor(out=out_ap, in0=t1[:P, :n], in1=t2[:P, :n], op=ALU.add),
             reads=[t1k, t2k], writes=[outkey])

    def inproj_phase(self, l, xcur):
        S, PS, dma, nc = self.S, self.PS, self.dma, self.nc
        with ExitStack() as st:
            hT = self.sb(st, "hT", [128, 8, T], BF16)
            self.make_A(l, 1)
            with ExitStack() as st1:
                xr = self.ring(st1, "xr", [128, 8, 512], F32, 2)
                sq = self.sb(st1, "sq", [128, 8, 512], F32)
                rs = self.sb(st1, "rs", [128, 512], F32)
                tmpr = self.ring(st1, "tmpr", [128, 512], F32, 2)
                xsrc = xcur.rearrange("(k p) t -> p k t", p=128)
                for ti, (t0, n) in enumerate(TILES):
                    xt, xk = xr.next()
                    dma("sp", xt[:, :, :n], xsrc[:, :, t0:t0 + n], writes=[xk])
                    s = 1 if ti == 0 else 0
                    self.adaln_tile(xt, xk, n, lambda k, s=s: self.A1[:, k, s:s + 1],
                                    lambda k, s=s: self.modT[:, k, s:s + 1],
                                    lambda k, t0=t0, n=n: hT[:, k, t0:t0 + n], ("hT", ti), sq, rs, tmpr, 7)
                S.barrier()
            if self.dbg == "adaln":
                dh = nc.dram_tensor("dbg_h", [D, T], BF16, kind="ExternalOutput").ap()
                dma("sp", dh.rearrange("(k p) t -> p k t", p=128), hT[:])
                return
            wr = self.ring(st, "wsl", [128, 8, 512], BF16, 2)
            wmla = self.sb(st, "wmla", [128, 8, 672], BF16)
            wuq = self.sb(st, "wuq", [128, 3, 768], BF16)
            wkk = self.sb(st, "wkk", [128, 2, 8, 64], BF16)
            wkv = self.sb(st, "wkv", [128, 2, 8, 64], BF16)
            sg = self.sb(st, "sg", [128, 8], F32)
            stg_f = self.ring(st, "stgf", [128, 512], F32, 4)
            stg_b = self.ring(st, "stgb", [128, 512], BF16, 4)
            xbr = self.ring(st, "xbr", [128, 512], BF16, 3)
            tabC = self.ring(st, "tabC", [128, 512], F32, 2)
            tabS = self.ring(st, "tabS", [128, 512], F32, 2)
            sqn = self.sb(st, "sqn", [128, 512], F32)
            rsn = self.sb(st, "rsn", [128, 512], F32)
            zc = self.sb(st, "zc", [128, 5, 512], F32)
            sqm = self.sb(st, "sqm", [128, 5, 512], F32)
            cqn = self.sb(st, "cqn", [128, 3, 512], BF16)
            ckvn = self.sb(st, "ckvn", [128, 2, 512], BF16)
            wsrc = self.w_in[l].rearrange("(k p) c -> p k c", p=128)
            dma("sp", sg[:], self.smallg[l], writes=["sg"])
            bankc = [0]

            def nbank():
                b = bankc[0] % 3
                bankc[0] += 1
                return b

            def mm_chunk(bank, wt, wk, c0, m, ti):
                t0, n = TILES[ti]
                for k in range(8):
                    S.op("pe", lambda e, k=k: e.matmul(PS[bank][:m, :n], lhsT=wt[:, k, c0:c0 + m],
                                                       rhs=hT[:, k, t0:t0 + n], start=(k == 0), stop=(k == 7)),
                         reads=[wk], writes=[("ps", bank)])

            def load_tabs(src, P, ti, prow0=0):
                t0, n = TILES[ti]
                Ct, ck = tabC.next()
                St, sk = tabS.next()
                dma("sp", Ct[:P, :n], src[0, prow0:prow0 + P, t0 - NCTX:t0 - NCTX + n], writes=[ck])
                dma("sp", St[:P, :n], src[1, prow0:prow0 + P, t0 - NCTX:t0 - NCTX + n], writes=[sk])
                return Ct, St, ck, sk

            def store(dst_ap, src_ap, key):
                dma("sp", dst_ap, src_ap, reads=[key])

            def evac_act(bank, m, n, func, dt_ring, **kw):
                o, ok = dt_ring.next()
                S.op("act", lambda e: e.activation(out=o[:m, :n], in_=PS[bank][:m, :n], func=func, **kw),
                     reads=[("ps", bank)], writes=[ok])
                return o, ok

            def tokmajor(wt, wk, c0, w, dst, ti):
                t0, n = TILES[ti]
                for sub in range(n // 128):
                    bank = nbank()
                    for k in range(8):
                        S.op("pe", lambda e, k=k, sub=sub: e.matmul(
                            PS[bank][:, :w], lhsT=hT[:, k, t0 + sub * 128:t0 + (sub + 1) * 128],
                            rhs=wt[:, k, c0:c0 + w], start=(k == 0), stop=(k == 7)),
                            reads=[wk], writes=[("ps", bank)])
                    o, ok = evac_act(bank, 128, w, AF.Copy, stg_b)
                    store(dst[t0 + sub * 128:t0 + (sub + 1) * 128, :], o[:, :w], ok)

            def rope_store(P, n, ti, xb, xk, Rm, Rkey, tabs, dst_ap, bank=3):
                if ti == 0:
                    store(dst_ap, xb[:P, :n], xk)
                else:
                    Ct, St, ck, sk = tabs
                    o, ok = stg_b.next()
                    S.op("pe", lambda e: e.matmul(PS[bank][:P, :n], lhsT=Rm[:P, :P], rhs=xb[:P, :n],
                                                  start=True, stop=True),
                         reads=[xk, Rkey], writes=[("ps", bank)])
                    t1, t1k = stg_f.next()
                    S.op("dve", lambda e: e.tensor_tensor(out=t1[:P, :n], in0=xb[:P, :n], in1=Ct[:P, :n],
                                                          op=ALU.mult), reads=[xk, ck], writes=[t1k])
                    t2, t2k = stg_f.next()
                    S.op("dve", lambda e: e.tensor_tensor(out=t2[:P, :n], in0=PS[bank][:P, :n], in1=St[:P, :n],
                                                          op=ALU.mult), reads=[("ps", bank), sk], writes=[t2k])
                    S.op("pool", lambda e: e.tensor_tensor(out=o[:P, :n], in0=t1[:P, :n], in1=t2[:P, :n],
                                                           op=ALU.add), reads=[t1k, t2k], writes=[ok])
                    store(dst_ap, o[:P, :n], ok)

            def headnorm(bank, n, gcol, inv_dim, ones_mat, ones_key):
                S.op("act", lambda e: e.activation(out=sqn[:, :n], in_=PS[bank][:, :n], func=AF.Square),
                     reads=[("ps", bank)], writes=["sqn"])
                S.op("pe", lambda e: e.matmul(PS[4][:, :n], lhsT=ones_mat[:], rhs=sqn[:, :n], start=True, stop=True),
                     reads=["sqn", ones_key], writes=[("ps", 4)])
                S.op("act", lambda e: e.activation(out=rsn[:, :n], in_=PS[4][:, :n], func=AF.Sqrt,
                                                   bias=self.epsc[:], scale=inv_dim),
                     reads=[("ps", 4), "epsc"], writes=["rsn"])
                S.op("dve", lambda e: e.reciprocal(out=rsn[:, :n], in_=rsn[:, :n]), reads=["rsn"], writes=["rsn"])
                xb, xk = xbr.next()
                S.op("dve", lambda e: e.scalar_tensor_tensor(out=xb[:, :n], in0=PS[bank][:, :n], scalar=gcol,
                                                             in1=rsn[:, :n], op0=ALU.mult, op1=ALU.mult),
                     reads=[("ps", bank), "rsn", "sg"], writes=[xk])
                return xb, xk

            groups = [("xa", 0, 512), ("ya", 512, 512), ("qb", 1024, 512), ("kb", 1536, 512), ("vb", 2048, 512),
                      ("qg", 2560, 512), ("kvg", 3072, 256)] + [("zg%d" % i, 4000 + 512 * i, 512) for i in range(8)]
            for gname, c0, w in groups:
                wt, wk = wr.next()
                dma("pool", wt[:, :, :w], wsrc[:, :, c0:c0 + w], writes=[wk])
                for ti, (t0, n) in enumerate(TILES):
                    if gname == "vb":
                        tokmajor(wt, wk, 0, 512, self.vb, ti)
                        continue
                    if gname == "kvg":
                        tokmajor(wt, wk, 128, 128, self.vg, ti)
                    tabs = None
                    if gname in ("qb", "kb", "qg", "kvg") and ti > 0:
                        tabs = load_tabs(self.rope64, 128, ti)
                    nch = 1 if gname == "kvg" else w // 128
                    for ch in range(nch):
                        bank = nbank()
                        mm_chunk(bank, wt, wk, ch * 128, 128, ti)
                        if gname == "xa":
                            o, ok = evac_act(bank, 128, n, AF.Copy, stg_f)
                            store(self.xaT[ch * 128:(ch + 1) * 128, t0:t0 + n], o[:, :n], ok)
                        elif gname == "ya":
                            o, ok = evac_act(bank, 128, n, AF.Gelu_apprx_tanh, stg_b)
                            store(self.yaT[ch * 128:(ch + 1) * 128, t0:t0 + n], o[:, :n], ok)
                        elif gname.startswith("zg"):
                            o, ok = evac_act(bank, 128, n, AF.Sigmoid, stg_b)
                            r0 = (c0 - 4000) + ch * 128
                            store(self.gT[r0:r0 + 128, t0:t0 + n], o[:, :n], ok)
                        elif gname in ("qb", "kb"):
                            xb, xk = evac_act(bank, 128, n, AF.Copy, xbr)
                            dst = self.qbT if gname == "qb" else self.kbT
                            rope_store(128, n, ti, xb, xk, self.r64, "r64", tabs,
                                       dst[ch * 128:(ch + 1) * 128, t0:t0 + n])
                        elif gname == "qg":
                            xb, xk = headnorm(bank, n, sg[:, 1:2], 1.0 / 64, self.bd64_f, "bd64_f")
                            rope_store(128, n, ti, xb, xk, self.r64, "r64", tabs,
                                       self.qgT[ch * 128:(ch + 1) * 128, t0:t0 + n])
                        elif gname == "kvg":
                            xb, xk = headnorm(bank, n, sg[:, 2:3], 1.0 / 64, self.bd64_f, "bd64_f")
                            rope_store(128, n, ti, xb, xk, self.r64, "r64", tabs, self.kgT[:, t0:t0 + n])
            dma("pool", wmla[:], wsrc[:, :, 3328:4000], writes=["wmla"])
            dma("pool", wuq[:], self.w_uq[l].rearrange("(j p) c -> p j c", p=128), writes=["wuq"])
            ukv5 = self.w_ukv[l].rearrange("(j p) (h t d) -> p j h t d", p=128, h=8, t=2)
            for j in range(2):
                dma("pool", wkk[:, j], ukv5[:, j, :, 0, :], writes=["wkk"])
                dma("pool", wkv[:, j], ukv5[:, j, :, 1, :], writes=["wkv"])
            for ti, (t0, n) in enumerate(TILES):
                tq = tk_ = None
                if ti > 0:
                    tq = load_tabs(self.ropem, 96, ti)
                    tk_ = load_tabs(self.ropem, 32, ti, prow0=64)
                for j in range(5):
                    bank = nbank()
                    mm_chunk(bank, wmla, "wmla", j * 128, 128, ti)
                    S.op("act", lambda e, j=j: e.copy(out=zc[:, j, :n], in_=PS[bank][:, :n]),
                         reads=[("ps", bank)], writes=["zc"])
                    S.op("act", lambda e, j=j: e.activation(out=sqm[:, j, :n], in_=PS[bank][:, :n], func=AF.Square),
                         reads=[("ps", bank)], writes=["sqm"])
                for (j0, nj, inv, gc0, dstn, dkey) in ((0, 3, 1.0 / 384, 3, cqn, "cqn"), (3, 2, 1.0 / 256, 6, ckvn, "ckvn")):
                    for jj in range(nj):
                        S.op("pe", lambda e, jj=jj: e.matmul(PS[4][:, :n], lhsT=self.ones_f[:], rhs=sqm[:, j0 + jj, :n],
                                                             start=(jj == 0), stop=(jj == nj - 1)),
                             reads=["sqm", "ones_f"], writes=[("ps", 4)])
                    S.op("act", lambda e: e.activation(out=rsn[:, :n], in_=PS[4][:, :n], func=AF.Sqrt,
                                                       bias=self.epsc[:], scale=inv),
                         reads=[("ps", 4), "epsc"], writes=["rsn"])
                    S.op("dve", lambda e: e.reciprocal(out=rsn[:, :n], in_=rsn[:, :n]), reads=["rsn"], writes=["rsn"])
                    for jj in range(nj):
                        S.op("dve", lambda e, jj=jj: e.scalar_tensor_tensor(
                            out=dstn[:, jj, :n], in0=zc[:, j0 + jj, :n], scalar=sg[:, gc0 + jj:gc0 + jj + 1],
                            in1=rsn[:, :n], op0=ALU.mult, op1=ALU.mult), reads=["zc", "rsn", "sg"], writes=[dkey])
                for h in range(8):
                    bank = nbank()
                    for j in range(3):
                        S.op("pe", lambda e, j=j, h=h: e.matmul(PS[bank][:96, :n], lhsT=wuq[:, j, h * 96:(h + 1) * 96],
                                                                rhs=cqn[:, j, :n], start=(j == 0), stop=(j == 2)),
                             reads=["wuq", "cqn"], writes=[("ps", bank)])
                    xb, xk = evac_act(bank, 96, n, AF.Copy, xbr)
                    rope_store(96, n, ti, xb, xk, self.r96, "r96", tq, self.qmT[h, :, t0:t0 + n])
                for h in range(8):
                    bank = nbank()
                    for j in range(2):
                        S.op("pe", lambda e, j=j, h=h: e.matmul(PS[bank][:64, :n], lhsT=wkk[:, j, h, :],
                                                                rhs=ckvn[:, j, :n], start=(j == 0), stop=(j == 1)),
                             reads=["wkk", "ckvn"], writes=[("ps", bank)])
                    o, ok = evac_act(bank, 64, n, AF.Copy, stg_b)
                    store(self.kmT[h, 0:64, t0:t0 + n], o[:64, :n], ok)
                for sub in range(n // 128):
                    bank = nbank()
                    for j in range(2):
                        S.op("pe", lambda e, j=j, sub=sub: e.matmul(
                            PS[bank][:, :512], lhsT=ckvn[:, j, sub * 128:(sub + 1) * 128],
                            rhs=wkv[:, j].rearrange("p h d -> p (h d)"), start=(j == 0), stop=(j == 1)),
                            reads=["wkv", "ckvn"], writes=[("ps", bank)])
                    o, ok = evac_act(bank, 128, 512, AF.Copy, stg_b)
                    store(self.vm[t0 + sub * 128:t0 + (sub + 1) * 128, :], o[:, :], ok)
                bank = nbank()
                mm_chunk(bank, wmla, "wmla", 640, 32, ti)
                xb, xk = evac_act(bank, 32, n, AF.Copy, xbr)
                if ti == 0:
                    for h in range(8):
                        store(self.kmT[h, 64:96, t0:t0 + n], xb[:32, :n], xk)
                else:
                    Ct, St, ck, sk = tk_
                    o, ok = stg_b.next()
                    S.op("pe", lambda e: e.matmul(PS[3][:32, :n], lhsT=self.r32[:32, :32], rhs=xb[:32, :n],
                                                  start=True, stop=True), reads=[xk, "r32"], writes=[("ps", 3)])
                    t1, t1k = stg_f.next()
                    S.op("dve", lambda e: e.tensor_tensor(out=t1[:32, :n], in0=xb[:32, :n], in1=Ct[:32, :n],
                                                          op=ALU.mult), reads=[xk, ck], writes=[t1k])
                    t2, t2k = stg_f.next()
                    S.op("dve", lambda e: e.tensor_tensor(out=t2[:32, :n], in0=PS[3][:32, :n], in1=St[:32, :n],
                                                          op=ALU.mult), reads=[("ps", 3), sk], writes=[t2k])
                    S.op("pool", lambda e: e.tensor_tensor(out=o[:32, :n], in0=t1[:32, :n], in1=t2[:32, :n],
                                                           op=ALU.add), reads=[t1k, t2k], writes=[ok])
                    for h in range(8):
                        store(self.kmT[h, 64:96, t0:t0 + n], o[:32, :n], ok)
            S.barrier()

    def lru_phase(self, l):
        S, PS, dma = self.S, self.PS, self.dma
        with ExitStack() as st:
            xa = self.sb(st, "xa", [128, T], F32)
            u = self.sb(st, "u", [128, T], F32)
            ub = self.sb(st, "ub", [128, T], BF16)
            gy = self.sb(st, "gy", [128, T], BF16)
            at = self.sb(st, "at", [128, T], F32)
            bt = self.sb(st, "bt", [128, T], F32)
            tm = self.sb(st, "tm", [128, T], F32)
            hf = self.sb(st, "hf", [128, T], F32)
            hr = self.sb(st, "hr", [128, T], F32)
            ho = self.sb(st, "ho", [128, T], BF16)
            bdw = self.ring(st, "bdw", [128, 128], BF16, 4)
            cw = self.sb(st, "cw", [128, 4, 5], F32)
            lv = self.sb(st, "lv", [128, 2, 3, 4], F32)
            negk = self.sb(st, "negk", [128, 2, 4], F32)
            dma("sp", cw[:], self.convw[l], writes=["cw"])
            dma("sp", lv[:], self.lru_vec[l], writes=["lv"])
            for d in range(2):
                S.op("act", lambda e, d=d: e.activation(out=negk[:, d, :], in_=lv[:, d, 2, :], func=AF.Exp, scale=-1.0),
                     reads=["lv"], writes=["negk"])
                S.op("act", lambda e, d=d: e.activation(out=negk[:, d, :], in_=negk[:, d, :], func=AF.Ln, bias=1.0),
                     reads=["negk"], writes=["negk"])
                S.op("dve", lambda e, d=d: e.tensor_scalar(out=negk[:, d, :], in0=negk[:, d, :], scalar1=-8.0,
                                                           scalar2=None, op0=ALU.mult), reads=["negk"], writes=["negk"])
            segs = [(0, NCTX), (NCTX, T)]
            for cc in range(4):
                dma("sp", xa[:], self.xaT[cc * 128:(cc + 1) * 128, :], writes=["xa"])
                dma("sp", gy[:], self.yaT[cc * 128:(cc + 1) * 128, :], writes=["gy"])
                S.op("act", lambda e: e.activation(out=u[:], in_=xa[:], func=AF.Identity, scale=cw[:, cc, 1:2],
                                                   bias=cw[:, cc, 4:5]), reads=["xa", "cw"], writes=["u"])
                for (s0, e0) in segs:
                    for (tap, osl, isl) in ((0, (s0 + 1, e0), (s0, e0 - 1)), (2, (s0, e0 - 1), (s0 + 1, e0)),
                                            (3, (s0, e0 - 2), (s0 + 2, e0))):
                        S.op("dve", lambda e, tap=tap, osl=osl, isl=isl: e.scalar_tensor_tensor(
                            out=u[:, osl[0]:osl[1]], in0=xa[:, isl[0]:isl[1]], scalar=cw[:, cc, tap:tap + 1],
                            in1=u[:, osl[0]:osl[1]], op0=ALU.mult, op1=ALU.add), reads=["xa", "u", "cw"], writes=["u"])
                S.op("pool", lambda e: e.tensor_copy(out=ub[:], in_=u[:]), reads=["u"], writes=["ub"])
                for d in range(2):
                    wts = []
                    for kind in range(2):
                        wt, wk = bdw.next()
                        dma("pool", wt[:], self.lru_bd[l, d, kind, cc], writes=[wk])
                        wts.append((wt, wk))
                    for ti, (t0, n) in enumerate(TILES):
                        for kind in range(2):
                            bank = (ti * 2 + kind) % 4
                            wt, wk = wts[kind]
                            S.op("pe", lambda e, wt=wt: e.matmul(PS[bank][:, :n], lhsT=wt[:], rhs=ub[:, t0:t0 + n],
                                                                 start=True, stop=True),
                                 reads=[wk, "ub"], writes=[("ps", bank)])
                            dst, dk = (at, "at") if kind == 0 else (bt, "bt")
                            S.op("act", lambda e, dst=dst, kind=kind: e.activation(
                                out=dst[:, t0:t0 + n], in_=PS[bank][:, :n], func=AF.Sigmoid,
                                bias=lv[:, d, kind, cc:cc + 1], scale=1.0), reads=[("ps", bank), "lv"], writes=[dk])
                    S.op("act", lambda e: e.activation(out=at[:], in_=at[:], func=AF.Exp, scale=negk[:, d, cc:cc + 1]),
                         reads=["at", "negk"], writes=["at"])
                    S.op("pool", lambda e: e.tensor_tensor(out=tm[:], in0=at[:], in1=at[:], op=ALU.mult),
                         reads=["at"], writes=["tm"])
                    S.op("act", lambda e: e.activation(out=tm[:], in_=tm[:], func=AF.Sqrt, scale=-1.0, bias=1.0),
                         reads=["tm"], writes=["tm"])
                    S.op("dve", lambda e: e.tensor_tensor(out=bt[:], in0=bt[:], in1=u[:], op=ALU.mult),
                         reads=["bt", "u"], writes=["bt"])
                    S.op("dve", lambda e: e.tensor_tensor(out=bt[:], in0=bt[:], in1=tm[:], op=ALU.mult),
                         reads=["bt", "tm"], writes=["bt"])
                    if d == 0:
                        pieces = [(0, 1088), (1088, 2176), (2176, 3264), (3264, T)]
                        for pi, (p0, p1) in enumerate(pieces):
                            init = 0.0 if pi == 0 else hf[:, p0 - 1:p0]
                            S.op("dve", lambda e, p0=p0, p1=p1, init=init: e.tensor_tensor_scan(
                                out=hf[:, p0:p1], data0=at[:, p0:p1], data1=bt[:, p0:p1], initial=init,
                                op0=ALU.mult, op1=ALU.add), reads=["at", "bt", "hf"], writes=["hf"])
                    else:
                        S.op("dve", lambda e: e.tensor_tensor_scan(
                            out=hr[:, 0:NCTX][:, ::-1], data0=at[:, 0:NCTX][:, ::-1], data1=bt[:, 0:NCTX][:, ::-1],
                            initial=0.0, op0=ALU.mult, op1=ALU.add), reads=["at", "bt"], writes=["hr"])
                        pieces = [(3328, T), (2304, 3328), (1280, 2304), (NCTX, 1280)]
                        for pi, (p0, p1) in enumerate(pieces):
                            init = hr[:, 0:1] if pi == 0 else hr[:, p1:p1 + 1]
                            S.op("dve", lambda e, p0=p0, p1=p1, init=init: e.tensor_tensor_scan(
                                out=hr[:, p0:p1][:, ::-1], data0=at[:, p0:p1][:, ::-1], data1=bt[:, p0:p1][:, ::-1],
                                initial=init, op0=ALU.mult, op1=ALU.add), reads=["at", "bt", "hr"], writes=["hr"])
                S.op("pool", lambda e: e.tensor_tensor(out=hf[:], in0=hf[:], in1=hr[:], op=ALU.add),
                     reads=["hf", "hr"], writes=["hf"])
                S.op("dve", lambda e: e.tensor_tensor(out=ho[:], in0=hf[:], in1=gy[:], op=ALU.mult),
                     reads=["hf", "gy"], writes=["ho"])
                dma("sp", self.oT[0, cc * 128:(cc + 1) * 128, :], ho[:], reads=["ho"])
            S.barrier()

    def attn_phase(self, l, need_ctx, lam_init):
        S, PS, dma = self.S, self.PS, self.dma
        with ExitStack() as st:
            kT = self.ring(st, "kT", [128, T], BF16, 2)
            vt = self.ring(st, "vt", [128, NKB, 128], BF16, 2)
            qt = self.ring(st, "qt", [128, 512], BF16, 2)
            pT = self.ring(st, "pT", [128, 512], BF16, 4)
            rec = self.ring(st, "rec", [128, 512], F32, 2)
            of = self.ring(st, "of", [128, 512], F32, 3)
            ob = self.ring(st, "ob", [128, 512], BF16, 2)
            sqd = self.sb(st, "sqd", [128, 512], F32)
            rsd = self.sb(st, "rsd", [128, 512], F32)
            sg = self.sb(st, "sga", [128, 8], F32)
            dl = self.sb(st, "dl", [128, 4, 64], F32)
            lam = self.sb(st, "lam", [128, 4], F32)
            dma("sp", sg[:], self.smallg[l], writes=["sga"])
            dma("sp", dl[:], self.diff_lam[l], writes=["dl"])
            for i in range(2):
                S.op("dve", lambda e, i=i: e.tensor_tensor(out=dl[:, 2 * i, :], in0=dl[:, 2 * i, :],
                                                           in1=dl[:, 2 * i + 1, :], op=ALU.mult), reads=["dl"], writes=["dl"])
                S.op("dve", lambda e, i=i: e.tensor_reduce(out=lam[:, i:i + 1], in_=dl[:, 2 * i, :], axis=AX.X, op=ALU.add),
                     reads=["dl"], writes=["lam"])
            S.op("act", lambda e: e.activation(out=lam[:, 0:2], in_=lam[:, 0:2], func=AF.Exp), reads=["lam"], writes=["lam"])
            S.op("dve", lambda e: e.tensor_tensor(out=lam[:, 2:3], in0=lam[:, 1:2], in1=lam[:, 0:1], op=ALU.subtract),
                 reads=["lam"], writes=["lam"])
            S.op("dve", lambda e: e.tensor_scalar(out=lam[:, 2:3], in0=lam[:, 2:3], scalar1=-lam_init, scalar2=None,
                                                  op0=ALU.add), reads=["lam"], writes=["lam"])
            S.op("dve", lambda e: e.tensor_scalar(out=lam[:, 3:4], in0=sg[:, 0:1], scalar1=1.0 - lam_init, scalar2=None,
                                                  op0=ALU.mult), reads=["sga", "lam"], writes=["lam"])

            qgroups = ([(0, 256, 2)] if need_ctx else []) + [(256 + 512 * i, 512, NKB) for i in range(8)]

            def run_job(kt, kk, vtile, vk, dv, qsrc_fn, variants, scale, finish):
                for (q0, n, nkb) in qgroups:
                    q, qk = qt.next()
                    qsrc_fn(q, qk, q0, n)
                    nv = len(variants)
                    sbanks = [0, 1, 2, 7]
                    seq = [(kb, v) for kb in range(nkb) for v in range(nv)]

                    def emit_qk(idx):
                        kb, v = seq[idx]
                        p0, pn = variants[v]
                        bank = sbanks[idx % 4]
                        S.op("pe", lambda e: e.matmul(PS[bank][:, :n], lhsT=kt[p0:p0 + pn, kb * 128:(kb + 1) * 128],
                                                      rhs=q[p0:p0 + pn, :n], start=True, stop=True),
                             reads=[kk, qk], writes=[("ps", bank)])
                        return bank

                    bank_of = {0: emit_qk(0)}
                    if len(seq) > 1:
                        bank_of[1] = emit_qk(1)
                    for idx in range(len(seq)):
                        kb, v = seq[idx]
                        bank = bank_of.pop(idx)
                        p, pk = pT.next()
                        S.op("act", lambda e, p=p, bank=bank: e.activation(out=p[:, :n], in_=PS[bank][:, :n],
                                                                           func=AF.Exp, scale=scale),
                             reads=[("ps", bank)], writes=[pk])
                        if idx + 2 < len(seq):
                            bank_of[idx + 2] = emit_qk(idx + 2)
                        S.op("pe", lambda e, p=p, kb=kb, v=v: e.matmul(PS[3 + v][:, :n], lhsT=vtile[:, kb, :],
                                                                      rhs=p[:, :n], start=(kb == 0), stop=(kb == nkb - 1)),
                             reads=[pk, vk], writes=[("ps", 3 + v)])
                        if dv == 128:
                            S.op("pe", lambda e, p=p, kb=kb, v=v: e.matmul(PS[5 + v][:, :n], lhsT=self.ones_b[:, :],
                                                                          rhs=p[:, :n], start=(kb == 0), stop=(kb == nkb - 1)),
                                 reads=[pk, "ones_b"], writes=[("ps", 5 + v)])
                    finish(q0, n)

            def load_v(src_ap, dv):
                vtile, vk = vt.next()
                if dv == 64:
                    S.op("pool", lambda e: e.memset(vtile[:, :, 64:128], 1.0), writes=[vk])
                dma("sp", vtile[:, :, :dv], src_ap.rearrange("(kb p) d -> p kb d", p=128), reads=[vk], writes=[vk])
                return vtile, vk

            def normalized(v, dv, n):
                r, rk = rec.next()
                S.op("dve", lambda e: e.reciprocal(out=r[:dv, :n], in_=PS[5 + v][:dv, :n]), reads=[("ps", 5 + v)], writes=[rk])
                o, ok = of.next()
                S.op("dve", lambda e: e.tensor_tensor(out=o[:dv, :n], in0=PS[3 + v][:dv, :n], in1=r[:dv, :n], op=ALU.mult),
                     reads=[("ps", 3 + v), rk], writes=[ok])
                return o, ok

            for h in range(4):
                kt, kk = kT.next()
                dma("sp", kt[:], self.kbT[h * 128:(h + 1) * 128, :], writes=[kk])
                vtile, vk = load_v(self.vb[:, h * 128:(h + 1) * 128], 128)

                def qsrc(q, qk, q0, n, h=h):
                    dma("sp", q[:, :n], self.qbT[h * 128:(h + 1) * 128, q0:q0 + n], writes=[qk])

                def finish(q0, n, h=h):
                    o1, o1k = normalized(0, 128, n)
                    o2, o2k = normalized(1, 128, n)
                    od, odk = of.next()
                    S.op("dve", lambda e: e.scalar_tensor_tensor(out=od[:, :n], in0=o2[:, :n], scalar=lam[:, 2:3],
                                                                 in1=o1[:, :n], op0=ALU.mult, op1=ALU.add),
                         reads=[o1k, o2k, "lam"], writes=[odk])
                    S.op("act", lambda e: e.activation(out=sqd[:, :n], in_=od[:, :n], func=AF.Square),
                         reads=[odk], writes=["sqd"])
                    S.op("pe", lambda e: e.matmul(PS[7][:, :n], lhsT=self.ones_f[:], rhs=sqd[:, :n], start=True, stop=True),
                         reads=["sqd", "ones_f"], writes=[("ps", 7)])
                    S.op("act", lambda e: e.activation(out=rsd[:, :n], in_=PS[7][:, :n], func=AF.Sqrt,
                                                       bias=self.epsc[:], scale=1.0 / 128),
                         reads=[("ps", 7), "epsc"], writes=["rsd"])
                    S.op("dve", lambda e: e.reciprocal(out=rsd[:, :n], in_=rsd[:, :n]), reads=["rsd"], writes=["rsd"])
                    o, ok = ob.next()
                    S.op("dve", lambda e: e.scalar_tensor_tensor(out=o[:, :n], in0=od[:, :n], scalar=lam[:, 3:4],
                                                                 in1=rsd[:, :n], op0=ALU.mult, op1=ALU.mult),
                         reads=[odk, "rsd", "lam"], writes=[ok])
                    dma("sp", self.oT[1, h * 128:(h + 1) * 128, q0:q0 + n], o[:, :n], reads=[ok])

                run_job(kt, kk, vtile, vk, 128, qsrc, [(0, 64), (64, 64)], 0.125, finish)

            def simple_finish(branch, h):
                def fin(q0, n):
                    r, rk = rec.next()
                    S.op("dve", lambda e: e.reciprocal(out=r[64:128, :n], in_=PS[3][64:128, :n]), reads=[("ps", 3)], writes=[rk])
                    o, ok = ob.next()
                    S.op("dve", lambda e: e.tensor_tensor(out=o[:64, :n], in0=PS[3][0:64, :n], in1=r[64:128, :n], op=ALU.mult),
                         reads=[("ps", 3), rk], writes=[ok])
                    dma("sp", self.oT[branch, h * 64:(h + 1) * 64, q0:q0 + n], o[:64, :n], reads=[ok])
                return fin

            for h in range(8):
                g = h // 4
                kt, kk = kT.next()
                dma("sp", kt[0:64, :], self.kgT[g * 64:(g + 1) * 64, :], writes=[kk])
                vtile, vk = load_v(self.vg[:, g * 64:(g + 1) * 64], 64)

                def qsrc(q, qk, q0, n, h=h):
                    dma("sp", q[0:64, :n], self.qgT[h * 64:(h + 1) * 64, q0:q0 + n], writes=[qk])

                run_job(kt, kk, vtile, vk, 64, qsrc, [(0, 64)], 0.125, simple_finish(2, h))
            for h in range(8):
                kt, kk = kT.next()
                dma("sp", kt[0:96, :], self.kmT[h], writes=[kk])
                vtile, vk = load_v(self.vm[:, h * 64:(h + 1) * 64], 64)

                def qsrc(q, qk, q0, n, h=h):
                    dma("sp", q[0:96, :n], self.qmT[h, :, q0:q0 + n], writes=[qk])

                run_job(kt, kk, vtile, vk, 64, qsrc, [(0, 96)], 96.0 ** -0.5, simple_finish(3, h))
            S.barrier()

    def merge_phase(self, l, need_ctx, xcur):
        S, PS, dma = self.S, self.PS, self.dma
        with ExitStack() as st:
            wb = self.sb(st, "wb", [128, 16, 1024], BF16)
            wo = self.sb(st, "wo", [128, 8, 1024], BF16)
            otr = self.ring(st, "otr", [128, 16, 512], BF16, 2)
            gtr = self.ring(st, "gtr", [128, 512], BF16, 4)
            macc = self.sb(st, "macc", [128, 8, 512], F32)
            mtmp = self.ring(st, "mtmp", [128, 512], F32, 2)
            mb = self.sb(st, "mb", [128, 8, 512], BF16)
            xr = self.ring(st, "xr2", [128, 8, 512], F32, 2)
            sq = self.sb(st, "sq2", [128, 8, 512], F32)
            rs = self.sb(st, "rs2", [128, 512], F32)
            tmpr = self.ring(st, "tmpr2", [128, 512], F32, 2)
            hst = self.ring(st, "hst", [128, 8, 512], BF16, 2)
            for k in range(4):
                dma("pool", wb[:, k * 4:(k + 1) * 4, :], self.w_br[l, k].rearrange("(c p) d -> p c d", p=128), writes=["wb"])
            dma("pool", wo[:], self.w_out[l].rearrange("(k p) d -> p k d", p=128), writes=["wo"])
            self.make_A(2 + l, 4)
            osrc = self.oT.rearrange("k (c p) t -> p k c t", p=128)
            xsrc = xcur.rearrange("(k p) t -> p k t", p=128)
            xdst = self.xT_a.rearrange("(k p) t -> p k t", p=128)
            hdst = self.h2T.rearrange("(k p) t -> p k t", p=128)
            tiles = list(enumerate(TILES)) if need_ctx else list(enumerate(TILES))[1:]
            bc = 0
            for ti, (t0, n) in tiles:
                s = 1 if ti == 0 else 0
                ot, otk = otr.next()
                for k in range(4):
                    dma("sp", ot[:, k * 4:(k + 1) * 4, :n], osrc[:, k, :, t0:t0 + n], writes=[otk])
                xt, xk = xr.next()
                dma("sp", xt[:, :, :n], xsrc[:, :, t0:t0 + n], writes=[xk])
                for j in range(8):
                    for k in range(4):
                        gt, gk = gtr.next()
                        r0 = k * 1024 + j * 128
                        dma("sp", gt[:, :n], self.gT[r0:r0 + 128, t0:t0 + n], writes=[gk])
                        bank = bc % 3
                        bc += 1
                        for c in range(4):
                            S.op("pe", lambda e, c=c, k=k, j=j: e.matmul(
                                PS[bank][:, :n], lhsT=wb[:, k * 4 + c, j * 128:(j + 1) * 128], rhs=ot[:, k * 4 + c, :n],
                                start=(c == 0), stop=(c == 3)), reads=["wb", otk], writes=[("ps", bank)])
                        if k == 0:
                            S.op("dve", lambda e, j=j, gt=gt: e.tensor_tensor(out=macc[:, j, :n], in0=PS[bank][:, :n],
                                                                              in1=gt[:, :n], op=ALU.mult),
                                 reads=[("ps", bank), gk], writes=[("macc", j)])
                        else:
                            mt, mk = mtmp.next()
                            S.op("dve", lambda e, gt=gt, mt=mt: e.tensor_tensor(out=mt[:, :n], in0=PS[bank][:, :n],
                                                                                in1=gt[:, :n], op=ALU.mult),
                                 reads=[("ps", bank), gk], writes=[mk])
                            S.op("pool", lambda e, j=j, mt=mt: e.tensor_tensor(out=macc[:, j, :n], in0=macc[:, j, :n],
                                                                               in1=mt[:, :n], op=ALU.add),
                                 reads=[mk, ("macc", j)], writes=[("macc", j)])
                    S.op("act", lambda e, j=j: e.copy(out=mb[:, j, :n], in_=macc[:, j, :n]),
                         reads=[("macc", j)], writes=[("mb", j)])
                for jo in range(8):
                    bank = 3 + (jo % 3)
                    for j in range(8):
                        S.op("pe", lambda e, j=j, jo=jo: e.matmul(PS[bank][:, :n], lhsT=wo[:, j, jo * 128:(jo + 1) * 128],
                                                                  rhs=mb[:, j, :n], start=(j == 0), stop=(j == 7)),
                             reads=["wo"] + [("mb", jj) for jj in range(8)], writes=[("ps", bank)])
                    S.op("dve", lambda e, jo=jo: e.scalar_tensor_tensor(
                        out=xt[:, jo, :n], in0=PS[bank][:, :n], scalar=self.modT[:, 16 + jo, s:s + 1],
                        in1=xt[:, jo, :n], op0=ALU.mult, op1=ALU.add), reads=[("ps", bank), xk, "modT"], writes=[xk])
                dma("sp", xdst[:, :, t0:t0 + n], xt[:, :, :n], reads=[xk])
                hs, hk = hst.next()
                self.adaln_tile(xt, xk, n, lambda k, s=s: self.A1[:, k, s:s + 1],
                                lambda k, s=s: self.modT[:, 24 + k, s:s + 1],
                                lambda k, hs=hs, n=n: hs[:, k, :n], hk, sq, rs, tmpr, 7)
                dma("sp", hdst[:, :, t0:t0 + n], hs[:, :, :n], reads=[hk])
            S.barrier()

    def peer_phase(self, l, need_ctx, final):
        S, PS, dma = self.S, self.PS, self.dma
        with ExitStack() as st:
            sb = lambda name, shape, dt: self.sb(st, name, shape, dt)
            wq = sb("wq", [128, 8, 1024], BF16)
            kbd = sb("kbd", [128, 8, 256], F32)
            h2r = self.ring(st, "h2r", [128, 8, PT], BF16, 2)
            qTh = sb("qTh", [128, 8, PT], F32)
            sc = sb("sc", [128, 8, 256], F32)
            v8 = sb("v8", [128, 8, 2, 16], F32)
            i8u = sb("i8u", [128, 8, 2, 16], U32)
            i8f = sb("i8f", [128, 8, 2, 16], F32)
            cand = sb("cand", [128, 8, 256], F32)
            cand2 = sb("cand2", [128, 8, 256], F32)
            wk1 = cand2
            best = sb("best", [128, 8, 16], F32)
            posu = sb("posu", [128, 8, 16], U32)
            posf = sb("posf", [128, 8, 16], F32)
            af = sb("af", [128, 8, 16], F32)
            bf = sb("bf", [128, 8, 16], F32)
            oh = sb("oh", [128, 128, 16], F32)
            iw = sb("iw", [128, 128], F32)
            jw = sb("jw", [128, 128], F32)
            gw = sb("gw", [128, 8, 16], F32)
            zs = sb("zs", [128, 8], F32)
            iT = sb("iT", [128, PT], F32)
            jT = sb("jT", [128, PT], F32)
            gTt = sb("gTt", [128, PT], F32)
            lhr = self.ring(st, "lhr", [128, 128], BF16, 4)
            rhr = self.ring(st, "rhr", [128, 128], BF16, 4)
            GT = sb("GT", [128, 128, PT], BF16)
            usr = self.ring(st, "usr", [128, 8, CG, 128], BF16, 2)
            vsr = self.ring(st, "vsr", [128, CG, 1024], BF16, 2)
            agr = self.ring(st, "agr", [128, PT], BF16, 3)
            mgr = self.ring(st, "mgr", [128, PT], BF16, 3)
            xt = sb("xtp", [128, 8, PT], F32)
            sq = cand2
            rs = sb("rs3", [128, PT], F32)
            tmpr = self.ring(st, "tmpr3", [128, PT], F32, 2)
            ost = cand
            dma("pool", wq[:], self.w_pq[l].rearrange("(k p) d -> p k d", p=128), writes=["wq"])
            dma("sp", kbd[:], self.keysbd[l].rearrange("h p n -> p h n"), writes=["kbd"])
            usrc = self.ubf[l].rearrange("(g p) (k c i) -> g p k c i", p=128, k=8, c=CG)
            vsrc = self.vbf[l].rearrange("(c i) d -> i c d", i=128)
            hsrc = self.h2T.rearrange("(k p) t -> p k t", p=128)
            xsrc = self.xT_a.rearrange("(k p) t -> p k t", p=128)
            tiles = PTILES if need_ctx else PTILES[1:]
            iota16 = self.iota_f[:, 0:16]
            iT2 = [iT, sb("iTb", [128, PT], F32)]
            jT2 = [jT, sb("jTb", [128, PT], F32)]
            gT2 = [gTt, sb("gTtb", [128, PT], F32)]
            h2of = {}

            def scores_topk(idx, sub):
                t0, n = tiles[idx]
                tsl = slice(sub * 128, (sub + 1) * 128)
                for h in range(8):
                    bank = 4 + (h % 2)
                    S.op("pe", lambda e, h=h: e.matmul(PS[bank][:, :256], lhsT=qTh[:, h, tsl], rhs=kbd[:, h, :],
                                                       start=True, stop=True),
                         reads=[("qTh", h), "kbd"], writes=[("ps", bank)])
                    S.op("act", lambda e, h=h: e.copy(out=sc[:, h, :], in_=PS[bank][:, :256]),
                         reads=[("ps", bank)], writes=[("sc", h)])
                for h in range(8):
                    for p in range(2):
                        src = sc[:, h, p * 128:(p + 1) * 128]
                        wks = wk1[:, h, p * 128:(p + 1) * 128]
                        S.op("dve", lambda e, h=h, p=p, src=src: e.max(out=v8[:, h, p, 0:8], in_=src),
                             reads=[("sc", h)], writes=["v8"])
                        S.op("dve", lambda e, h=h, p=p, src=src: e.max_index(out=i8u[:, h, p, 0:8], in_max=v8[:, h, p, 0:8],
                                                                              in_values=src),
                             reads=[("sc", h), "v8"], writes=["i8u"])
                        S.op("dve", lambda e, h=h, p=p, src=src, wks=wks: e.match_replace(
                            out=wks, in_to_replace=v8[:, h, p, 0:8], in_values=src, imm_value=-1e30),
                            reads=[("sc", h), "v8"], writes=["cand2"])
                        S.op("dve", lambda e, h=h, p=p, wks=wks: e.max(out=v8[:, h, p, 8:16], in_=wks),
                             reads=["cand2"], writes=["v8"])
                        S.op("dve", lambda e, h=h, p=p, wks=wks: e.max_index(out=i8u[:, h, p, 8:16],
                                                                             in_max=v8[:, h, p, 8:16], in_values=wks),
                             reads=["cand2", "v8"], writes=["i8u"])
                S.op("dve", lambda e: e.tensor_copy(out=i8f[:], in_=i8u[:]), reads=["i8u"], writes=["i8f"])
                S.op("dve", lambda e: e.tensor_tensor(
                    out=cand[:].rearrange("p h (a b) -> p h a b", b=16),
                    in0=v8[:, :, 0, :].unsqueeze(3).to_broadcast([128, 8, 16, 16]),
                    in1=v8[:, :, 1, :].unsqueeze(2).to_broadcast([128, 8, 16, 16]), op=ALU.add),
                    reads=["v8"], writes=["cand"])
                for h in range(8):
                    S.op("dve", lambda e, h=h: e.max(out=best[:, h, 0:8], in_=cand[:, h, :]), reads=["cand"], writes=["best"])
                    S.op("dve", lambda e, h=h: e.max_index(out=posu[:, h, 0:8], in_max=best[:, h, 0:8], in_values=cand[:, h, :]),
                         reads=["cand", "best"], writes=["posu"])
                    S.op("dve", lambda e, h=h: e.match_replace(out=cand2[:, h, :], in_to_replace=best[:, h, 0:8],
                                                               in_values=cand[:, h, :], imm_value=-1e30),
                         reads=["cand", "best"], writes=["cand2"])
                    S.op("dve", lambda e, h=h: e.max(out=best[:, h, 8:16], in_=cand2[:, h, :]), reads=["cand2"], writes=["best"])
                    S.op("dve", lambda e, h=h: e.max_index(out=posu[:, h, 8:16], in_max=best[:, h, 8:16], in_values=cand2[:, h, :]),
                         reads=["cand2", "best"], writes=["posu"])
                S.op("dve", lambda e: e.tensor_single_scalar(out=posf[:].bitcast(U32), in_=posu[:], scalar=4,
                                                             op=ALU.logical_shift_right), reads=["posu"], writes=["posf"])
                S.op("dve", lambda e: e.tensor_copy(out=af[:], in_=posf[:].bitcast(U32)), reads=["posf"], writes=["af"])
                S.op("dve", lambda e: e.tensor_single_scalar(out=posf[:].bitcast(U32), in_=posu[:], scalar=15,
                                                             op=ALU.bitwise_and), reads=["posu", "af"], writes=["posf"])
                S.op("dve", lambda e: e.tensor_copy(out=bf[:], in_=posf[:].bitcast(U32)), reads=["posf"], writes=["bf"])
                for (src, p, dst, dk) in ((af, 0, iw, "iw"), (bf, 1, jw, "jw")):
                    S.op("dve", lambda e, src=src: e.tensor_tensor(
                        out=oh[:], in0=src[:].rearrange("p h k -> p (h k)").unsqueeze(2).to_broadcast([128, 128, 16]),
                        in1=iota16.unsqueeze(1).to_broadcast([128, 128, 16]), op=ALU.is_equal),
                        reads=["af", "bf", "iota_f"], writes=["oh"])
                    S.op("dve", lambda e, p=p: e.tensor_tensor(
                        out=oh[:].rearrange("p (h k) a -> p h k a", k=16),
                        in0=oh[:].rearrange("p (h k) a -> p h k a", k=16),
                        in1=i8f[:, :, p, :].unsqueeze(2).to_broadcast([128, 8, 16, 16]), op=ALU.mult),
                        reads=["oh", "i8f"], writes=["oh"])
                    S.op("dve", lambda e, dst=dst: e.tensor_reduce(out=dst[:], in_=oh[:], axis=AX.X, op=ALU.add),
                         reads=["oh"], writes=[dk])
                S.op("dve", lambda e: e.tensor_tensor(out=gw[:], in0=best[:],
                                                      in1=best[:, :, 0:1].to_broadcast([128, 8, 16]), op=ALU.subtract),
                     reads=["best"], writes=["gw"])
                S.op("act", lambda e: e.activation(out=gw[:], in_=gw[:], func=AF.Exp), reads=["gw"], writes=["gw"])
                S.op("dve", lambda e: e.tensor_reduce(out=zs[:], in_=gw[:], axis=AX.X, op=ALU.add), reads=["gw"], writes=["zs"])
                S.op("dve", lambda e: e.reciprocal(out=zs[:], in_=zs[:]), reads=["zs"], writes=["zs"])
                S.op("dve", lambda e: e.tensor_tensor(out=gw[:], in0=gw[:], in1=zs[:].unsqueeze(2).to_broadcast([128, 8, 16]),
                                                      op=ALU.mult), reads=["gw", "zs"], writes=["gw"])

            def transposes(idx, sub):
                par = idx % 2
                tsl = slice(sub * 128, (sub + 1) * 128)
                for (src_ap, sk, dstT, dk, bank) in ((iw[:], "iw", iT2[par], ("iT", par), 4), (jw[:], "jw", jT2[par], ("jT", par), 5),
                                                     (gw[:].rearrange("p h k -> p (h k)"), "gw", gT2[par], ("gTt", par), 4)):
                    S.op("pe", lambda e, src_ap=src_ap, bank=bank: e.transpose(out=PS[bank][:, :128], in_=src_ap,
                                                                               identity=self.ident[:]),
                         reads=[sk, "ident"], writes=[("ps", bank)])
                    S.op("act", lambda e, dstT=dstT, bank=bank: e.copy(out=dstT[:, tsl], in_=PS[bank][:, :128]),
                         reads=[("ps", bank)], writes=[dk])

            def stageA(idx, part):
                t0, n = tiles[idx]
                if part == 0:
                    h2, h2k = h2r.next()
                    h2of[idx] = (h2, h2k)
                    dma("sp", h2[:], hsrc[:, :, t0:t0 + n], writes=[h2k])
                    for h in range(8):
                        bank = 4 + (h % 2)
                        for k in range(8):
                            S.op("pe", lambda e, h=h, k=k: e.matmul(PS[bank][:, :n], lhsT=wq[:, k, h * 128:(h + 1) * 128],
                                                                    rhs=h2[:, k, :], start=(k == 0), stop=(k == 7)),
                                 reads=["wq", h2k], writes=[("ps", bank)])
                        S.op("act", lambda e, h=h: e.copy(out=qTh[:, h, :], in_=PS[bank][:, :n]),
                             reads=[("ps", bank)], writes=[("qTh", h)])
                    scores_topk(idx, 0)
                elif part == 1:
                    transposes(idx, 0)
                    scores_topk(idx, 1)
                else:
                    transposes(idx, 1)

            for part in range(3):
                stageA(0, part)
            hooks = {2: 0, 12: 1, 24: 2}
            for idx, (t0, n) in enumerate(tiles):
                s = 1 if t0 < NCTX else 0
                par = idx % 2
                iTc, jTc, gTc = iT2[par], jT2[par], gT2[par]
                h2, h2k = h2of.pop(idx)
                dma("sp", xt[:], xsrc[:, :, t0:t0 + n], writes=["xtp"])
                for tg in range(n // 4):
                    bank = 4 + (tg % 2)
                    for tt in range(4):
                        tok = tg * 4 + tt
                        lh, lk = lhr.next()
                        rh, rk = rhr.next()
                        S.op("dve", lambda e, lh=lh, tok=tok: e.tensor_scalar(
                            out=lh[:], in0=self.iota_f[:], scalar1=iTc[:, tok:tok + 1], scalar2=gTc[:, tok:tok + 1],
                            op0=ALU.is_equal, op1=ALU.mult), reads=[("iT", par), ("gTt", par), "iota_f"], writes=[lk])
                        S.op("pool" if tt % 2 == 0 else "dve", lambda e, rh=rh, tok=tok: e.tensor_scalar(
                            out=rh[:], in0=self.iota_f[:], scalar1=jTc[:, tok:tok + 1], scalar2=None,
                            op0=ALU.is_equal), reads=[("jT", par), "iota_f"], writes=[rk])
                        S.op("pe", lambda e, lh=lh, rh=rh, tt=tt: e.matmul(PS[bank][:, tt * 128:(tt + 1) * 128], lhsT=lh[:],
                                                                           rhs=rh[:], start=True, stop=True),
                             reads=[lk, rk], writes=[("ps", bank)])
                    S.op("act", lambda e, tg=tg, bank=bank: e.copy(
                        out=GT[:, :, tg * 4:(tg + 1) * 4].rearrange("p j t -> p t j"),
                        in_=PS[bank][:, :].rearrange("p (t j) -> p t j", j=128)), reads=[("ps", bank)], writes=["GT"])
                slabs = {}

                def emit_A(c):
                    cg, cl = divmod(c, CG)
                    if cl == 0:
                        if cg in hooks and idx + 1 < len(tiles):
                            stageA(idx + 1, hooks[cg])
                        us, uk = usr.next()
                        vs, vk = vsr.next()
                        dma("sp", us[:], usrc[cg], writes=[uk])
                        dma("sp", vs[:], vsrc[:, cg * CG:(cg + 1) * CG, :], writes=[vk])
                        slabs[cg] = (us, uk, vs, vk)
                    us, uk, vs, vk = slabs[cg]
                    bank = 6 + (c % 2)
                    for k in range(8):
                        S.op("pe", lambda e, k=k: e.matmul(PS[bank][:, :n], lhsT=us[:, k, cl, :], rhs=h2[:, k, :],
                                                           start=(k == 0), stop=(k == 7)),
                             reads=[uk, h2k], writes=[("ps", bank)])

                emit_A(0)
                for c in range(128):
                    cg, cl = divmod(c, CG)
                    us, uk, vs, vk = slabs[cg]
                    bank = 6 + (c % 2)
                    ag, agk = agr.next()
                    S.op("act", lambda e, ag=ag, bank=bank: e.activation(out=ag[:], in_=PS[bank][:, :n],
                                                                         func=AF.Gelu_apprx_tanh),
                         reads=[("ps", bank)], writes=[agk])
                    mg, mgk = mgr.next()
                    S.op("pool", lambda e, ag=ag, mg=mg, c=c: e.tensor_tensor(out=mg[:], in0=ag[:], in1=GT[:, c, :],
                                                                              op=ALU.mult), reads=[agk, "GT"], writes=[mgk])
                    if c + 1 < 128:
                        emit_A(c + 1)
                    for dc in range(8):
                        S.op("pe", lambda e, dc=dc, cl=cl, mg=mg, c=c, vs=vs: e.matmul(
                            PS[dc // 2][:, (dc % 2) * PT:(dc % 2 + 1) * PT], lhsT=vs[:, cl, dc * 128:(dc + 1) * 128],
                            rhs=mg[:], start=(c == 0), stop=(c == 127)), reads=[vk, mgk], writes=[("pso", dc)])
                    if cl == CG - 1:
                        slabs.pop(cg)
                for dc in range(8):
                    S.op("dve", lambda e, dc=dc: e.scalar_tensor_tensor(
                        out=xt[:, dc, :], in0=PS[dc // 2][:, (dc % 2) * PT:(dc % 2 + 1) * PT],
                        scalar=self.modT[:, 40 + dc, s:s + 1], in1=xt[:, dc, :], op0=ALU.mult, op1=ALU.add),
                        reads=[("pso", dc), "xtp", "modT"], writes=["xtp"])
                if not final:
                    dma("sp", xsrc[:, :, t0:t0 + n], xt[:], reads=["xtp"])
                else:
                    self.adaln_tile(xt, "xtp", n, lambda k: self.ngs[:, 4, k:k + 1], lambda k: None,
                                    lambda k: ost[:, k, :], "cand", sq, rs, tmpr, 6, sqkey="cand2")
                    dma("sp", self.outT.rearrange("(k p) t -> p k t", p=128)[:, :, t0 - NCTX:t0 - NCTX + n], ost[:],
                        reads=["cand"])
            S.barrier()


def _lay(v):
    return np.ascontiguousarray(np.asarray(v, np.float32).reshape(-1, 128).T)


def _rope_tables():
    t = np.arange(NLAT)
    r = (t // 64).astype(np.float32)
    c = (t % 64).astype(np.float32)

    def tab(dim):
        nf = dim // 4
        inv = (np.float32(10000.0) ** (-np.arange(nf, dtype=np.float32) / np.float32(nf))).astype(np.float32)
        ang = np.concatenate([r[:, None] * inv[None, :], c[:, None] * inv[None, :]], axis=1).astype(np.float32)
        return np.cos(ang).astype(np.float32), np.sin(ang).astype(np.float32)

    c64, s64 = tab(64)
    c32, s32 = tab(32)
    rope64 = np.zeros((2, 128, NLAT), np.float32)
    for row in range(128):
        rope64[0, row] = c64[:, row % 32]
        rope64[1, row] = s64[:, row % 32]
    ropem = np.zeros((2, 96, NLAT), np.float32)
    ropem[0, :64] = 1.0
    for row in range(32):
        ropem[0, 64 + row] = c32[:, row % 16]
        ropem[1, 64 + row] = s32[:, row % 16]
    rm = np.zeros((3, 128, 128), np.float32)
    for B in (0, 64):
        for m in range(32):
            rm[0, B + m + 32, B + m] = -1.0
            rm[0, B + m, B + m + 32] = 1.0
    for m in range(16):
        rm[1, 64 + m + 16, 64 + m] = -1.0
        rm[1, 64 + m, 64 + m + 16] = 1.0
        rm[2, m + 16, m] = -1.0
        rm[2, m, m + 16] = 1.0
    return rope64, ropem, rm


def prep_inputs(inp):
    g = {k: np.asarray(v) for k, v in inp.items()}
    L = DEPTH
    shared = {}
    shared["w_mod"] = g["w_mod"]
    shared["b_mod_lay"] = np.ascontiguousarray(g["b_mod"].reshape(L, 48, 128).transpose(0, 2, 1))
    ngl = np.zeros((128, 5, 8), np.float32)
    ngl[:, 0] = _lay(g["norm1_g"][0]); ngl[:, 1] = _lay(g["norm1_g"][1])
    ngl[:, 2] = _lay(g["norm2_g"][0]); ngl[:, 3] = _lay(g["norm2_g"][1])
    ngl[:, 4] = _lay(g["final_norm_g"])
    shared["norm_g_lay"] = ngl
    shared["w_in"] = g["w_in"]
    cw = np.zeros((L, 128, 4, 5), np.float32)
    for l in range(L):
        for tap in range(4):
            cw[l, :, :, tap] = g["conv_w"][l, tap].reshape(4, 128).T
        cw[l, :, :, 4] = g["conv_b"][l].reshape(4, 128).T
    shared["convw_lay"] = cw
    bd = np.zeros((L, 2, 2, 4, 128, 128), np.float32)
    for l in range(L):
        for d in range(2):
            for kind, wname in enumerate(("lru_wa", "lru_wi")):
                w = g[wname][l, d]
                for cc in range(4):
                    bd[l, d, kind, cc, 0:64, 0:64] = w[2 * cc]
                    bd[l, d, kind, cc, 64:128, 64:128] = w[2 * cc + 1]
    shared["lru_bd"] = bd
    lv = np.zeros((L, 128, 2, 3, 4), np.float32)
    for l in range(L):
        for d in range(2):
            for kind, nm in enumerate(("lru_ba", "lru_bi", "lru_lambda")):
                lv[l, :, d, kind, :] = g[nm][l, d].reshape(4, 128).T
    shared["lru_vec"] = lv
    shared["diff_lam_rep"] = np.ascontiguousarray(np.broadcast_to(g["diff_lam"][:, None], (L, 128, 4, 64)))
    sgm = np.zeros((L, 128, 8), np.float32)
    for l in range(L):
        sgm[l, :, 0] = g["diff_subln_g"][l]
        sgm[l, :, 1] = np.tile(g["gqa_qnorm_g"][l], 2)
        sgm[l, :, 2] = np.tile(g["gqa_knorm_g"][l], 2)
        sgm[l, :, 3:6] = g["mla_qnorm_g"][l].reshape(3, 128).T
        sgm[l, :, 6:8] = g["mla_kvnorm_g"][l].reshape(2, 128).T
    shared["smallg"] = sgm
    shared["mla_w_uq"] = g["mla_w_uq"]
    shared["mla_w_ukv"] = g["mla_w_ukv"]
    shared["w_branch"] = g["w_branch"]
    shared["w_out"] = g["w_out"]
    shared["peer_wq"] = g["peer_wq"]
    kb = np.zeros((L, 8, 128, 256), np.float32)
    for p in range(2):
        kb[:, :, p * 64:(p + 1) * 64, p * 128:(p + 1) * 128] = g["peer_keys"][:, :, p].transpose(0, 1, 3, 2)
    shared["keysbd"] = kb
    shared["peer_uT"] = np.ascontiguousarray(
        g["peer_u"].reshape(L, 128, 128 // CG, CG, 8, 128).transpose(0, 2, 5, 4, 3, 1).reshape(L, 4096, 4096))
    shared["peer_vP"] = np.ascontiguousarray(
        g["peer_v"].reshape(L, 128, 128, D).transpose(0, 2, 1, 3).reshape(L, 16384, D))
    rope64, ropem, rm = _rope_tables()
    shared["rope64"] = rope64
    shared["ropem"] = ropem
    shared["rmats"] = rm
    cst = np.zeros((3, 128, 128), np.float32)
    cst[0] = np.eye(128, dtype=np.float32)
    cst[1] = 1.0
    cst[2, :64, :64] = 1.0
    cst[2, 64:, 64:] = 1.0
    shared["consts"] = cst
    shared["iota_in"] = np.ascontiguousarray(np.broadcast_to(np.arange(128, dtype=np.float32)[None, :], (128, 128)))
    maps = []
    for b in range(8):
        m = dict(shared)
        xall = np.concatenate([g["ctx"][b], g["x"][b]], axis=0)
        m["xT_in"] = np.ascontiguousarray(xall.T)
        cvv = np.zeros((128, 8, 2), np.float32)
        cvv[:, :, 0] = _lay(g["c"][b])
        cvv[:, :, 1] = _lay(g["c_ctx"])
        m["cvec"] = cvv
        maps.append(m)
    return maps


def kernel(**inputs):
    maps = prep_inputs(inputs)
    nc = Prog().build()
    res = run_bass_kernel_spmd(nc, maps, core_ids=list(range(8)))
    out = np.stack([np.ascontiguousarray(res.results[b]["outT"].T) for b in range(8)], axis=0)
    return out.astype(np.float32)
```

```python
import math
import numpy as np
from contextlib import ExitStack
import concourse.bass as bass
import concourse.mybir as mybir
from concourse.bass_utils import run_bass_kernel_spmd

F32 = mybir.dt.float32
BF16 = mybir.dt.bfloat16
U32 = mybir.dt.uint32
AF = mybir.ActivationFunctionType
ALU = mybir.AluOpType
AX = mybir.AxisListType

D = 1024
NCTX = 256
NLAT = 4096
T = NCTX + NLAT
DEPTH = 2
EPS = 1e-6
IN_COLS = 8096
NKB = T // 128
TILES = [(0, 256)] + [(256 + 512 * i, 512) for i in range(8)]
PT = 256
PTILES = [(i * PT, PT) for i in range(T // PT)]
CG = 2
TB = 16
UROWS = (128 // CG) * 128
UCOLS = 8 * CG * 128


class Sched:
    def __init__(self, nc, es, n_dma=40):
        self.nc = nc
        self.E = {"pe": nc.tensor, "act": nc.scalar, "dve": nc.vector, "pool": nc.gpsimd, "sp": nc.sync}
        self.sem = {k: es.enter_context(nc.semaphore("s_" + k)) for k in self.E}
        self.cnt = {k: 0 for k in self.E}
        self.seen = {k: {} for k in self.E}
        self.nd = n_dma
        self.dsem = [es.enter_context(nc.semaphore("d%d" % i)) for i in range(n_dma)]
        self.dval = [0] * n_dma
        self.dnext = 0
        self.lastw = {}
        self.readers = {}
        self.same_wait = {"pe": False, "act": True, "dve": True, "pool": True, "sp": True}

    def _wait(self, ek, key, val):
        if self.seen[ek].get(key, 0) >= val:
            return
        kind, idx = key
        if kind == "e" and idx == ek and not self.same_wait[ek]:
            return
        sem = self.sem[idx] if kind == "e" else self.dsem[idx]
        self.E[ek].wait_ge(sem, val)
        self.seen[ek][key] = val

    def op(self, ek, fn, reads=(), writes=(), dma=False):
        deps = {}
        for r in reads:
            t = self.lastw.get(r)
            if t is not None and deps.get(t[0], 0) < t[1]:
                deps[t[0]] = t[1]
        for w in writes:
            t = self.lastw.get(w)
            if t is not None and deps.get(t[0], 0) < t[1]:
                deps[t[0]] = t[1]
            for k, v in self.readers.get(w, {}).items():
                if deps.get(k, 0) < v:
                    deps[k] = v
        for k, v in deps.items():
            self._wait(ek, k, v)
        if dma:
            i = self.dnext
            self.dnext = (i + 1) % self.nd
            if self.dval[i] > 0:
                self._wait(ek, ("d", i), self.dval[i])
            ins = fn(self.E[ek])
            self.dval[i] += 16
            ins.then_inc(self.dsem[i], 16)
            tok = (("d", i), self.dval[i])
        else:
            ins = fn(self.E[ek])
            self.cnt[ek] += 1
            ins.then_inc(self.sem[ek], 1)
            tok = (("e", ek), self.cnt[ek])
        for w in writes:
            self.lastw[w] = tok
            self.readers[w] = {}
        for r in reads:
            d = self.readers.setdefault(r, {})
            if d.get(tok[0], 0) < tok[1]:
                d[tok[0]] = tok[1]
        return tok

    def barrier(self):
        for ek in self.E:
            for k2 in self.E:
                if k2 != ek and self.cnt[k2] > 0:
                    self._wait(ek, ("e", k2), self.cnt[k2])
            for i in range(self.nd):
                if self.dval[i] > 0:
                    self._wait(ek, ("d", i), self.dval[i])
        self.lastw.clear()
        self.readers.clear()


class Ring:
    def __init__(self, tensors, name):
        self.t = tensors
        self.name = name
        self.i = 0

    def next(self):
        j = self.i % len(self.t)
        self.i += 1
        return self.t[j], (self.name, j)


class Prog:
    def __init__(self, dbg=None):
        self.dbg = dbg
        nc = self.nc = bass.Bass("TRN2", target_bir_lowering=False)
        self.uid = 0

        def din(name, shape, dt=F32):
            return nc.dram_tensor(name, list(shape), dt, kind="ExternalInput").ap()

        def dscr(name, shape, dt):
            kind = "ExternalOutput" if dbg else "Internal"
            return nc.dram_tensor(name, list(shape), dt, kind=kind).ap()

        self.xT_in = din("xT_in", [D, T])
        self.cvec = din("cvec", [128, 8, 2])
        self.w_mod = din("w_mod", [DEPTH, D, 6 * D])
        self.b_mod = din("b_mod_lay", [DEPTH, 128, 48])
        self.ng = din("norm_g_lay", [128, 5, 8])
        self.w_in = din("w_in", [DEPTH, D, IN_COLS])
        self.convw = din("convw_lay", [DEPTH, 128, 4, 5])
        self.lru_bd = din("lru_bd", [DEPTH, 2, 2, 4, 128, 128])
        self.lru_vec = din("lru_vec", [DEPTH, 128, 2, 3, 4])
        self.diff_lam = din("diff_lam_rep", [DEPTH, 128, 4, 64])
        self.smallg = din("smallg", [DEPTH, 128, 8])
        self.w_uq = din("mla_w_uq", [DEPTH, 384, 768])
        self.w_ukv = din("mla_w_ukv", [DEPTH, 256, 1024])
        self.w_br = din("w_branch", [DEPTH, 4, 512, 1024])
        self.w_out = din("w_out", [DEPTH, D, D])
        self.w_pq = din("peer_wq", [DEPTH, D, D])
        self.keysbd = din("keysbd", [DEPTH, 8, 128, 256])
        self.UT = din("peer_uT", [DEPTH, UROWS, UCOLS])
        self.VP = din("peer_vP", [DEPTH, 16384, D])
        self.rope64 = din("rope64", [2, 128, NLAT])
        self.ropem = din("ropem", [2, 96, NLAT])
        self.rmats = din("rmats", [3, 128, 128])
        self.consts = din("consts", [3, 128, 128])
        self.iota_in = din("iota_in", [128, 128])
        self.outT = nc.dram_tensor("outT", [D, NLAT], F32, kind="ExternalOutput").ap()

        self.xT_a = dscr("xT_a", [D, T], F32)
        self.xaT = dscr("xaT", [512, T], F32)
        self.yaT = dscr("yaT", [512, T], BF16)
        self.qbT = dscr("qbT", [512, T], BF16)
        self.kbT = dscr("kbT", [512, T], BF16)
        self.vb = dscr("vb", [T, 512], BF16)
        self.qgT = dscr("qgT", [512, T], BF16)
        self.kgT = dscr("kgT", [128, T], BF16)
        self.vg = dscr("vg", [T, 128], BF16)
        self.qmT = dscr("qmT", [8, 96, T], BF16)
        self.kmT = dscr("kmT", [8, 96, T], BF16)
        self.vm = dscr("vm", [T, 512], BF16)
        self.gT = dscr("gT", [4096, T], BF16)
        self.oT = dscr("oT", [4, 512, T], BF16)
        self.h2T = dscr("h2T", [D, T], BF16)
        self.ubf = [nc.dram_tensor("ubf%d" % l, [UROWS, UCOLS], BF16, kind="Internal").ap() for l in range(DEPTH)]
        self.vbf = [nc.dram_tensor("vbf%d" % l, [16384, D], BF16, kind="Internal").ap() for l in range(DEPTH)]

    def sb(self, st, name, shape, dt):
        self.uid += 1
        return st.enter_context(self.nc.sbuf_tensor("%s_%d" % (name, self.uid), list(shape), dt))

    def ring(self, st, name, shape, dt, n):
        self.uid += 1
        return Ring([self.sb(st, "%s%d" % (name, i), shape, dt) for i in range(n)], "%s_%d" % (name, self.uid))

    def dma(self, q, out, in_, reads=(), writes=()):
        return self.S.op(q, lambda e: e.dma_start(out=out, in_=in_), reads=reads, writes=writes, dma=True)

    def build(self):
        nc = self.nc
        es = ExitStack()
        with es:
            S = self.S = Sched(nc, es)
            self.PSW = [es.enter_context(nc.psum_tensor("psw%d" % i, [128, 1024], F32)) for i in range(4)]
            self.PS = [self.PSW[i // 2][:, (i % 2) * 512:(i % 2 + 1) * 512] for i in range(8)]
            sb, dma = self.sb, self.dma
            self.ident = sb(es, "ident", [128, 128], F32)
            self.ones_f = sb(es, "ones_f", [128, 128], F32)
            self.bd64_f = sb(es, "bd64_f", [128, 128], F32)
            self.ones_b = sb(es, "ones_b", [128, 128], BF16)
            self.r64 = sb(es, "r64", [128, 128], BF16)
            self.r96 = sb(es, "r96", [128, 128], BF16)
            self.r32 = sb(es, "r32", [128, 128], BF16)
            self.epsc = sb(es, "epsc", [128, 1], F32)
            self.iota_f = sb(es, "iota_f", [128, 128], F32)
            self.cv = sb(es, "cv", [128, 8, 2], F32)
            self.ngs = sb(es, "ngs", [128, 5, 8], F32)
            self.modT = sb(es, "modT", [128, 48, 2], F32)
            self.A1 = sb(es, "A1", [128, 8, 2], F32)
            dma("sp", self.ident[:], self.consts[0], writes=["ident"])
            dma("sp", self.ones_f[:], self.consts[1], writes=["ones_f"])
            dma("sp", self.bd64_f[:], self.consts[2], writes=["bd64_f"])
            dma("pool", self.ones_b[:], self.consts[1], writes=["ones_b"])
            dma("pool", self.r64[:], self.rmats[0], writes=["r64"])
            dma("pool", self.r96[:], self.rmats[1], writes=["r96"])
            dma("pool", self.r32[:], self.rmats[2], writes=["r32"])
            dma("sp", self.iota_f[:], self.iota_in, writes=["iota_f"])
            dma("sp", self.cv[:], self.cvec, writes=["cv"])
            dma("sp", self.ngs[:], self.ng, writes=["ngs"])
            S.op("dve", lambda e: e.memset(self.epsc[:], EPS), writes=["epsc"])
            S.op("act", lambda e: e.activation(out=self.cv[:], in_=self.cv[:], func=AF.Silu), reads=["cv"], writes=["cv"])
            if self.dbg in (None, "peer"):
                for l in range(DEPTH):
                    for i in range(8):
                        dma("pool", self.ubf[l][i * (UROWS // 8):(i + 1) * (UROWS // 8), :],
                            self.UT[l, i * (UROWS // 8):(i + 1) * (UROWS // 8), :])
                    for i in range(16):
                        dma("pool", self.vbf[l][i * 1024:(i + 1) * 1024, :], self.VP[l, i * 1024:(i + 1) * 1024, :])
            xcur = self.xT_in
            stop = False
            for l in range(DEPTH):
                need_ctx = l < DEPTH - 1
                lam_init = 0.8 - 0.6 * math.exp(-0.3 * l)
                S.barrier()
                self.mod_phase(l)
                if self.dbg == "mod":
                    dm = nc.dram_tensor("dbg_mod", [128, 96], F32, kind="ExternalOutput").ap()
                    dma("sp", dm, self.modT[:].rearrange("p j s -> p (j s)"), reads=["modT"])
                    break
                S.barrier()
                self.inproj_phase(l, xcur)
                if self.dbg == "inproj":
                    break
                S.barrier()
                self.lru_phase(l)
                if self.dbg == "lru":
                    break
                S.barrier()
                self.attn_phase(l, need_ctx, lam_init)
                if self.dbg == "attn":
                    break
                S.barrier()
                self.merge_phase(l, need_ctx, xcur)
                xcur = self.xT_a
                if self.dbg == "merge":
                    break
                S.barrier()
                self.peer_phase(l, need_ctx, final=(l == DEPTH - 1))
                if self.dbg == "peer":
                    break
            S.barrier()
        return nc

    def mod_phase(self, l):
        S, PS, dma = self.S, self.PS, self.dma
        with ExitStack() as st:
            wm = self.ring(st, "wm", [128, 8, 512], F32, 2)
            bm = self.sb(st, "bm", [128, 48], F32)
            dma("sp", bm[:], self.b_mod[l], writes=["bm"])
            wsrc = self.w_mod[l].rearrange("(k p) c -> p k c", p=128)
            for g in range(12):
                wt, wk = wm.next()
                dma("sp", wt[:], wsrc[:, :, g * 512:(g + 1) * 512], writes=[wk])
                for jj in range(4):
                    j = g * 4 + jj
                    for k in range(8):
                        S.op("pe", lambda e, wt=wt, jj=jj, k=k, j=j: e.matmul(
                            PS[0][:, 2 * j:2 * j + 2], lhsT=wt[:, k, jj * 128:(jj + 1) * 128], rhs=self.cv[:, k, :],
                            start=(k == 0), stop=(k == 7)), reads=[wk, "cv"], writes=[("ps", 0)])
            for s in range(2):
                S.op("dve", lambda e, s=s: e.tensor_tensor(
                    out=self.modT[:, :, s], in0=PS[0][:, 0:96].rearrange("p (j s) -> p j s", s=2)[:, :, s],
                    in1=bm[:], op=ALU.add), reads=[("ps", 0), "bm"], writes=["modT"])
            S.barrier()

    def make_A(self, gidx, scale_idx):
        for s in range(2):
            self.S.op("dve", lambda e, s=s: e.scalar_tensor_tensor(
                out=self.A1[:, :, s], in0=self.modT[:, scale_idx * 8:(scale_idx + 1) * 8, s], scalar=1.0,
                in1=self.ngs[:, gidx, :], op0=ALU.add, op1=ALU.mult), reads=["modT", "ngs"], writes=["A1"])

    def adaln_tile(self, xt, xkey, n, acol_fn, bias_fn, hout_fn, hkey, sq, rs, tmpr, psb, sqkey="sq"):
        S, PS = self.S, self.PS
        for k in range(8):
            S.op("act", lambda e, k=k: e.activation(out=sq[:, k, :n], in_=xt[:, k, :n], func=AF.Square),
                 reads=[xkey], writes=[sqkey])
        for k in range(8):
            S.op("pe", lambda e, k=k: e.matmul(PS[psb][:, :n], lhsT=self.ones_f[:], rhs=sq[:, k, :n],
                                               start=(k == 0), stop=(k == 7)),
                 reads=[sqkey, "ones_f"], writes=[("ps", psb)])
        S.op("act", lambda e: e.activation(out=rs[:, :n], in_=PS[psb][:, :n], func=AF.Sqrt,
                                           bias=self.epsc[:], scale=1.0 / D),
             reads=[("ps", psb), "epsc"], writes=["rs"])
        S.op("dve", lambda e: e.reciprocal(out=rs[:, :n], in_=rs[:, :n]), reads=["rs"], writes=["rs"])
        for k in range(8):
            tt, tk = tmpr.next()
            S.op("dve", lambda e, k=k, tt=tt: e.scalar_tensor_tensor(
                out=tt[:, :n], in0=xt[:, k, :n], scalar=acol_fn(k), in1=rs[:, :n],
                op0=ALU.mult, op1=ALU.mult), reads=[xkey, "A1", "rs"], writes=[tk])
            b = bias_fn(k)
            if b is None:
                S.op("act", lambda e, k=k, tt=tt: e.copy(out=hout_fn(k), in_=tt[:, :n]), reads=[tk], writes=[hkey])
            else:
                S.op("act", lambda e, k=k, tt=tt, b=b: e.activation(
                    out=hout_fn(k), in_=tt[:, :n], func=AF.Identity, bias=b, scale=1.0),
                    reads=[tk, "modT"], writes=[hkey])

    def rope(self, P, n, xb, xkey, Rm, Rkey, Ct, St, tabkey, out_ap, outkey, bank, stg_f):
        S, PS = self.S, self.PS
        S.op("pe", lambda e: e.matmul(PS[bank][:P, :n], lhsT=Rm[:P, :P], rhs=xb[:P, :n], start=True, stop=True),
             reads=[xkey, Rkey], writes=[("ps", bank)])
        t1, t1k = stg_f.next()
        S.op("dve", lambda e: e.tensor_tensor(out=t1[:P, :n], in0=xb[:P, :n], in1=Ct[:P, :n], op=ALU.mult),
             reads=[xkey, tabkey], writes=[t1k])
        t2, t2k = stg_f.next()
        S.op("dve", lambda e: e.tensor_tensor(out=t2[:P, :n], in0=PS[bank][:P, :n], in1=St[:P, :n], op=ALU.mult),
             reads=[("ps", bank), tabkey], writes=[t2k])
        S.op("pool", lambda e: e.tensor_tensor(out=out_ap, in0=t1[:P, :n], in1=t2[:P, :n], op=ALU.add),
             reads=[t1k, t2k], writes=[outkey])

    def inproj_phase(self, l, xcur):
        S, PS, dma, nc = self.S, self.PS, self.dma, self.nc
        with ExitStack() as st:
            hT = self.sb(st, "hT", [128, 8, T], BF16)
            self.make_A(l, 1)
            with ExitStack() as st1:
                xr = self.ring(st1, "xr", [128, 8, 512], F32, 2)
                sq = self.sb(st1, "sq", [128, 8, 512], F32)
                rs = self.sb(st1, "rs", [128, 512], F32)
                tmpr = self.ring(st1, "tmpr", [128, 512], F32, 2)
                xsrc = xcur.rearrange("(k p) t -> p k t", p=128)
                for ti, (t0, n) in enumerate(TILES):
                    xt, xk = xr.next()
                    dma("sp", xt[:, :, :n], xsrc[:, :, t0:t0 + n], writes=[xk])
                    s = 1 if ti == 0 else 0
                    self.adaln_tile(xt, xk, n, lambda k, s=s: self.A1[:, k, s:s + 1],
                                    lambda k, s=s: self.modT[:, k, s:s + 1],
                                    lambda k, t0=t0, n=n: hT[:, k, t0:t0 + n], ("hT", ti), sq, rs, tmpr, 7)
                S.barrier()
            if self.dbg == "adaln":
                dh = nc.dram_tensor("dbg_h", [D, T], BF16, kind="ExternalOutput").ap()
                dma("sp", dh.rearrange("(k p) t -> p k t", p=128), hT[:])
                return
            wr = self.ring(st, "wsl", [128, 8, 512], BF16, 2)
            wmla = self.sb(st, "wmla", [128, 8, 672], BF16)
            wuq = self.sb(st, "wuq", [128, 3, 768], BF16)
            wkk = self.sb(st, "wkk", [128, 2, 8, 64], BF16)
            wkv = self.sb(st, "wkv", [128, 2, 8, 64], BF16)
            sg = self.sb(st, "sg", [128, 8], F32)
            stg_f = self.ring(st, "stgf", [128, 512], F32, 4)
            stg_b = self.ring(st, "stgb", [128, 512], BF16, 4)
            xbr = self.ring(st, "xbr", [128, 512], BF16, 3)
            tabC = self.ring(st, "tabC", [128, 512], F32, 2)
            tabS = self.ring(st, "tabS", [128, 512], F32, 2)
            sqn = self.sb(st, "sqn", [128, 512], F32)
            rsn = self.sb(st, "rsn", [128, 512], F32)
            zc = self.sb(st, "zc", [128, 5, 512], F32)
            sqm = self.sb(st, "sqm", [128, 5, 512], F32)
            cqn = self.sb(st, "cqn", [128, 3, 512], BF16)
            ckvn = self.sb(st, "ckvn", [128, 2, 512], BF16)
            wsrc = self.w_in[l].rearrange("(k p) c -> p k c", p=128)
            dma("sp", sg[:], self.smallg[l], writes=["sg"])
            bankc = [0]

            def nbank():
                b = bankc[0] % 3
                bankc[0] += 1
                return b

            def mm_chunk(bank, wt, wk, c0, m, ti):
                t0, n = TILES[ti]
                for k in range(8):
                    S.op("pe", lambda e, k=k: e.matmul(PS[bank][:m, :n], lhsT=wt[:, k, c0:c0 + m],
                                                       rhs=hT[:, k, t0:t0 + n], start=(k == 0), stop=(k == 7)),
                         reads=[wk], writes=[("ps", bank)])

            def load_tabs(src, P, ti, prow0=0):
                t0, n = TILES[ti]
                Ct, ck = tabC.next()
                St, sk = tabS.next()
                dma("sp", Ct[:P, :n], src[0, prow0:prow0 + P, t0 - NCTX:t0 - NCTX + n], writes=[ck])
                dma("sp", St[:P, :n], src[1, prow0:prow0 + P, t0 - NCTX:t0 - NCTX + n], writes=[sk])
                return Ct, St, ck, sk

            def store(dst_ap, src_ap, key):
                dma("sp", dst_ap, src_ap, reads=[key])

            def evac_act(bank, m, n, func, dt_ring, **kw):
                o, ok = dt_ring.next()
                S.op("act", lambda e: e.activation(out=o[:m, :n], in_=PS[bank][:m, :n], func=func, **kw),
                     reads=[("ps", bank)], writes=[ok])
                return o, ok

            def tokmajor(wt, wk, c0, w, dst, ti):
                t0, n = TILES[ti]
                for sub in range(n // 128):
                    bank = nbank()
                    for k in range(8):
                        S.op("pe", lambda e, k=k, sub=sub: e.matmul(
                            PS[bank][:, :w], lhsT=hT[:, k, t0 + sub * 128:t0 + (sub + 1) * 128],
                            rhs=wt[:, k, c0:c0 + w], start=(k == 0), stop=(k == 7)),
                            reads=[wk], writes=[("ps", bank)])
                    o, ok = evac_act(bank, 128, w, AF.Copy, stg_b)
                    store(dst[t0 + sub * 128:t0 + (sub + 1) * 128, :], o[:, :w], ok)

            def rope_store(P, n, ti, xb, xk, Rm, Rkey, tabs, dst_ap, bank=3):
                if ti == 0:
                    store(dst_ap, xb[:P, :n], xk)
                else:
                    Ct, St, ck, sk = tabs
                    o, ok = stg_b.next()
                    S.op("pe", lambda e: e.matmul(PS[bank][:P, :n], lhsT=Rm[:P, :P], rhs=xb[:P, :n],
                                                  start=True, stop=True),
                         reads=[xk, Rkey], writes=[("ps", bank)])
                    t1, t1k = stg_f.next()
                    S.op("dve", lambda e: e.tensor_tensor(out=t1[:P, :n], in0=xb[:P, :n], in1=Ct[:P, :n],
                                                          op=ALU.mult), reads=[xk, ck], writes=[t1k])
                    t2, t2k = stg_f.next()
                    S.op("dve", lambda e: e.tensor_tensor(out=t2[:P, :n], in0=PS[bank][:P, :n], in1=St[:P, :n],
                                                          op=ALU.mult), reads=[("ps", bank), sk], writes=[t2k])
                    S.op("pool", lambda e: e.tensor_tensor(out=o[:P, :n], in0=t1[:P, :n], in1=t2[:P, :n],
                                                           op=ALU.add), reads=[t1k, t2k], writes=[ok])
                    store(dst_ap, o[:P, :n], ok)

            def headnorm(bank, n, gcol, inv_dim, ones_mat, ones_key):
                S.op("act", lambda e: e.activation(out=sqn[:, :n], in_=PS[bank][:, :n], func=AF.Square),
                     reads=[("ps", bank)], writes=["sqn"])
                S.op("pe", lambda e: e.matmul(PS[4][:, :n], lhsT=ones_mat[:], rhs=sqn[:, :n], start=True, stop=True),
                     reads=["sqn", ones_key], writes=[("ps", 4)])
                S.op("act", lambda e: e.activation(out=rsn[:, :n], in_=PS[4][:, :n], func=AF.Sqrt,
                                                   bias=self.epsc[:], scale=inv_dim),
                     reads=[("ps", 4), "epsc"], writes=["rsn"])
                S.op("dve", lambda e: e.reciprocal(out=rsn[:, :n], in_=rsn[:, :n]), reads=["rsn"], writes=["rsn"])
                xb, xk = xbr.next()
                S.op("dve", lambda e: e.scalar_tensor_tensor(out=xb[:, :n], in0=PS[bank][:, :n], scalar=gcol,
                                                             in1=rsn[:, :n], op0=ALU.mult, op1=ALU.mult),
                     reads=[("ps", bank), "rsn", "sg"], writes=[xk])
                return xb, xk

            groups = [("xa", 0, 512), ("ya", 512, 512), ("qb", 1024, 512), ("kb", 1536, 512), ("vb", 2048, 512),
                      ("qg", 2560, 512), ("kvg", 3072, 256)] + [("zg%d" % i, 4000 + 512 * i, 512) for i in range(8)]
            for gname, c0, w in groups:
                wt, wk = wr.next()
                dma("pool", wt[:, :, :w], wsrc[:, :, c0:c0 + w], writes=[wk])
                for ti, (t0, n) in enumerate(TILES):
                    if gname == "vb":
                        tokmajor(wt, wk, 0, 512, self.vb, ti)
                        continue
                    if gname == "kvg":
                        tokmajor(wt, wk, 128, 128, self.vg, ti)
                    tabs = None
                    if gname in ("qb", "kb", "qg", "kvg") and ti > 0:
                        tabs = load_tabs(self.rope64, 128, ti)
                    nch = 1 if gname == "kvg" else w // 128
                    for ch in range(nch):
                        bank = nbank()
                        mm_chunk(bank, wt, wk, ch * 128, 128, ti)
                        if gname == "xa":
                            o, ok = evac_act(bank, 128, n, AF.Copy, stg_f)
                            store(self.xaT[ch * 128:(ch + 1) * 128, t0:t0 + n], o[:, :n], ok)
                        elif gname == "ya":
                            o, ok = evac_act(bank, 128, n, AF.Gelu_apprx_tanh, stg_b)
                            store(self.yaT[ch * 128:(ch + 1) * 128, t0:t0 + n], o[:, :n], ok)
                        elif gname.startswith("zg"):
                            o, ok = evac_act(bank, 128, n, AF.Sigmoid, stg_b)
                            r0 = (c0 - 4000) + ch * 128
                            store(self.gT[r0:r0 + 128, t0:t0 + n], o[:, :n], ok)
                        elif gname in ("qb", "kb"):
                            xb, xk = evac_act(bank, 128, n, AF.Copy, xbr)
                            dst = self.qbT if gname == "qb" else self.kbT
                            rope_store(128, n, ti, xb, xk, self.r64, "r64", tabs,
                                       dst[ch * 128:(ch + 1) * 128, t0:t0 + n])
                        elif gname == "qg":
                            xb, xk = headnorm(bank, n, sg[:, 1:2], 1.0 / 64, self.bd64_f, "bd64_f")
                            rope_store(128, n, ti, xb, xk, self.r64, "r64", tabs,
                                       self.qgT[ch * 128:(ch + 1) * 128, t0:t0 + n])
                        elif gname == "kvg":
                            xb, xk = headnorm(bank, n, sg[:, 2:3], 1.0 / 64, self.bd64_f, "bd64_f")
                            rope_store(128, n, ti, xb, xk, self.r64, "r64", tabs, self.kgT[:, t0:t0 + n])
            dma("pool", wmla[:], wsrc[:, :, 3328:4000], writes=["wmla"])
            dma("pool", wuq[:], self.w_uq[l].rearrange("(j p) c -> p j c", p=128), writes=["wuq"])
            ukv5 = self.w_ukv[l].rearrange("(j p) (h t d) -> p j h t d", p=128, h=8, t=2)
            for j in range(2):
                dma("pool", wkk[:, j], ukv5[:, j, :, 0, :], writes=["wkk"])
                dma("pool", wkv[:, j], ukv5[:, j, :, 1, :], writes=["wkv"])
            for ti, (t0, n) in enumerate(TILES):
                tq = tk_ = None
                if ti > 0:
                    tq = load_tabs(self.ropem, 96, ti)
                    tk_ = load_tabs(self.ropem, 32, ti, prow0=64)
                for j in range(5):
                    bank = nbank()
                    mm_chunk(bank, wmla, "wmla", j * 128, 128, ti)
                    S.op("act", lambda e, j=j: e.copy(out=zc[:, j, :n], in_=PS[bank][:, :n]),
                         reads=[("ps", bank)], writes=["zc"])
                    S.op("act", lambda e, j=j: e.activation(out=sqm[:, j, :n], in_=PS[bank][:, :n], func=AF.Square),
                         reads=[("ps", bank)], writes=["sqm"])
                for (j0, nj, inv, gc0, dstn, dkey) in ((0, 3, 1.0 / 384, 3, cqn, "cqn"), (3, 2, 1.0 / 256, 6, ckvn, "ckvn")):
                    for jj in range(nj):
                        S.op("pe", lambda e, jj=jj: e.matmul(PS[4][:, :n], lhsT=self.ones_f[:], rhs=sqm[:, j0 + jj, :n],
                                                             start=(jj == 0), stop=(jj == nj - 1)),
                             reads=["sqm", "ones_f"], writes=[("ps", 4)])
                    S.op("act", lambda e: e.activation(out=rsn[:, :n], in_=PS[4][:, :n], func=AF.Sqrt,
                                                       bias=self.epsc[:], scale=inv),
                         reads=[("ps", 4), "epsc"], writes=["rsn"])
                    S.op("dve", lambda e: e.reciprocal(out=rsn[:, :n], in_=rsn[:, :n]), reads=["rsn"], writes=["rsn"])
                    for jj in range(nj):
                        S.op("dve", lambda e, jj=jj: e.scalar_tensor_tensor(
                            out=dstn[:, jj, :n], in0=zc[:, j0 + jj, :n], scalar=sg[:, gc0 + jj:gc0 + jj + 1],
                            in1=rsn[:, :n], op0=ALU.mult, op1=ALU.mult), reads=["zc", "rsn", "sg"], writes=[dkey])
                for h in range(8):
                    bank = nbank()
                    for j in range(3):
                        S.op("pe", lambda e, j=j, h=h: e.matmul(PS[bank][:96, :n], lhsT=wuq[:, j, h * 96:(h + 1) * 96],
                                                                rhs=cqn[:, j, :n], start=(j == 0), stop=(j == 2)),
                             reads=["wuq", "cqn"], writes=[("ps", bank)])
                    xb, xk = evac_act(bank, 96, n, AF.Copy, xbr)
                    rope_store(96, n, ti, xb, xk, self.r96, "r96", tq, self.qmT[h, :, t0:t0 + n])
                for h in range(8):
                    bank = nbank()
                    for j in range(2):
                        S.op("pe", lambda e, j=j, h=h: e.matmul(PS[bank][:64, :n], lhsT=wkk[:, j, h, :],
                                                                rhs=ckvn[:, j, :n], start=(j == 0), stop=(j == 1)),
                             reads=["wkk", "ckvn"], writes=[("ps", bank)])
                    o, ok = evac_act(bank, 64, n, AF.Copy, stg_b)
                    store(self.kmT[h, 0:64, t0:t0 + n], o[:64, :n], ok)
                for sub in range(n // 128):
                    bank = nbank()
                    for j in range(2):
                        S.op("pe", lambda e, j=j, sub=sub: e.matmul(
                            PS[bank][:, :512], lhsT=ckvn[:, j, sub * 128:(sub + 1) * 128],
                            rhs=wkv[:, j].rearrange("p h d -> p (h d)"), start=(j == 0), stop=(j == 1)),
                            reads=["wkv", "ckvn"], writes=[("ps", bank)])
                    o, ok = evac_act(bank, 128, 512, AF.Copy, stg_b)
                    store(self.vm[t0 + sub * 128:t0 + (sub + 1) * 128, :], o[:, :], ok)
                bank = nbank()
                mm_chunk(bank, wmla, "wmla", 640, 32, ti)
                xb, xk = evac_act(bank, 32, n, AF.Copy, xbr)
                if ti == 0:
                    for h in range(8):
                        store(self.kmT[h, 64:96, t0:t0 + n], xb[:32, :n], xk)
                else:
                    Ct, St, ck, sk = tk_
                    o, ok = stg_b.next()
                    S.op("pe", lambda e: e.matmul(PS[3][:32, :n], lhsT=self.r32[:32, :32], rhs=xb[:32, :n],
                                                  start=True, stop=True), reads=[xk, "r32"], writes=[("ps", 3)])
                    t1, t1k = stg_f.next()
                    S.op("dve", lambda e: e.tensor_tensor(out=t1[:32, :n], in0=xb[:32, :n], in1=Ct[:32, :n],
                                                          op=ALU.mult), reads=[xk, ck], writes=[t1k])
                    t2, t2k = stg_f.next()
                    S.op("dve", lambda e: e.tensor_tensor(out=t2[:32, :n], in0=PS[3][:32, :n], in1=St[:32, :n],
                                                          op=ALU.mult), reads=[("ps", 3), sk], writes=[t2k])
                    S.op("pool", lambda e: e.tensor_tensor(out=o[:32, :n], in0=t1[:32, :n], in1=t2[:32, :n],
                                                           op=ALU.add), reads=[t1k, t2k], writes=[ok])
                    for h in range(8):
                        store(self.kmT[h, 64:96, t0:t0 + n], o[:32, :n], ok)
            S.barrier()

    def lru_phase(self, l):
        S, PS, dma = self.S, self.PS, self.dma
        with ExitStack() as st:
            xa = self.sb(st, "xa", [128, T], F32)
            u = self.sb(st, "u", [128, T], F32)
            ub = self.sb(st, "ub", [128, T], BF16)
            gy = self.sb(st, "gy", [128, T], BF16)
            at = self.sb(st, "at", [128, T], F32)
            bt = self.sb(st, "bt", [128, T], F32)
            tm = self.sb(st, "tm", [128, T], F32)
            hf = self.sb(st, "hf", [128, T], F32)
            hr = self.sb(st, "hr", [128, T], F32)
            ho = self.sb(st, "ho", [128, T], BF16)
            bdw = self.ring(st, "bdw", [128, 128], BF16, 4)
            cw = self.sb(st, "cw", [128, 4, 5], F32)
            lv = self.sb(st, "lv", [128, 2, 3, 4], F32)
            negk = self.sb(st, "negk", [128, 2, 4], F32)
            dma("sp", cw[:], self.convw[l], writes=["cw"])
            dma("sp", lv[:], self.lru_vec[l], writes=["lv"])
            for d in range(2):
                S.op("act", lambda e, d=d: e.activation(out=negk[:, d, :], in_=lv[:, d, 2, :], func=AF.Exp, scale=-1.0),
                     reads=["lv"], writes=["negk"])
                S.op("act", lambda e, d=d: e.activation(out=negk[:, d, :], in_=negk[:, d, :], func=AF.Ln, bias=1.0),
                     reads=["negk"], writes=["negk"])
                S.op("dve", lambda e, d=d: e.tensor_scalar(out=negk[:, d, :], in0=negk[:, d, :], scalar1=-8.0,
                                                           scalar2=None, op0=ALU.mult), reads=["negk"], writes=["negk"])
            segs = [(0, NCTX), (NCTX, T)]
            for cc in range(4):
                dma("sp", xa[:], self.xaT[cc * 128:(cc + 1) * 128, :], writes=["xa"])
                dma("sp", gy[:], self.yaT[cc * 128:(cc + 1) * 128, :], writes=["gy"])
                S.op("act", lambda e: e.activation(out=u[:], in_=xa[:], func=AF.Identity, scale=cw[:, cc, 1:2],
                                                   bias=cw[:, cc, 4:5]), reads=["xa", "cw"], writes=["u"])
                for (s0, e0) in segs:
                    for (tap, osl, isl) in ((0, (s0 + 1, e0), (s0, e0 - 1)), (2, (s0, e0 - 1), (s0 + 1, e0)),
                                            (3, (s0, e0 - 2), (s0 + 2, e0))):
                        S.op("dve", lambda e, tap=tap, osl=osl, isl=isl: e.scalar_tensor_tensor(
                            out=u[:, osl[0]:osl[1]], in0=xa[:, isl[0]:isl[1]], scalar=cw[:, cc, tap:tap + 1],
                            in1=u[:, osl[0]:osl[1]], op0=ALU.mult, op1=ALU.add), reads=["xa", "u", "cw"], writes=["u"])
                S.op("pool", lambda e: e.tensor_copy(out=ub[:], in_=u[:]), reads=["u"], writes=["ub"])
                for d in range(2):
                    wts = []
                    for kind in range(2):
                        wt, wk = bdw.next()
                        dma("pool", wt[:], self.lru_bd[l, d, kind, cc], writes=[wk])
                        wts.append((wt, wk))
                    for ti, (t0, n) in enumerate(TILES):
                        for kind in range(2):
                            bank = (ti * 2 + kind) % 4
                            wt, wk = wts[kind]
                            S.op("pe", lambda e, wt=wt: e.matmul(PS[bank][:, :n], lhsT=wt[:], rhs=ub[:, t0:t0 + n],
                                                                 start=True, stop=True),
                                 reads=[wk, "ub"], writes=[("ps", bank)])
                            dst, dk = (at, "at") if kind == 0 else (bt, "bt")
                            S.op("act", lambda e, dst=dst, kind=kind: e.activation(
                                out=dst[:, t0:t0 + n], in_=PS[bank][:, :n], func=AF.Sigmoid,
                                bias=lv[:, d, kind, cc:cc + 1], scale=1.0), reads=[("ps", bank), "lv"], writes=[dk])
                    S.op("act", lambda e: e.activation(out=at[:], in_=at[:], func=AF.Exp, scale=negk[:, d, cc:cc + 1]),
                         reads=["at", "negk"], writes=["at"])
                    S.op("pool", lambda e: e.tensor_tensor(out=tm[:], in0=at[:], in1=at[:], op=ALU.mult),
                         reads=["at"], writes=["tm"])
                    S.op("act", lambda e: e.activation(out=tm[:], in_=tm[:], func=AF.Sqrt, scale=-1.0, bias=1.0),
                         reads=["tm"], writes=["tm"])
                    S.op("dve", lambda e: e.tensor_tensor(out=bt[:], in0=bt[:], in1=u[:], op=ALU.mult),
                         reads=["bt", "u"], writes=["bt"])
                    S.op("dve", lambda e: e.tensor_tensor(out=bt[:], in0=bt[:], in1=tm[:], op=ALU.mult),
                         reads=["bt", "tm"], writes=["bt"])
                    if d == 0:
                        pieces = [(0, 1088), (1088, 2176), (2176, 3264), (3264, T)]
                        for pi, (p0, p1) in enumerate(pieces):
                            init = 0.0 if pi == 0 else hf[:, p0 - 1:p0]
                            S.op("dve", lambda e, p0=p0, p1=p1, init=init: e.tensor_tensor_scan(
                                out=hf[:, p0:p1], data0=at[:, p0:p1], data1=bt[:, p0:p1], initial=init,
                                op0=ALU.mult, op1=ALU.add), reads=["at", "bt", "hf"], writes=["hf"])
                    else:
                        S.op("dve", lambda e: e.tensor_tensor_scan(
                            out=hr[:, 0:NCTX][:, ::-1], data0=at[:, 0:NCTX][:, ::-1], data1=bt[:, 0:NCTX][:, ::-1],
                            initial=0.0, op0=ALU.mult, op1=ALU.add), reads=["at", "bt"], writes=["hr"])
                        pieces = [(3328, T), (2304, 3328), (1280, 2304), (NCTX, 1280)]
                        for pi, (p0, p1) in enumerate(pieces):
                            init = hr[:, 0:1] if pi == 0 else hr[:, p1:p1 + 1]
                            S.op("dve", lambda e, p0=p0, p1=p1, init=init: e.tensor_tensor_scan(
                                out=hr[:, p0:p1][:, ::-1], data0=at[:, p0:p1][:, ::-1], data1=bt[:, p0:p1][:, ::-1],
                                initial=init, op0=ALU.mult, op1=ALU.add), reads=["at", "bt", "hr"], writes=["hr"])
                S.op("pool", lambda e: e.tensor_tensor(out=hf[:], in0=hf[:], in1=hr[:], op=ALU.add),
                     reads=["hf", "hr"], writes=["hf"])
                S.op("dve", lambda e: e.tensor_tensor(out=ho[:], in0=hf[:], in1=gy[:], op=ALU.mult),
                     reads=["hf", "gy"], writes=["ho"])
                dma("sp", self.oT[0, cc * 128:(cc + 1) * 128, :], ho[:], reads=["ho"])
            S.barrier()

    def attn_phase(self, l, need_ctx, lam_init):
        S, PS, dma = self.S, self.PS, self.dma
        with ExitStack() as st:
            kT = self.ring(st, "kT", [128, T], BF16, 2)
            vt = self.ring(st, "vt", [128, NKB, 128], BF16, 2)
            qt = self.ring(st, "qt", [128, 512], BF16, 2)
            pT = self.ring(st, "pT", [128, 2, 512], BF16, 3)
            rec = self.ring(st, "rec", [128, 512], F32, 2)
            of = self.ring(st, "of", [128, 512], F32, 3)
            ob = self.ring(st, "ob", [128, 512], BF16, 2)
            sqd = self.sb(st, "sqd", [128, 512], F32)
            rsd = self.sb(st, "rsd", [128, 512], F32)
            sg = self.sb(st, "sga", [128, 8], F32)
            dl = self.sb(st, "dl", [128, 4, 64], F32)
            lam = self.sb(st, "lam", [128, 4], F32)
            dma("sp", sg[:], self.smallg[l], writes=["sga"])
            dma("sp", dl[:], self.diff_lam[l], writes=["dl"])
            for i in range(2):
                S.op("dve", lambda e, i=i: e.tensor_tensor(out=dl[:, 2 * i, :], in0=dl[:, 2 * i, :],
                                                           in1=dl[:, 2 * i + 1, :], op=ALU.mult), reads=["dl"], writes=["dl"])
                S.op("dve", lambda e, i=i: e.tensor_reduce(out=lam[:, i:i + 1], in_=dl[:, 2 * i, :], axis=AX.X, op=ALU.add),
                     reads=["dl"], writes=["lam"])
            S.op("act", lambda e: e.activation(out=lam[:, 0:2], in_=lam[:, 0:2], func=AF.Exp), reads=["lam"], writes=["lam"])
            S.op("dve", lambda e: e.tensor_tensor(out=lam[:, 2:3], in0=lam[:, 1:2], in1=lam[:, 0:1], op=ALU.subtract),
                 reads=["lam"], writes=["lam"])
            S.op("dve", lambda e: e.tensor_scalar(out=lam[:, 2:3], in0=lam[:, 2:3], scalar1=-lam_init, scalar2=None,
                                                  op0=ALU.add), reads=["lam"], writes=["lam"])
            S.op("dve", lambda e: e.tensor_scalar(out=lam[:, 3:4], in0=sg[:, 0:1], scalar1=1.0 - lam_init, scalar2=None,
                                                  op0=ALU.mult), reads=["sga", "lam"], writes=["lam"])

            qgroups = ([(0, 256, 2)] if need_ctx else []) + [(256 + 512 * i, 512, NKB) for i in range(8)]

            def run_job(kt, kk, vtile, vk, dv, qsrc_fn, variants, scale, finish):
                PSW = self.PSW
                for (q0, n, nkb) in qgroups:
                    q, qk = qt.next()
                    qsrc_fn(q, qk, q0, n)
                    nv = len(variants)
                    seq = [(kb, v) for kb in range(nkb) for v in range(nv)]
                    pairs = [seq[i:i + 2] for i in range(0, len(seq), 2)]

                    def emit_qk_pair(pi):
                        w = pi % 2
                        for half, (kb, v) in enumerate(pairs[pi]):
                            p0, pn = variants[v]
                            bank = 2 * w + half
                            S.op("pe", lambda e, kb=kb, p0=p0, pn=pn, bank=bank: e.matmul(
                                PS[bank][:, :n], lhsT=kt[p0:p0 + pn, kb * 128:(kb + 1) * 128], rhs=q[p0:p0 + pn, :n],
                                start=True, stop=True), reads=[kk, qk], writes=[("ps", bank)])

                    emit_qk_pair(0)
                    for pi in range(len(pairs)):
                        w = pi % 2
                        p, pk = pT.next()
                        S.op("act", lambda e, p=p, w=w: e.activation(
                            out=p[:, :, :n], in_=PSW[w][:].rearrange("p (b x) -> p b x", b=2)[:, :, :n],
                            func=AF.Exp, scale=scale), reads=[("ps", 2 * w), ("ps", 2 * w + 1)], writes=[pk])
                        if pi + 1 < len(pairs):
                            emit_qk_pair(pi + 1)
                        for half, (kb, v) in enumerate(pairs[pi]):
                            S.op("pe", lambda e, p=p, kb=kb, v=v, half=half: e.matmul(
                                PS[4 + v][:, :n], lhsT=vtile[:, kb, :], rhs=p[:, half, :n],
                                start=(kb == 0), stop=(kb == nkb - 1)), reads=[pk, vk], writes=[("ps", 4 + v)])
                            if dv == 128:
                                S.op("pe", lambda e, p=p, kb=kb, v=v, half=half: e.matmul(
                                    PS[6 + v][:, :n], lhsT=self.ones_b[:, :], rhs=p[:, half, :n],
                                    start=(kb == 0), stop=(kb == nkb - 1)), reads=[pk, "ones_b"], writes=[("ps", 6 + v)])
                    finish(q0, n)

            def load_v(src_ap, dv):
                vtile, vk = vt.next()
                if dv == 64:
                    S.op("pool", lambda e: e.memset(vtile[:, :, 64:128], 1.0), writes=[vk])
                dma("sp", vtile[:, :, :dv], src_ap.rearrange("(kb p) d -> p kb d", p=128), reads=[vk], writes=[vk])
                return vtile, vk

            def normalized(v, dv, n):
                r, rk = rec.next()
                S.op("dve", lambda e: e.reciprocal(out=r[:dv, :n], in_=PS[6 + v][:dv, :n]), reads=[("ps", 6 + v)], writes=[rk])
                o, ok = of.next()
                S.op("dve", lambda e: e.tensor_tensor(out=o[:dv, :n], in0=PS[4 + v][:dv, :n], in1=r[:dv, :n], op=ALU.mult),
                     reads=[("ps", 4 + v), rk], writes=[ok])
                return o, ok

            for h in range(4):
                kt, kk = kT.next()
                dma("sp", kt[:], self.kbT[h * 128:(h + 1) * 128, :], writes=[kk])
                vtile, vk = load_v(self.vb[:, h * 128:(h + 1) * 128], 128)

                def qsrc(q, qk, q0, n, h=h):
                    dma("sp", q[:, :n], self.qbT[h * 128:(h + 1) * 128, q0:q0 + n], writes=[qk])

                def finish(q0, n, h=h):
                    o1, o1k = normalized(0, 128, n)
                    o2, o2k = normalized(1, 128, n)
                    od, odk = of.next()
                    S.op("dve", lambda e: e.scalar_tensor_tensor(out=od[:, :n], in0=o2[:, :n], scalar=lam[:, 2:3],
                                                                 in1=o1[:, :n], op0=ALU.mult, op1=ALU.add),
                         reads=[o1k, o2k, "lam"], writes=[odk])
                    S.op("act", lambda e: e.activation(out=sqd[:, :n], in_=od[:, :n], func=AF.Square),
                         reads=[odk], writes=["sqd"])
                    S.op("pe", lambda e: e.matmul(PS[7][:, :n], lhsT=self.ones_f[:], rhs=sqd[:, :n], start=True, stop=True),
                         reads=["sqd", "ones_f"], writes=[("ps", 7)])
                    S.op("act", lambda e: e.activation(out=rsd[:, :n], in_=PS[7][:, :n], func=AF.Sqrt,
                                                       bias=self.epsc[:], scale=1.0 / 128),
                         reads=[("ps", 7), "epsc"], writes=["rsd"])
                    S.op("dve", lambda e: e.reciprocal(out=rsd[:, :n], in_=rsd[:, :n]), reads=["rsd"], writes=["rsd"])
                    o, ok = ob.next()
                    S.op("dve", lambda e: e.scalar_tensor_tensor(out=o[:, :n], in0=od[:, :n], scalar=lam[:, 3:4],
                                                                 in1=rsd[:, :n], op0=ALU.mult, op1=ALU.mult),
                         reads=[odk, "rsd", "lam"], writes=[ok])
                    dma("sp", self.oT[1, h * 128:(h + 1) * 128, q0:q0 + n], o[:, :n], reads=[ok])

                run_job(kt, kk, vtile, vk, 128, qsrc, [(0, 64), (64, 64)], 0.125, finish)

            def simple_finish(branch, h):
                def fin(q0, n):
                    r, rk = rec.next()
                    S.op("dve", lambda e: e.reciprocal(out=r[64:128, :n], in_=PS[4][64:128, :n]), reads=[("ps", 4)], writes=[rk])
                    o, ok = ob.next()
                    S.op("dve", lambda e: e.tensor_tensor(out=o[:64, :n], in0=PS[4][0:64, :n], in1=r[64:128, :n], op=ALU.mult),
                         reads=[("ps", 4), rk], writes=[ok])
                    dma("sp", self.oT[branch, h * 64:(h + 1) * 64, q0:q0 + n], o[:64, :n], reads=[ok])
                return fin

            for h in range(8):
                g = h // 4
                kt, kk = kT.next()
                dma("sp", kt[0:64, :], self.kgT[g * 64:(g + 1) * 64, :], writes=[kk])
                vtile, vk = load_v(self.vg[:, g * 64:(g + 1) * 64], 64)

                def qsrc(q, qk, q0, n, h=h):
                    dma("sp", q[0:64, :n], self.qgT[h * 64:(h + 1) * 64, q0:q0 + n], writes=[qk])

                run_job(kt, kk, vtile, vk, 64, qsrc, [(0, 64)], 0.125, simple_finish(2, h))
            for h in range(8):
                kt, kk = kT.next()
                dma("sp", kt[0:96, :], self.kmT[h], writes=[kk])
                vtile, vk = load_v(self.vm[:, h * 64:(h + 1) * 64], 64)

                def qsrc(q, qk, q0, n, h=h):
                    dma("sp", q[0:96, :n], self.qmT[h, :, q0:q0 + n], writes=[qk])

                run_job(kt, kk, vtile, vk, 64, qsrc, [(0, 96)], 96.0 ** -0.5, simple_finish(3, h))
            S.barrier()

    def merge_phase(self, l, need_ctx, xcur):
        S, PS, dma = self.S, self.PS, self.dma
        with ExitStack() as st:
            wb = self.sb(st, "wb", [128, 16, 1024], BF16)
            wo = self.sb(st, "wo", [128, 8, 1024], BF16)
            otr = self.ring(st, "otr", [128, 16, 512], BF16, 2)
            gtr = self.ring(st, "gtr", [128, 512], BF16, 4)
            macc = self.sb(st, "macc", [128, 8, 512], F32)
            mtmp = self.ring(st, "mtmp", [128, 512], F32, 2)
            mb = self.sb(st, "mb", [128, 8, 512], BF16)
            xr = self.ring(st, "xr2", [128, 8, 512], F32, 2)
            sq = self.sb(st, "sq2", [128, 8, 512], F32)
            rs = self.sb(st, "rs2", [128, 512], F32)
            tmpr = self.ring(st, "tmpr2", [128, 512], F32, 2)
            hst = self.ring(st, "hst", [128, 8, 512], BF16, 2)
            for k in range(4):
                dma("pool", wb[:, k * 4:(k + 1) * 4, :], self.w_br[l, k].rearrange("(c p) d -> p c d", p=128), writes=["wb"])
            dma("pool", wo[:], self.w_out[l].rearrange("(k p) d -> p k d", p=128), writes=["wo"])
            self.make_A(2 + l, 4)
            osrc = self.oT.rearrange("k (c p) t -> p k c t", p=128)
            xsrc = xcur.rearrange("(k p) t -> p k t", p=128)
            xdst = self.xT_a.rearrange("(k p) t -> p k t", p=128)
            hdst = self.h2T.rearrange("(k p) t -> p k t", p=128)
            tiles = list(enumerate(TILES)) if need_ctx else list(enumerate(TILES))[1:]
            bc = 0
            for ti, (t0, n) in tiles:
                s = 1 if ti == 0 else 0
                ot, otk = otr.next()
                for k in range(4):
                    dma("sp", ot[:, k * 4:(k + 1) * 4, :n], osrc[:, k, :, t0:t0 + n], writes=[otk])
                xt, xk = xr.next()
                dma("sp", xt[:, :, :n], xsrc[:, :, t0:t0 + n], writes=[xk])
                for j in range(8):
                    for k in range(4):
                        gt, gk = gtr.next()
                        r0 = k * 1024 + j * 128
                        dma("sp", gt[:, :n], self.gT[r0:r0 + 128, t0:t0 + n], writes=[gk])
                        bank = bc % 3
                        bc += 1
                        for c in range(4):
                            S.op("pe", lambda e, c=c, k=k, j=j: e.matmul(
                                PS[bank][:, :n], lhsT=wb[:, k * 4 + c, j * 128:(j + 1) * 128], rhs=ot[:, k * 4 + c, :n],
                                start=(c == 0), stop=(c == 3)), reads=["wb", otk], writes=[("ps", bank)])
                        if k == 0:
                            S.op("dve", lambda e, j=j, gt=gt: e.tensor_tensor(out=macc[:, j, :n], in0=PS[bank][:, :n],
                                                                              in1=gt[:, :n], op=ALU.mult),
                                 reads=[("ps", bank), gk], writes=[("macc", j)])
                        else:
                            mt, mk = mtmp.next()
                            S.op("dve", lambda e, gt=gt, mt=mt: e.tensor_tensor(out=mt[:, :n], in0=PS[bank][:, :n],
                                                                                in1=gt[:, :n], op=ALU.mult),
                                 reads=[("ps", bank), gk], writes=[mk])
                            S.op("pool", lambda e, j=j, mt=mt: e.tensor_tensor(out=macc[:, j, :n], in0=macc[:, j, :n],
                                                                               in1=mt[:, :n], op=ALU.add),
                                 reads=[mk, ("macc", j)], writes=[("macc", j)])
                    S.op("act", lambda e, j=j: e.copy(out=mb[:, j, :n], in_=macc[:, j, :n]),
                         reads=[("macc", j)], writes=[("mb", j)])
                for jo in range(8):
                    bank = 3 + (jo % 3)
                    for j in range(8):
                        S.op("pe", lambda e, j=j, jo=jo: e.matmul(PS[bank][:, :n], lhsT=wo[:, j, jo * 128:(jo + 1) * 128],
                                                                  rhs=mb[:, j, :n], start=(j == 0), stop=(j == 7)),
                             reads=["wo"] + [("mb", jj) for jj in range(8)], writes=[("ps", bank)])
                    S.op("dve", lambda e, jo=jo: e.scalar_tensor_tensor(
                        out=xt[:, jo, :n], in0=PS[bank][:, :n], scalar=self.modT[:, 16 + jo, s:s + 1],
                        in1=xt[:, jo, :n], op0=ALU.mult, op1=ALU.add), reads=[("ps", bank), xk, "modT"], writes=[xk])
                dma("sp", xdst[:, :, t0:t0 + n], xt[:, :, :n], reads=[xk])
                hs, hk = hst.next()
                self.adaln_tile(xt, xk, n, lambda k, s=s: self.A1[:, k, s:s + 1],
                                lambda k, s=s: self.modT[:, 24 + k, s:s + 1],
                                lambda k, hs=hs, n=n: hs[:, k, :n], hk, sq, rs, tmpr, 7)
                dma("sp", hdst[:, :, t0:t0 + n], hs[:, :, :n], reads=[hk])
            S.barrier()

    def peer_phase(self, l, need_ctx, final):
        S, PS, dma = self.S, self.PS, self.dma
        with ExitStack() as st:
            sb = lambda name, shape, dt: self.sb(st, name, shape, dt)
            wq = sb("wq", [128, 8, 1024], BF16)
            kbd = sb("kbd", [128, 8, 256], F32)
            h2r = self.ring(st, "h2r", [128, 8, PT], BF16, 2)
            qTh = sb("qTh", [128, 8, PT], F32)
            sc = sb("sc", [128, 8, 256], F32)
            v8 = sb("v8", [128, 8, 2, 16], F32)
            i8u = sb("i8u", [128, 8, 2, 16], U32)
            i8f = sb("i8f", [128, 8, 2, 16], F32)
            cand = sb("cand", [128, 8, 256], F32)
            cand2 = sb("cand2", [128, 8, 256], F32)
            wk1 = cand2
            best = sb("best", [128, 8, 16], F32)
            posu = sb("posu", [128, 8, 16], U32)
            posf = sb("posf", [128, 8, 16], F32)
            af = sb("af", [128, 8, 16], F32)
            bf = sb("bf", [128, 8, 16], F32)
            oh = sb("oh", [128, 128, 16], F32)
            iw = sb("iw", [128, 128], F32)
            jw = sb("jw", [128, 128], F32)
            gw = sb("gw", [128, 8, 16], F32)
            zs = sb("zs", [128, 8], F32)
            iT = sb("iT", [128, PT], F32)
            jT = sb("jT", [128, PT], F32)
            gTt = sb("gTt", [128, PT], F32)
            lhr = self.ring(st, "lhr", [128, TB, 128], BF16, 2)
            rhr = self.ring(st, "rhr", [128, TB, 128], BF16, 2)
            GT = sb("GT", [128, 128, PT], BF16)
            usr = self.ring(st, "usr", [128, 8, CG, 128], BF16, 2)
            vsr = self.ring(st, "vsr", [128, CG, 1024], BF16, 2)
            agr = self.ring(st, "agr", [128, PT], BF16, 3)
            mgr = self.ring(st, "mgr", [128, PT], BF16, 3)
            xt = sb("xtp", [128, 8, PT], F32)
            sq = cand2
            rs = sb("rs3", [128, PT], F32)
            tmpr = self.ring(st, "tmpr3", [128, PT], F32, 2)
            ost = cand
            dma("pool", wq[:], self.w_pq[l].rearrange("(k p) d -> p k d", p=128), writes=["wq"])
            dma("sp", kbd[:], self.keysbd[l].rearrange("h p n -> p h n"), writes=["kbd"])
            usrc = self.ubf[l].rearrange("(g p) (k c i) -> g p k c i", p=128, k=8, c=CG)
            vsrc = self.vbf[l].rearrange("(c i) d -> i c d", i=128)
            hsrc = self.h2T.rearrange("(k p) t -> p k t", p=128)
            xsrc = self.xT_a.rearrange("(k p) t -> p k t", p=128)
            tiles = PTILES if need_ctx else PTILES[1:]
            iota16 = self.iota_f[:, 0:16]
            iT2 = [iT, sb("iTb", [128, PT], F32)]
            jT2 = [jT, sb("jTb", [128, PT], F32)]
            gT2 = [gTt, sb("gTtb", [128, PT], F32)]
            h2of = {}

            def scores_topk(idx, sub):
                t0, n = tiles[idx]
                tsl = slice(sub * 128, (sub + 1) * 128)
                for h in range(8):
                    bank = 4 + (h % 2)
                    S.op("pe", lambda e, h=h: e.matmul(PS[bank][:, :256], lhsT=qTh[:, h, tsl], rhs=kbd[:, h, :],
                                                       start=True, stop=True),
                         reads=[("qTh", h), "kbd"], writes=[("ps", bank)])
                    S.op("act", lambda e, h=h: e.copy(out=sc[:, h, :], in_=PS[bank][:, :256]),
                         reads=[("ps", bank)], writes=[("sc", h)])
                for h in range(8):
                    for p in range(2):
                        src = sc[:, h, p * 128:(p + 1) * 128]
                        wks = wk1[:, h, p * 128:(p + 1) * 128]
                        S.op("dve", lambda e, h=h, p=p, src=src: e.max(out=v8[:, h, p, 0:8], in_=src),
                             reads=[("sc", h)], writes=["v8"])
                        S.op("dve", lambda e, h=h, p=p, src=src: e.max_index(out=i8u[:, h, p, 0:8], in_max=v8[:, h, p, 0:8],
                                                                              in_values=src),
                             reads=[("sc", h), "v8"], writes=["i8u"])
                        S.op("dve", lambda e, h=h, p=p, src=src, wks=wks: e.match_replace(
                            out=wks, in_to_replace=v8[:, h, p, 0:8], in_values=src, imm_value=-1e30),
                            reads=[("sc", h), "v8"], writes=["cand2"])
                        S.op("dve", lambda e, h=h, p=p, wks=wks: e.max(out=v8[:, h, p, 8:16], in_=wks),
                             reads=["cand2"], writes=["v8"])
                        S.op("dve", lambda e, h=h, p=p, wks=wks: e.max_index(out=i8u[:, h, p, 8:16],
                                                                             in_max=v8[:, h, p, 8:16], in_values=wks),
                             reads=["cand2", "v8"], writes=["i8u"])
                S.op("dve", lambda e: e.tensor_copy(out=i8f[:], in_=i8u[:]), reads=["i8u"], writes=["i8f"])
                S.op("dve", lambda e: e.tensor_tensor(
                    out=cand[:].rearrange("p h (a b) -> p h a b", b=16),
                    in0=v8[:, :, 0, :].unsqueeze(3).to_broadcast([128, 8, 16, 16]),
                    in1=v8[:, :, 1, :].unsqueeze(2).to_broadcast([128, 8, 16, 16]), op=ALU.add),
                    reads=["v8"], writes=["cand"])
                for h in range(8):
                    S.op("dve", lambda e, h=h: e.max(out=best[:, h, 0:8], in_=cand[:, h, :]), reads=["cand"], writes=["best"])
                    S.op("dve", lambda e, h=h: e.max_index(out=posu[:, h, 0:8], in_max=best[:, h, 0:8], in_values=cand[:, h, :]),
                         reads=["cand", "best"], writes=["posu"])
                    S.op("dve", lambda e, h=h: e.match_replace(out=cand2[:, h, :], in_to_replace=best[:, h, 0:8],
                                                               in_values=cand[:, h, :], imm_value=-1e30),
                         reads=["cand", "best"], writes=["cand2"])
                    S.op("dve", lambda e, h=h: e.max(out=best[:, h, 8:16], in_=cand2[:, h, :]), reads=["cand2"], writes=["best"])
                    S.op("dve", lambda e, h=h: e.max_index(out=posu[:, h, 8:16], in_max=best[:, h, 8:16], in_values=cand2[:, h, :]),
                         reads=["cand2", "best"], writes=["posu"])
                S.op("dve", lambda e: e.tensor_single_scalar(out=posf[:].bitcast(U32), in_=posu[:], scalar=4,
                                                             op=ALU.logical_shift_right), reads=["posu"], writes=["posf"])
                S.op("dve", lambda e: e.tensor_copy(out=af[:], in_=posf[:].bitcast(U32)), reads=["posf"], writes=["af"])
                S.op("dve", lambda e: e.tensor_single_scalar(out=posf[:].bitcast(U32), in_=posu[:], scalar=15,
                                                             op=ALU.bitwise_and), reads=["posu", "af"], writes=["posf"])
                S.op("dve", lambda e: e.tensor_copy(out=bf[:], in_=posf[:].bitcast(U32)), reads=["posf"], writes=["bf"])
                for (src, p, dst, dk) in ((af, 0, iw, "iw"), (bf, 1, jw, "jw")):
                    S.op("dve", lambda e, src=src: e.tensor_tensor(
                        out=oh[:], in0=src[:].rearrange("p h k -> p (h k)").unsqueeze(2).to_broadcast([128, 128, 16]),
                        in1=iota16.unsqueeze(1).to_broadcast([128, 128, 16]), op=ALU.is_equal),
                        reads=["af", "bf", "iota_f"], writes=["oh"])
                    S.op("dve", lambda e, p=p: e.tensor_tensor(
                        out=oh[:].rearrange("p (h k) a -> p h k a", k=16),
                        in0=oh[:].rearrange("p (h k) a -> p h k a", k=16),
                        in1=i8f[:, :, p, :].unsqueeze(2).to_broadcast([128, 8, 16, 16]), op=ALU.mult),
                        reads=["oh", "i8f"], writes=["oh"])
                    S.op("dve", lambda e, dst=dst: e.tensor_reduce(out=dst[:], in_=oh[:], axis=AX.X, op=ALU.add),
                         reads=["oh"], writes=[dk])
                S.op("dve", lambda e: e.tensor_tensor(out=gw[:], in0=best[:],
                                                      in1=best[:, :, 0:1].to_broadcast([128, 8, 16]), op=ALU.subtract),
                     reads=["best"], writes=["gw"])
                S.op("act", lambda e: e.activation(out=gw[:], in_=gw[:], func=AF.Exp), reads=["gw"], writes=["gw"])
                S.op("dve", lambda e: e.tensor_reduce(out=zs[:], in_=gw[:], axis=AX.X, op=ALU.add), reads=["gw"], writes=["zs"])
                S.op("dve", lambda e: e.reciprocal(out=zs[:], in_=zs[:]), reads=["zs"], writes=["zs"])
                S.op("dve", lambda e: e.tensor_tensor(out=gw[:], in0=gw[:], in1=zs[:].unsqueeze(2).to_broadcast([128, 8, 16]),
                                                      op=ALU.mult), reads=["gw", "zs"], writes=["gw"])

            def transposes(idx, sub):
                par = idx % 2
                tsl = slice(sub * 128, (sub + 1) * 128)
                for (src_ap, sk, dstT, dk, bank) in ((iw[:], "iw", iT2[par], ("iT", par), 4), (jw[:], "jw", jT2[par], ("jT", par), 5),
                                                     (gw[:].rearrange("p h k -> p (h k)"), "gw", gT2[par], ("gTt", par), 4)):
                    S.op("pe", lambda e, src_ap=src_ap, bank=bank: e.transpose(out=PS[bank][:, :128], in_=src_ap,
                                                                               identity=self.ident[:]),
                         reads=[sk, "ident"], writes=[("ps", bank)])
                    S.op("act", lambda e, dstT=dstT, bank=bank: e.copy(out=dstT[:, tsl], in_=PS[bank][:, :128]),
                         reads=[("ps", bank)], writes=[dk])

            def stageA(idx, part):
                t0, n = tiles[idx]
                if part == 0:
                    h2, h2k = h2r.next()
                    h2of[idx] = (h2, h2k)
                    dma("sp", h2[:], hsrc[:, :, t0:t0 + n], writes=[h2k])
                    for h in range(8):
                        bank = 4 + (h % 2)
                        for k in range(8):
                            S.op("pe", lambda e, h=h, k=k: e.matmul(PS[bank][:, :n], lhsT=wq[:, k, h * 128:(h + 1) * 128],
                                                                    rhs=h2[:, k, :], start=(k == 0), stop=(k == 7)),
                                 reads=["wq", h2k], writes=[("ps", bank)])
                        S.op("act", lambda e, h=h: e.copy(out=qTh[:, h, :], in_=PS[bank][:, :n]),
                             reads=[("ps", bank)], writes=[("qTh", h)])
                    scores_topk(idx, 0)
                elif part == 1:
                    transposes(idx, 0)
                    scores_topk(idx, 1)
                else:
                    transposes(idx, 1)

            for part in range(3):
                stageA(0, part)
            hooks = {4: 0, 24: 1, 48: 2}
            for idx, (t0, n) in enumerate(tiles):
                s = 1 if t0 < NCTX else 0
                par = idx % 2
                iTc, jTc, gTc = iT2[par], jT2[par], gT2[par]
                h2, h2k = h2of.pop(idx)
                dma("sp", xt[:], xsrc[:, :, t0:t0 + n], writes=["xtp"])
                iota3 = self.iota_f[:].unsqueeze(1).to_broadcast([128, TB, 128])
                for tb in range(n // TB):
                    tk0 = tb * TB
                    lh, lk = lhr.next()
                    rh, rk = rhr.next()
                    S.op("dve", lambda e, lh=lh, tk0=tk0: e.tensor_tensor(
                        out=lh[:], in0=iota3, in1=iTc[:, tk0:tk0 + TB].unsqueeze(2).to_broadcast([128, TB, 128]),
                        op=ALU.is_equal), reads=[("iT", par), "iota_f"], writes=[lk])
                    S.op("dve", lambda e, lh=lh, tk0=tk0: e.tensor_tensor(
                        out=lh[:], in0=lh[:], in1=gTc[:, tk0:tk0 + TB].unsqueeze(2).to_broadcast([128, TB, 128]),
                        op=ALU.mult), reads=[lk, ("gTt", par)], writes=[lk])
                    S.op("dve", lambda e, rh=rh, tk0=tk0: e.tensor_tensor(
                        out=rh[:], in0=iota3, in1=jTc[:, tk0:tk0 + TB].unsqueeze(2).to_broadcast([128, TB, 128]),
                        op=ALU.is_equal), reads=[("jT", par), "iota_f"], writes=[rk])
                    for q4 in range(TB // 4):
                        tg = tb * (TB // 4) + q4
                        bank = 4 + (tg % 2)
                        for tt in range(4):
                            t = q4 * 4 + tt
                            S.op("pe", lambda e, lh=lh, rh=rh, tt=tt, t=t: e.matmul(
                                PS[bank][:, tt * 128:(tt + 1) * 128], lhsT=lh[:, t, :], rhs=rh[:, t, :],
                                start=True, stop=True), reads=[lk, rk], writes=[("ps", bank)])
                        S.op("act", lambda e, tg=tg, bank=bank: e.copy(
                            out=GT[:, :, tg * 4:(tg + 1) * 4].rearrange("p j t -> p t j"),
                            in_=PS[bank][:, :].rearrange("p (t j) -> p t j", j=128)), reads=[("ps", bank)], writes=["GT"])
                slabs = {}

                def emit_A(c):
                    cg, cl = divmod(c, CG)
                    if cl == 0:
                        if cg in hooks and idx + 1 < len(tiles):
                            stageA(idx + 1, hooks[cg])
                        us, uk = usr.next()
                        vs, vk = vsr.next()
                        dma("sp", us[:], usrc[cg], writes=[uk])
                        dma("sp", vs[:], vsrc[:, cg * CG:(cg + 1) * CG, :], writes=[vk])
                        slabs[cg] = (us, uk, vs, vk)
                    us, uk, vs, vk = slabs[cg]
                    bank = 6 + (c % 2)
                    for k in range(8):
                        S.op("pe", lambda e, k=k: e.matmul(PS[bank][:, :n], lhsT=us[:, k, cl, :], rhs=h2[:, k, :],
                                                           start=(k == 0), stop=(k == 7)),
                             reads=[uk, h2k], writes=[("ps", bank)])

                emit_A(0)
                for c in range(128):
                    cg, cl = divmod(c, CG)
                    us, uk, vs, vk = slabs[cg]
                    bank = 6 + (c % 2)
                    ag, agk = agr.next()
                    S.op("act", lambda e, ag=ag, bank=bank: e.activation(out=ag[:], in_=PS[bank][:, :n],
                                                                         func=AF.Gelu_apprx_tanh),
                         reads=[("ps", bank)], writes=[agk])
                    mg, mgk = mgr.next()
                    S.op("pool", lambda e, ag=ag, mg=mg, c=c: e.tensor_tensor(out=mg[:], in0=ag[:], in1=GT[:, c, :],
                                                                              op=ALU.mult), reads=[agk, "GT"], writes=[mgk])
                    if c + 1 < 128:
                        emit_A(c + 1)
                    for dc in range(8):
                        S.op("pe", lambda e, dc=dc, cl=cl, mg=mg, c=c, vs=vs: e.matmul(
                            PS[dc // 2][:, (dc % 2) * PT:(dc % 2 + 1) * PT], lhsT=vs[:, cl, dc * 128:(dc + 1) * 128],
                            rhs=mg[:], start=(c == 0), stop=(c == 127)), reads=[vk, mgk], writes=[("pso", dc)])
                    if cl == CG - 1:
                        slabs.pop(cg)
                for dc in range(8):
                    S.op("dve", lambda e, dc=dc: e.scalar_tensor_tensor(
                        out=xt[:, dc, :], in0=PS[dc // 2][:, (dc % 2) * PT:(dc % 2 + 1) * PT],
                        scalar=self.modT[:, 40 + dc, s:s + 1], in1=xt[:, dc, :], op0=ALU.mult, op1=ALU.add),
                        reads=[("pso", dc), "xtp", "modT"], writes=["xtp"])
                if not final:
                    dma("sp", xsrc[:, :, t0:t0 + n], xt[:], reads=["xtp"])
                else:
                    self.adaln_tile(xt, "xtp", n, lambda k: self.ngs[:, 4, k:k + 1], lambda k: None,
                                    lambda k: ost[:, k, :], "cand", sq, rs, tmpr, 6, sqkey="cand2")
                    dma("sp", self.outT.rearrange("(k p) t -> p k t", p=128)[:, :, t0 - NCTX:t0 - NCTX + n], ost[:],
                        reads=["cand"])
            S.barrier()


def _lay(v):
    return np.ascontiguousarray(np.asarray(v, np.float32).reshape(-1, 128).T)


def _rope_tables():
    t = np.arange(NLAT)
    r = (t // 64).astype(np.float32)
    c = (t % 64).astype(np.float32)

    def tab(dim):
        nf = dim // 4
        inv = (np.float32(10000.0) ** (-np.arange(nf, dtype=np.float32) / np.float32(nf))).astype(np.float32)
        ang = np.concatenate([r[:, None] * inv[None, :], c[:, None] * inv[None, :]], axis=1).astype(np.float32)
        return np.cos(ang).astype(np.float32), np.sin(ang).astype(np.float32)

    c64, s64 = tab(64)
    c32, s32 = tab(32)
    rope64 = np.zeros((2, 128, NLAT), np.float32)
    for row in range(128):
        rope64[0, row] = c64[:, row % 32]
        rope64[1, row] = s64[:, row % 32]
    ropem = np.zeros((2, 96, NLAT), np.float32)
    ropem[0, :64] = 1.0
    for row in range(32):
        ropem[0, 64 + row] = c32[:, row % 16]
        ropem[1, 64 + row] = s32[:, row % 16]
    rm = np.zeros((3, 128, 128), np.float32)
    for B in (0, 64):
        for m in range(32):
            rm[0, B + m + 32, B + m] = -1.0
            rm[0, B + m, B + m + 32] = 1.0
    for m in range(16):
        rm[1, 64 + m + 16, 64 + m] = -1.0
        rm[1, 64 + m, 64 + m + 16] = 1.0
        rm[2, m + 16, m] = -1.0
        rm[2, m, m + 16] = 1.0
    return rope64, ropem, rm


def prep_inputs(inp):
    g = {k: np.asarray(v) for k, v in inp.items()}
    L = DEPTH
    shared = {}
    shared["w_mod"] = g["w_mod"]
    shared["b_mod_lay"] = np.ascontiguousarray(g["b_mod"].reshape(L, 48, 128).transpose(0, 2, 1))
    ngl = np.zeros((128, 5, 8), np.float32)
    ngl[:, 0] = _lay(g["norm1_g"][0]); ngl[:, 1] = _lay(g["norm1_g"][1])
    ngl[:, 2] = _lay(g["norm2_g"][0]); ngl[:, 3] = _lay(g["norm2_g"][1])
    ngl[:, 4] = _lay(g["final_norm_g"])
    shared["norm_g_lay"] = ngl
    shared["w_in"] = g["w_in"]
    cw = np.zeros((L, 128, 4, 5), np.float32)
    for l in range(L):
        for tap in range(4):
            cw[l, :, :, tap] = g["conv_w"][l, tap].reshape(4, 128).T
        cw[l, :, :, 4] = g["conv_b"][l].reshape(4, 128).T
    shared["convw_lay"] = cw
    bd = np.zeros((L, 2, 2, 4, 128, 128), np.float32)
    for l in range(L):
        for d in range(2):
            for kind, wname in enumerate(("lru_wa", "lru_wi")):
                w = g[wname][l, d]
                for cc in range(4):
                    bd[l, d, kind, cc, 0:64, 0:64] = w[2 * cc]
                    bd[l, d, kind, cc, 64:128, 64:128] = w[2 * cc + 1]
    shared["lru_bd"] = bd
    lv = np.zeros((L, 128, 2, 3, 4), np.float32)
    for l in range(L):
        for d in range(2):
            for kind, nm in enumerate(("lru_ba", "lru_bi", "lru_lambda")):
                lv[l, :, d, kind, :] = g[nm][l, d].reshape(4, 128).T
    shared["lru_vec"] = lv
    shared["diff_lam_rep"] = np.ascontiguousarray(np.broadcast_to(g["diff_lam"][:, None], (L, 128, 4, 64)))
    sgm = np.zeros((L, 128, 8), np.float32)
    for l in range(L):
        sgm[l, :, 0] = g["diff_subln_g"][l]
        sgm[l, :, 1] = np.tile(g["gqa_qnorm_g"][l], 2)
        sgm[l, :, 2] = np.tile(g["gqa_knorm_g"][l], 2)
        sgm[l, :, 3:6] = g["mla_qnorm_g"][l].reshape(3, 128).T
        sgm[l, :, 6:8] = g["mla_kvnorm_g"][l].reshape(2, 128).T
    shared["smallg"] = sgm
    shared["mla_w_uq"] = g["mla_w_uq"]
    shared["mla_w_ukv"] = g["mla_w_ukv"]
    shared["w_branch"] = g["w_branch"]
    shared["w_out"] = g["w_out"]
    shared["peer_wq"] = g["peer_wq"]
    kb = np.zeros((L, 8, 128, 256), np.float32)
    for p in range(2):
        kb[:, :, p * 64:(p + 1) * 64, p * 128:(p + 1) * 128] = g["peer_keys"][:, :, p].transpose(0, 1, 3, 2)
    shared["keysbd"] = kb
    shared["peer_uT"] = np.ascontiguousarray(
        g["peer_u"].reshape(L, 128, 128 // CG, CG, 8, 128).transpose(0, 2, 5, 4, 3, 1).reshape(L, UROWS, UCOLS))
    shared["peer_vP"] = np.ascontiguousarray(
        g["peer_v"].reshape(L, 128, 128, D).transpose(0, 2, 1, 3).reshape(L, 16384, D))
    rope64, ropem, rm = _rope_tables()
    shared["rope64"] = rope64
    shared["ropem"] = ropem
    shared["rmats"] = rm
    cst = np.zeros((3, 128, 128), np.float32)
    cst[0] = np.eye(128, dtype=np.float32)
    cst[1] = 1.0
    cst[2, :64, :64] = 1.0
    cst[2, 64:, 64:] = 1.0
    shared["consts"] = cst
    shared["iota_in"] = np.ascontiguousarray(np.broadcast_to(np.arange(128, dtype=np.float32)[None, :], (128, 128)))
    maps = []
    for b in range(8):
        m = dict(shared)
        xall = np.concatenate([g["ctx"][b], g["x"][b]], axis=0)
        m["xT_in"] = np.ascontiguousarray(xall.T)
        cvv = np.zeros((128, 8, 2), np.float32)
        cvv[:, :, 0] = _lay(g["c"][b])
        cvv[:, :, 1] = _lay(g["c_ctx"])
        m["cvec"] = cvv
        maps.append(m)
    return maps


def kernel(**inputs):
    maps = prep_inputs(inputs)
    nc = Prog().build()
    res = run_bass_kernel_spmd(nc, maps, core_ids=list(range(8)))
    out = np.stack([np.ascontiguousarray(res.results[b]["outT"].T) for b in range(8)], axis=0)
    return out.astype(np.float32)
```
